# Optimizing a Trainium2 kernel written in Bass

```python
import jax, jax.numpy as jnp
from jax import lax
import numpy as np

D_MODEL = 1024
BATCH = 8
SEQ = 4096
DEPTH = 1
DEC_BATCH = 32
DEC_SEQ = 64
PAST_LEN = 4096

CHUNK = 64
HEAD_DIM = 64
N_HEADS = D_MODEL // HEAD_DIM
RWKV_W = N_HEADS * HEAD_DIM
LORA_W = 64
LORA_A = 64
LORA_G = 128
SHIFT_W = 3 * RWKV_W + LORA_W + LORA_A + LORA_G
GMLP_W = D_MODEL
GMLP_CHUNK = 128
GMLP_GC = 128
GMLP_GROUPS = GMLP_W // GMLP_GC
IN_W = SHIFT_W + 2 * GMLP_W + 2 * D_MODEL
D_FF = -(-8 * D_MODEL // (3 * 256)) * 256
RMS_EPS = 1e-6
GN_EPS = 64e-5
LN_EPS = 1e-5

kernel_name = "rwkv7_gmlp_griffin_merge_stream_step"


def _rms_norm(x, g):
    xf = x.astype(jnp.float32)
    y = xf * lax.rsqrt(jnp.mean(xf * xf, axis=-1, keepdims=True) + RMS_EPS)
    return (y * g.astype(jnp.float32)).astype(x.dtype)


def _layer_norm(x, g, b):
    xf = x.astype(jnp.float32)
    mean = jnp.mean(xf, axis=-1, keepdims=True)
    var = jnp.mean(jnp.square(xf - mean), axis=-1, keepdims=True)
    return (xf - mean) * lax.rsqrt(var + LN_EPS) * g.astype(jnp.float32) + b.astype(jnp.float32)


def _wkv7_scan(S0, r, w, k, v, a, b):
    xs = tuple(jnp.swapaxes(t, 0, 1) for t in (r, w, k, v, a, b))

    def step(S, inp):
        r_t, w_t, k_t, v_t, a_t, b_t = inp
        sa = jnp.einsum('bhvk,bhk->bhv', S, a_t)
        S = S * w_t[:, :, None, :] + sa[..., None] * b_t[:, :, None, :] + v_t[..., None] * k_t[:, :, None, :]
        y = jnp.einsum('bhvk,bhk->bhv', S, r_t)
        return S, y

    S, ys = lax.scan(step, S0, xs)
    return jnp.swapaxes(ys, 0, 1), S


def _rwkv7(p, prev_row, S0, lp):
    B, T, _ = p.shape
    p_prev = jnp.concatenate([prev_row.astype(p.dtype), p[:, :-1]], axis=1)
    xm = p + (p_prev - p) * lp['mu_shift']
    o = 0
    r = xm[..., o:o + RWKV_W]; o += RWKV_W
    k = xm[..., o:o + RWKV_W]; o += RWKV_W
    v = xm[..., o:o + RWKV_W]; o += RWKV_W
    dw = xm[..., o:o + LORA_W]; o += LORA_W
    da = xm[..., o:o + LORA_A]; o += LORA_A
    dg = xm[..., o:o + LORA_G]
    f32 = lambda t: t.astype(jnp.float32)
    w_log = -jax.nn.softplus(-(f32(lp['w0']) + f32(jnp.tanh(dw) @ lp['w_lora_w']))) - 0.5
    decay = jnp.exp(-jnp.exp(w_log))
    a_rate = jax.nn.sigmoid(f32(lp['a0']) + f32(da @ lp['w_lora_a']))
    g = jax.nn.sigmoid(dg) @ lp['w_lora_g']
    hd = lambda t: f32(t).reshape(B, T, N_HEADS, HEAD_DIM)
    kk = hd(k * lp['k_k'])
    kk = kk / jnp.maximum(jnp.sqrt(jnp.sum(kk * kk, axis=-1, keepdims=True)), 1e-12)
    k_mod = f32(k) * (1.0 + (a_rate - 1.0) * f32(lp['k_a']))
    rh, kh, vh = hd(r), hd(k_mod), hd(v)
    ah = hd(a_rate)
    y, S = _wkv7_scan(S0.astype(jnp.float32), rh, hd(decay), kh, vh, -kk, kk * ah)
    mean = jnp.mean(y, axis=-1, keepdims=True)
    var = jnp.mean(jnp.square(y - mean), axis=-1, keepdims=True)
    yn = ((y - mean) * lax.rsqrt(var + GN_EPS)).reshape(B, T, RWKV_W) * f32(lp['gn_g']) + f32(lp['gn_b'])
    bonus = (jnp.sum(rh * kh * f32(lp['r_k']), axis=-1, keepdims=True) * vh).reshape(B, T, RWKV_W)
    out = ((yn + bonus) * f32(g)).astype(p.dtype)
    return out, S, p[:, -1:]


def _gmlp(u, vg, chunk, lp):
    B, T, _ = vg.shape
    vn = _layer_norm(vg, lp['ln_v_g'], lp['ln_v_b'])
    vc = vn.reshape(B, T // chunk, chunk, GMLP_GROUPS, GMLP_GC)
    mask = jnp.tril(jnp.ones((chunk, chunk), jnp.float32))
    ws = lp['w_spatial'][:, :chunk, :chunk].astype(jnp.float32) * mask
    bias = jnp.transpose(lp['b_spatial'][:, :chunk].astype(jnp.float32))
    z = jnp.einsum('gts,bnsgc->bntgc', ws, vc) + bias[None, None, :, :, None]
    out = (u.astype(jnp.float32) * z.reshape(B, T, GMLP_W)).astype(u.dtype)
    return out, vn.astype(u.dtype)


def _layer(x, prev_row, S0, chunk, lp):
    h = _rms_norm(x, lp['g_pre_mix'])
    p = h @ lp['w_in']
    o = SHIFT_W
    p_shift = p[..., :o]
    pu = p[..., o:o + GMLP_W]; o += GMLP_W
    pv = p[..., o:o + GMLP_W]; o += GMLP_W
    pga = p[..., o:o + D_MODEL]; o += D_MODEL
    pgb = p[..., o:o + D_MODEL]
    o_a, S, last_row = _rwkv7(p_shift, prev_row, S0, lp)
    o_b, vn = _gmlp(pu, pv, chunk, lp)
    m = jax.nn.sigmoid(pga) * (o_a @ lp['w_proj_a']) + jax.nn.sigmoid(pgb) * (o_b @ lp['w_proj_b'])
    x = x + _rms_norm(m @ lp['w_o'], lp['g_post_mix'])
    f = _rms_norm(x, lp['g_pre_ffn'])
    f = (jax.nn.silu(f @ lp['w_gate']) * (f @ lp['w_up'])) @ lp['w_down']
    x = x + _rms_norm(f, lp['g_post_ffn'])
    return x, last_row, S.astype(x.dtype), vn


def setup_inputs(seed: int = 0) -> dict:
    key = jax.random.key(seed)
    ks = jax.random.split(key, 32)
    nrm = lambda k, shape, s: jax.random.normal(k, shape, jnp.float32) * s
    L = DEPTH
    return {
        "x_prompt": nrm(ks[0], (BATCH, SEQ, D_MODEL), 1.0),
        "x_sample": nrm(ks[1], (DEC_BATCH, DEC_SEQ, D_MODEL), 1.0),
        "state_shift": nrm(ks[2], (L, DEC_BATCH, 1, SHIFT_W), 1.0),
        "state_wkv": nrm(ks[3], (L, DEC_BATCH, N_HEADS, HEAD_DIM, HEAD_DIM), 0.1),
        "g_pre_mix": 1.0 + nrm(ks[4], (L, D_MODEL), 0.05),
        "w_in": nrm(ks[5], (L, D_MODEL, IN_W), D_MODEL ** -0.5),
        "mu_shift": jax.random.uniform(ks[6], (L, SHIFT_W), jnp.float32),
        "w0": jax.random.uniform(ks[7], (L, RWKV_W), jnp.float32, -4.0, -0.5),
        "w_lora_w": nrm(ks[8], (L, LORA_W, RWKV_W), 0.5 * LORA_W ** -0.5),
        "a0": nrm(ks[9], (L, RWKV_W), 0.1),
        "w_lora_a": nrm(ks[10], (L, LORA_A, RWKV_W), 0.5 * LORA_A ** -0.5),
        "w_lora_g": nrm(ks[11], (L, LORA_G, RWKV_W), LORA_G ** -0.5),
        "k_k": 0.85 + nrm(ks[12], (L, RWKV_W), 0.05),
        "k_a": 1.0 + nrm(ks[13], (L, RWKV_W), 0.05),
        "r_k": nrm(ks[14], (L, N_HEADS, HEAD_DIM), 0.1),
        "gn_g": 1.0 + nrm(ks[15], (L, RWKV_W), 0.05),
        "gn_b": nrm(ks[16], (L, RWKV_W), 0.01),
        "ln_v_g": 1.0 + nrm(ks[17], (L, GMLP_W), 0.05),
        "ln_v_b": nrm(ks[18], (L, GMLP_W), 0.01),
        "w_spatial": nrm(ks[19], (L, GMLP_GROUPS, GMLP_CHUNK, GMLP_CHUNK), GMLP_CHUNK ** -0.5),
        "b_spatial": 1.0 + nrm(ks[20], (L, GMLP_GROUPS, GMLP_CHUNK), 0.05),
        "w_proj_a": nrm(ks[21], (L, RWKV_W, D_MODEL), RWKV_W ** -0.5),
        "w_proj_b": nrm(ks[22], (L, GMLP_W, D_MODEL), GMLP_W ** -0.5),
        "w_o": nrm(ks[23], (L, D_MODEL, D_MODEL), D_MODEL ** -0.5),
        "g_post_mix": 1.0 + nrm(ks[24], (L, D_MODEL), 0.05),
        "g_pre_ffn": 1.0 + nrm(ks[25], (L, D_MODEL), 0.05),
        "w_gate": nrm(ks[26], (L, D_MODEL, D_FF), D_MODEL ** -0.5),
        "w_up": nrm(ks[27], (L, D_MODEL, D_FF), D_MODEL ** -0.5),
        "w_down": nrm(ks[28], (L, D_FF, D_MODEL), D_FF ** -0.5),
        "g_post_ffn": 1.0 + nrm(ks[29], (L, D_MODEL), 0.05),
    }


def reference(x_prompt, x_sample, state_shift, state_wkv, g_pre_mix, w_in, mu_shift, w0, w_lora_w, a0,
              w_lora_a, w_lora_g, k_k, k_a, r_k, gn_g, gn_b, ln_v_g, ln_v_b, w_spatial, b_spatial,
              w_proj_a, w_proj_b, w_o, g_post_mix, g_pre_ffn, w_gate, w_up, w_down, g_post_ffn):
    yp, ys = x_prompt, x_sample
    shift_p, wkv_p, shift_s, wkv_s, v_s = [], [], [], [], []
    Bp = x_prompt.shape[0]
    for l in range(DEPTH):
        lp = {
            'g_pre_mix': g_pre_mix[l], 'w_in': w_in[l], 'mu_shift': mu_shift[l], 'w0': w0[l],
            'w_lora_w': w_lora_w[l], 'a0': a0[l], 'w_lora_a': w_lora_a[l], 'w_lora_g': w_lora_g[l],
            'k_k': k_k[l], 'k_a': k_a[l], 'r_k': r_k[l], 'gn_g': gn_g[l], 'gn_b': gn_b[l],
            'ln_v_g': ln_v_g[l], 'ln_v_b': ln_v_b[l], 'w_spatial': w_spatial[l], 'b_spatial': b_spatial[l],
            'w_proj_a': w_proj_a[l], 'w_proj_b': w_proj_b[l], 'w_o': w_o[l], 'g_post_mix': g_post_mix[l],
            'g_pre_ffn': g_pre_ffn[l], 'w_gate': w_gate[l], 'w_up': w_up[l], 'w_down': w_down[l],
            'g_post_ffn': g_post_ffn[l],
        }
        zero_row = jnp.zeros((Bp, 1, SHIFT_W), x_prompt.dtype)
        zero_S = jnp.zeros((Bp, N_HEADS, HEAD_DIM, HEAD_DIM), jnp.float32)
        yp, sp, wp, _ = _layer(yp, zero_row, zero_S, GMLP_CHUNK, lp)
        ys, ss, wsm, vs = _layer(ys, state_shift[l], state_wkv[l], x_sample.shape[1], lp)
        shift_p.append(sp); wkv_p.append(wp); shift_s.append(ss); wkv_s.append(wsm); v_s.append(vs)
    return (yp, ys, jnp.stack(shift_p), jnp.stack(wkv_p), jnp.stack(shift_s), jnp.stack(wkv_s), jnp.stack(v_s))
```

```python
import contextlib
import numpy as np
import concourse.bass as bass
import concourse.mybir as mybir
from concourse.bass_utils import run_bass_kernel_spmd

F32 = mybir.dt.float32
BF16 = mybir.dt.bfloat16
AF = mybir.ActivationFunctionType
ALU = mybir.AluOpType
AX = mybir.AxisListType

D = 1024
SEQ = 4096
NCORE = 8
SHIFT_W = 3328
IN_W = 7424
DFF = 2816
NFC = 22
RMS_EPS = 1e-6
GN_EPS = 64e-5
LN_EPS = 1e-5
C0 = float(np.exp(-0.5))
NSLOT = 4
NUNIT = 44
N_PTILES = SEQ // 128
N_STILES = 2
PIPELINE = True
PUMP_EVERY = 3
DMA_ENG_MAP = {"pool": "act"}


class Sched:
    ENGS = ("pe", "dve", "act", "pool", "sp")

    def __init__(self, nc, stack, n_dma_sems=28):
        self.nc = nc
        self.ops = {e: [] for e in self.ENGS}
        self.cnt = {e: 0 for e in self.ENGS}
        self.waited = {e: {} for e in self.ENGS}
        self.res = {}
        self.dummy = {}
        self.excl = set()
        self.semobj = {}
        for e in self.ENGS:
            self.semobj["c_" + e] = stack.enter_context(nc.semaphore("c_" + e))
        for i in range(n_dma_sems):
            self.semobj["d%d" % i] = stack.enter_context(nc.semaphore("d%d" % i))
        self.duse = [0] * n_dma_sems
        self.dpool = {"sp": list(range(0, 14)), "act": list(range(14, 22)), "pool": list(range(22, n_dma_sems))}
        self.drr = {"sp": 0, "act": 0, "pool": 0}

    def _deps(self, eng, reads, writes, xreads=()):
        need = {}

        def add(tok, raw):
            if tok is None:
                return
            key, val, teng = tok
            if teng == eng and eng == "pe":
                return
            if need.get(key, 0) < val:
                need[key] = val

        for r in reads:
            st = self.res.get(r)
            if st is not None:
                add(st["w"], True)
        for w in writes:
            st = self.res.get(w)
            if st is not None:
                add(st["w"], False)
                for key, (val, teng) in st["r"].items():
                    add((key, val, teng), False)
        for r in xreads:
            st = self.res.get(r)
            if st is not None:
                add(st["w"], True)
                for key, (val, teng) in st["r"].items():
                    if teng != eng:
                        add((key, val, teng), False)
        out = []
        wd = self.waited[eng]
        for key, val in need.items():
            if wd.get(key, 0) < val:
                wd[key] = val
                out.append((key, val))
        return out

    def _commit(self, tok, reads, writes):
        key, val, teng = tok
        for r in reads:
            st = self.res.setdefault(r, {"w": None, "r": {}})
            old = st["r"].get(key)
            if old is None or old[0] < val:
                st["r"][key] = (val, teng)
        for w in writes:
            self.res[w] = {"w": tok, "r": {}}

    def op(self, eng, fn, reads=(), writes=()):
        xreads = [r for r in reads if r in self.excl]
        reads = [r for r in reads if r not in self.excl]
        waits = self._deps(eng, reads, writes, xreads)
        reads = list(reads) + xreads
        dummy = self.dummy.get(eng)
        if dummy is not None and any(k == "c_pe" for k, _ in waits):
            self.cnt[eng] += 1
            self.ops[eng].append((waits, dummy, "c_" + eng, 1))
            waits = []
        self.cnt[eng] += 1
        tok = ("c_" + eng, self.cnt[eng], eng)
        self.ops[eng].append((waits, fn, "c_" + eng, 1))
        self._commit(tok, reads, writes)

    def dma(self, eng, fn, reads=(), writes=()):
        eng = DMA_ENG_MAP.get(eng, eng)
        waits = self._deps(eng, reads, writes)
        pl = self.dpool[eng]
        j = pl[self.drr[eng] % len(pl)]
        self.drr[eng] += 1
        key = "d%d" % j
        if self.duse[j] > 0:
            prev = 16 * self.duse[j]
            if self.waited[eng].get(key, 0) < prev:
                self.waited[eng][key] = prev
                waits.append((key, prev))
        self.duse[j] += 1
        tok = (key, 16 * self.duse[j], None)
        self.ops[eng].append((waits, fn, key, 16))
        self._commit(tok, reads, writes)

    def finish(self, eng="sp"):
        waits = []
        for j, u in enumerate(self.duse):
            if u > 0:
                waits.append(("d%d" % j, 16 * u))
        for e in self.ENGS:
            if e != eng and self.cnt[e] > 0:
                waits.append(("c_" + e, self.cnt[e]))
        self.ops[eng].append((waits, None, None, 0))

    def replay(self, block):
        def run(e, handle):
            for waits, fn, key, inc in self.ops[e]:
                for k, v in waits:
                    handle.wait_ge(self.semobj[k], v)
                if fn is not None:
                    fn(handle).then_inc(self.semobj[key], inc)

        @block.sync
        def _(sync):
            run("sp", sync)

        @block.tensor
        def _(tensor):
            run("pe", tensor)

        @block.vector
        def _(vector):
            run("dve", vector)

        @block.scalar
        def _(scalar):
            run("act", scalar)

        @block.gpsimd
        def _(gpsimd):
            run("pool", gpsimd)


def build_program(n_ptiles=N_PTILES, n_stiles=N_STILES):
    nc = bass.Bass("TRN2", target_bir_lowering=False)
    npt, nst = n_ptiles, n_stiles
    nseq_s = 2 * nst

    def din(name, shape, dt=F32):
        return nc.dram_tensor(name, list(shape), dt, kind="ExternalInput").ap()

    def dout(name, shape):
        return nc.dram_tensor(name, list(shape), F32, kind="ExternalOutput").ap()

    xp = din("xp", [npt * 128, D])
    xs = din("xs", [nst * 128, D])
    sshift = din("sshift", [nseq_s, SHIFT_W])
    swkv = din("swkv", [nseq_s, 16, 64, 64])
    w_in = din("w_in", [D, IN_W])
    w_pa = din("w_pa", [D, D])
    w_pb = din("w_pb", [D, D])
    w_o = din("w_o", [D, D])
    w_gate = din("w_gate", [D, DFF])
    w_up = din("w_up", [D, DFF])
    w_down = din("w_down", [DFF, D])
    bcp = din("bcp", [9, 128, D])
    mu_bc = din("mu_bc", [128, SHIFT_W])
    gvec = din("gvec", [128, 16])
    lora = din("lora", [3, 128, D])
    wa0 = din("wa0", [2, D])
    wsT = din("wsT", [2, 128, 8, 128])
    bspv = din("bspv", [2, 8, 128])
    cst = din("cst", [128, 128 * 7 + 4])
    selc = din("selc", [2, 384])

    yp = dout("yp", [npt * 128, D])
    ys = dout("ys", [nst * 128, D])
    shp = dout("shp", [1, SHIFT_W])
    wkvp = dout("wkvp", [16, 64, 64])
    shs = dout("shs", [nseq_s, SHIFT_W])
    wkvs = dout("wkvs", [nseq_s, 16, 64, 64])
    vs = dout("vs", [nst * 128, D])

    scr = nc.dram_tensor("scr", [NUNIT, 128, 4096], BF16, kind="Internal").ap()
    ssmu_d = nc.dram_tensor("ssmu_d", [nst, 2, SHIFT_W], BF16, kind="Internal").ap()

    with contextlib.ExitStack() as st:
        S = Sched(nc, st)

        def sb(name, shape, dt=F32):
            return st.enter_context(nc.sbuf_tensor(name, list(shape), dt))

        def ps(name, shape, dt=F32):
            return st.enter_context(nc.psum_tensor(name, list(shape), dt))

        def mm(out, lhsT, rhs, start, stop, r, w):
            S.op("pe", lambda e: e.matmul(out, lhsT=lhsT, rhs=rhs, start=start, stop=stop), r, w)

        def tr(out, in_, ident, r, w):
            S.op("pe", lambda e: e.transpose(out=out, in_=in_, identity=ident), r, w)

        def act(out, in_, func, r, w, scale=None, bias=None, accum=None):
            kw = {}
            if scale is not None:
                kw["scale"] = scale
            if bias is not None:
                kw["bias"] = bias
            if accum is not None:
                kw["accum_out"] = accum
            S.op("act", lambda e: e.activation(out=out, in_=in_, func=func, **kw), r, w)

        def tt(eng, out, in0, in1, op, r, w):
            S.op(eng, lambda e: e.tensor_tensor(out=out, in0=in0, in1=in1, op=op), r, w)

        def ts(eng, out, in0, s1, s2, op0, op1, r, w):
            if op1 is None:
                S.op(eng, lambda e: e.tensor_scalar(out=out, in0=in0, scalar1=s1, scalar2=None, op0=op0), r, w)
            else:
                S.op(eng, lambda e: e.tensor_scalar(out=out, in0=in0, scalar1=s1, scalar2=s2, op0=op0, op1=op1), r, w)

        def stt(out, in0, scalar, in1, op0, op1, r, w):
            S.op("dve", lambda e: e.scalar_tensor_tensor(out=out, in0=in0, scalar=scalar, in1=in1, op0=op0, op1=op1), r, w)

        def cp(eng, out, in_, r, w):
            if eng == "act":
                S.op("act", lambda e: e.activation(out=out, in_=in_, func=AF.Copy), r, w)
            else:
                S.op(eng, lambda e: e.tensor_copy(out=out, in_=in_), r, w)

        def red(out, in_, r, w):
            S.op("dve", lambda e: e.tensor_reduce(out=out, in_=in_, axis=AX.X, op=ALU.add), r, w)

        def recip(out, in_, r, w):
            S.op("dve", lambda e: e.reciprocal(out=out, in_=in_), r, w)

        def mset(eng, ap, val, w):
            S.op(eng, lambda e: e.memset(ap, val), (), w)

        def ld(out, in_, r, w, eng="sp"):
            S.dma(eng, lambda e: e.dma_start(out=out, in_=in_), r, w)

        def ldnc(out, in_, r, w, eng="sp"):
            S.dma(eng, lambda e: e.dma_start(out=out, in_=in_, allow_slow_non_contiguous=True), r, w)

        bc = [sb("bc%d" % i, [128, D]) for i in range(9)]
        KK, KA, RK, GNG, GNB, LNG, LNB, GPM, GPF = range(9)
        lw = [sb("lw%d" % i, [128, D], BF16) for i in range(3)]
        w0hl = sb("w0hl", [2, 3, D], BF16)
        ones2 = sb("ones2", [2, 128], BF16)
        wst = [sb("wst%d" % v, [128, 8, 128], BF16) for v in range(2)]
        osel = sb("osel", [2, 2, 128], BF16)
        csb = sb("csb", [128, 128 * 7 + 4])
        idb = sb("idb", [128, 128], BF16)
        sel = sb("sel", [2, 128], BF16)
        gv = sb("gv", [128, 16])
        idf = csb[:, 0:128]
        MT2 = [csb[:, 128:384], csb[:, 384:640]]
        TRI = [csb[:, 256:384], csb[:, 512:640]]
        MSN = [csb[:, 640:768], csb[:, 768:896]]
        SEGI = [csb[:, 896:898], csb[:, 898:900]]

        xt = [sb("xt%d" % i, [128, D]) for i in range(2)]
        hb = sb("hb", [128, D], BF16)
        hT = [sb("hT%d" % i, [128, 8, 128], BF16) for i in range(2)]
        hTp = sb("hTp", [128, 8, 128], BF16)
        Rk = [sb("rkv%d" % i, [128, D]) for i in range(3)]
        T = [sb("T%d" % i, [128, D]) for i in range(8)]
        fTb = sb("fTb", [128, 8, 128], BF16)
        B = [sb("B%d" % i, [128, D], BF16) for i in range(7)]
        arT = sb("arT", [128, 8, 2, 128], BF16)
        Xb = [sb("X%d" % i, [128, 8, 128], BF16) for i in range(2)]
        XTb = [sb("XT%d" % i, [128, 8, 128], BF16) for i in range(2)]
        PTb = [sb("PT%d" % i, [128, 8, 128], BF16) for i in range(2)]
        M2a = sb("M2a", [128, 8, 256], BF16)
        M2b = sb("M2b", [128, 8, 256], BF16)
        W1b = sb("W1b", [128, 8, 64], BF16)
        Ub = sb("Ub", [128, 8, 64], BF16)
        actT = sb("actT", [128, NFC, 128], BF16)
        Hf = [sb("Hf%d" % i, [128, 8, 64]) for i in range(2)]
        Hb = [sb("Hb%d" % i, [128, 8, 64], BF16) for i in range(2)]
        lin = sb("lin", [128, 2, 128], BF16)
        ssb = sb("ssb", [2, 512], BF16)
        sm = sb("sm", [128, 112])
        gC = sb("gC", [128, 8, 2])
        ring = [sb("ring%d" % i, [128, 4096], BF16) for i in range(NSLOT)]

        dmy = sb("dmy", [128, 8])
        PA = ps("PA", [128, 1024])
        PB = ps("PB", [128, 1024])
        PC = ps("PC", [128, 1024])
        PD = ps("PD", [128, 512])
        PT = ps("PT", [128, 1024], BF16)
        S.excl = {"PA0", "PA1", "PB0", "PB1", "PC0", "PC1", "PD", "PT"}
        BANKS = [(PA[:, 0:512], "PA0"), (PA[:, 512:1024], "PA1"), (PB[:, 0:512], "PB0"),
                 (PB[:, 512:1024], "PB1"), (PC[:, 0:512], "PC0"), (PC[:, 512:1024], "PC1")]

        def v3(ap, d):
            return ap.rearrange("p (a d) -> p a d", d=d)

        mset("dve", dmy[:], 0.0, ["dmy"])
        for i in range(9):
            ld(bc[i][:], bcp[i], (), ["bc%d" % i])
        ld(csb[:], cst, (), ["csb"])
        ld(gv[:], gvec, (), ["gv"])
        cp("dve", idb[:], idf, ["csb"], ["idb"])
        mset("dve", ones2[:], 1.0, ["ones2"])
        ld(T[0][0:2, 0:384], selc, (), ["T0"])
        cp("dve", sel[:], T[0][0:2, 0:128], ["T0"], ["sel"])
        cp("dve", osel[:].rearrange("o a t -> o (a t)"), T[0][0:2, 128:384], ["T0"], ["osel"])
        ld(T[7][0:2, :], bspv.rearrange("v g t -> v (g t)"), (), ["T7"])
        cp("dve", w0hl[:, 2, :], T[7][0:2, :], ["T7"], ["w0hl"])
        for i in range(3):
            ld(T[1 + (i % 2)][:], lora[i], (), ["T%d" % (1 + (i % 2))])
            cp("dve", lw[i][:], T[1 + (i % 2)][:], ["T%d" % (1 + (i % 2))], ["lw%d" % i])
        for j in range(2):
            tj = T[3 + j]
            nm = "T%d" % (3 + j)
            ld(tj[0:1, :], wa0[j:j + 1, :], (), [nm])
            ld(tj[1:2, :], wa0[j:j + 1, :], (), [nm])
            cp("dve", w0hl[:, j, :], tj[0:2, :], [nm], ["w0hl"])
            tt("dve", tj[0:2, :], tj[0:2, :], w0hl[:, j, :], ALU.subtract, [nm, "w0hl"], [nm])
            cp("dve", B[0][0:2, j * D:(j + 1) * D] if False else B[j][0:2, :], tj[0:2, :], [nm], ["B%d" % j])
            ld(w0hl[1:2, j, :], B[j][0:1, :], ["B%d" % j], ["w0hl"])
        for v in range(2):
            tv = T[5 + v]
            nm = "T%d" % (5 + v)
            ld(v3(tv[:], 128), wsT[v], (), [nm])
            tt("dve", wst[v][:], v3(tv[:], 128), TRI[v].unsqueeze(1).to_broadcast([128, 8, 128]), ALU.mult,
               [nm, "csb"], ["wst%d" % v])
        for j in range(nst):
            for q in range(4):
                c0 = q * 1024
                c1 = min(SHIFT_W, c0 + 1024)
                n = c1 - c0
                ld(T[0][0:2, 0:n], sshift[2 * j:2 * j + 2, c0:c1], (), ["T0"])
                ld(T[1][0:2, 0:n], mu_bc[0:2, c0:c1], (), ["T1"])
                tt("dve", B[2][0:2, 0:n], T[0][0:2, 0:n], T[1][0:2, 0:n], ALU.mult, ["T0", "T1"], ["B2"])
                ld(ssmu_d[j, :, c0:c1], B[2][0:2, 0:n], ["B2"], ["ssmu_d"], eng="pool")

        w_in_v = w_in.rearrange("(kc p) c -> p kc c", p=128)
        w_pa_v = w_pa.rearrange("(kc p) c -> p kc c", p=128)
        w_pb_v = w_pb.rearrange("(kc p) c -> p kc c", p=128)
        w_o_v = w_o.rearrange("(kc p) c -> p kc c", p=128)
        w_gate_v = w_gate.rearrange("(kc p) c -> p kc c", p=128)
        w_up_v = w_up.rearrange("(kc p) c -> p kc c", p=128)
        w_down_v = w_down.rearrange("(rc p) c -> p rc c", p=128)

        pp_cnt = [0]
        ew_rr = [0]

        def ew_eng():
            ew_rr[0] += 1
            return ("dve", "act")[ew_rr[0] % 2]

        def pp_piece(u, q, src, nk, ncol, gcol, mu_cols, mu_inv):
            i = pp_cnt[0] % 4
            pp_cnt[0] += 1
            stg = T[i]
            snm = "T%d" % i
            ob = B[i]
            onm = "B%d" % i
            n = nk * ncol
            sv = stg[:, 0:n].rearrange("p (k c) -> p k c", c=ncol)
            ov = ob[:, 0:n].rearrange("p (k c) -> p k c", c=ncol)
            ld(sv, src, (), [snm])
            if mu_cols is not None:
                mt = T[4 + i]
                mnm = "T%d" % (4 + i)
                ld(mt[:, 0:ncol], mu_bc[:, mu_cols:mu_cols + ncol], (), [mnm])
                if mu_inv:
                    ts("dve", mt[:, 0:ncol], mt[:, 0:ncol], -1.0, 1.0, ALU.mult, ALU.add, [mnm], [mnm])
                for k in range(nk):
                    stt(ov[:, k, :], sv[:, k, :], gv[:, gcol + k:gcol + k + 1], mt[:, 0:ncol], ALU.mult, ALU.mult,
                        [snm, mnm, "gv"], [onm])
            elif gcol is not None:
                for k in range(nk):
                    e = ew_eng()
                    if e == "act":
                        act(ov[:, k, :], sv[:, k, :], AF.Copy, [snm, "gv"], [onm], scale=gv[:, gcol + k:gcol + k + 1])
                    else:
                        ts(e, ov[:, k, :], sv[:, k, :], gv[:, gcol + k:gcol + k + 1], None, ALU.mult, None,
                           [snm, "gv"], [onm])
            else:
                e = ew_eng()
                cp(e, ob[:, 0:n], stg[:, 0:n], [snm], [onm])
            ld(scr[u][:, q * 1024:q * 1024 + n], ob[:, 0:n], [onm], [("scr", u, q)], eng="pool")

        units = []
        u = 0
        U_RKV = u
        for b in range(6):
            for var in range(2):
                for q in range(4):
                    pp_piece(u, q, w_in_v[:, 2 * q:2 * q + 2, b * 512:(b + 1) * 512], 2, 512, 2 * q, b * 512, var == 0)
                u += 1
        U_LORA = u
        for var in range(2):
            for hh in range(2):
                pp_piece(u, var * 2 + hh, w_in_v[:, 4 * hh:4 * hh + 4, 3072:3328], 4, 256, 4 * hh, 3072, var == 0)
        u += 1
        U_VG = u
        for (base, col0) in ((0, 4352),):
            for b in range(2):
                for q in range(4):
                    pp_piece(u, q, w_in_v[:, 2 * q:2 * q + 2, col0 + b * 512:col0 + (b + 1) * 512], 2, 512, 2 * q, None, False)
                u += 1
        U_U = u
        for b in range(2):
            for q in range(4):
                pp_piece(u, q, w_in_v[:, 2 * q:2 * q + 2, 3328 + b * 512:3328 + (b + 1) * 512], 2, 512, 2 * q, None, False)
            u += 1
        U_GA = u
        for col0 in (5376, 6400):
            for b in range(2):
                for q in range(4):
                    pp_piece(u, q, w_in_v[:, 2 * q:2 * q + 2, col0 + b * 512:col0 + (b + 1) * 512], 2, 512, 2 * q, None, False)
                u += 1
        U_PA = u
        for wv in (w_pa_v, w_pb_v, w_o_v):
            for b in range(2):
                for q in range(4):
                    pp_piece(u, q, wv[:, 2 * q:2 * q + 2, b * 512:(b + 1) * 512], 2, 512, None, None, False)
                u += 1
        U_FFN = u
        for j in range(11):
            for m, wv in enumerate((w_gate_v, w_up_v)):
                for hh in range(2):
                    pp_piece(u, m * 2 + hh, wv[:, 4 * hh:4 * hh + 4, j * 256:(j + 1) * 256], 4, 256, 8 + 4 * hh, None, False)
            u += 1
        U_DOWN = u
        for qd in range(6):
            for rc in range(4):
                if 4 * qd + rc < NFC:
                    pp_piece(u, rc, w_down_v[:, 4 * qd + rc:4 * qd + rc + 1, :], 1, 1024, None, None, False)
            u += 1
        assert u == NUNIT

        ring_ctr = [0]

        def load_unit(uidx, npc=4):
            s = ring_ctr[0] % NSLOT
            ring_ctr[0] += 1
            ld(ring[s][:, 0:npc * 1024], scr[uidx][:, 0:npc * 1024], [("scr", uidx, q) for q in range(npc)], [("ring", s)])
            return ring[s], ("ring", s)

        bank_ctr = [0]

        bank_pool = [6]

        SCAN_BANKS = [BANKS[0], BANKS[1], BANKS[2], (PD[:], "PD")]

        def next_bank():
            if bank_pool[0] == 3:
                b = SCAN_BANKS[bank_ctr[0] % 4]
            else:
                b = BANKS[bank_ctr[0] % bank_pool[0]]
            bank_ctr[0] += 1
            return b

        ptb_ctr = [0]
        PT_ALT = [(PT[:], "PT"), (PD[:].bitcast(BF16), "PD")]

        def transpose_tm_to_fm(src, srcn, dst, dstn, evac_eng):
            pt, ptn = PT_ALT[ptb_ctr[0] % 2]
            ptb_ctr[0] += 1
            for kc in range(8):
                tr(pt[:, kc * 128:(kc + 1) * 128], src[:, kc * 128:(kc + 1) * 128], idb[:], [srcn, "idb"], [ptn])
            cp("dve", dst, pt.rearrange("p (k t) -> p k t", t=128), [ptn], [dstn])

        def rms_rstd(src_ap, srcn, junk, junkn, eps):
            act(junk, src_ap, AF.Square, srcn, [junkn, "sm0"], accum=sm[:, 0:1])
            act(sm[:, 1:2], sm[:, 0:1], AF.Sqrt, ["sm0"], ["sm1"], scale=1.0 / D, bias=eps)
            recip(sm[:, 1:2], sm[:, 1:2], ["sm1"], ["sm1"])
            return sm[:, 1:2]

        tiles = []
        for i in range(npt):
            tiles.append(dict(v=0, x=xp[i * 128:(i + 1) * 128, :], y=yp[i * 128:(i + 1) * 128, :],
                              segs=[(0, 128, 0)], first=(i == 0), last=(i == npt - 1), samp=None))
        for j in range(nst):
            tiles.append(dict(v=1, x=xs[j * 128:(j + 1) * 128, :], y=ys[j * 128:(j + 1) * 128, :],
                              segs=[(0, 64, 0), (64, 64, 1)], first=True, last=True, samp=j))

        def emit_state_out(slot, dst):
            for c in range(8):
                tr(PC[0:64, c * 128:(c + 1) * 128], Hf[slot][:, c, :], idf, ["Hf%d" % slot, "csb"], ["PC0", "PC1"])
            cp("act", T[0][0:64, :], PC[0:64, :], ["PC0", "PC1"], ["T0"])
            ld(dst.rearrange("h v k -> v h k"), T[0][0:64, :].rearrange("v (h k) -> v h k", k=64), ["T0"], [], eng="pool")

        pending = []
        pump_ctr = [0]
        PT32 = PT[:].bitcast(F32)[:, 0:512]

        def pump(force=False):
            pump_ctr[0] += 1
            if pending and force:
                pending.pop(0)()

        def drain():
            while pending:
                pending.pop(0)()

        def make_side_chunks(tl, v, hTc, hTn):
            ch = []
            VG = T[0]
            vnb = B[0]

            def vg_unit(b):
                rv, rvn = load_unit(U_VG + b)
                wv_ = v3(rv[:], 512)
                for kc in range(8):
                    mm(PC[:, b * 512:(b + 1) * 512], hTc[:, kc, :], wv_[:, kc, :], kc == 0, kc == 7, [hTn, rvn], ["PC%d" % b])

            def vg_post():
                cp("act", VG[:], PC[:], ["PC0", "PC1"], ["T0"])
                S.op("dve", lambda e: e.bn_stats(out=sm[:, 96:102], in_=T[0][:, 0:512]), ["T0"], ["sm96"])
                S.op("dve", lambda e: e.bn_stats(out=sm[:, 102:108], in_=T[0][:, 512:1024]), ["T0"], ["sm96"])
                S.op("dve", lambda e: e.bn_aggr(out=sm[:, 108:110], in_=sm[:, 96:108]), ["sm96"], ["sm108"])
                act(sm[:, 110:111], sm[:, 109:110], AF.Sqrt, ["sm108"], ["sm110"], bias=LN_EPS)
                recip(sm[:, 110:111], sm[:, 110:111], ["sm110"], ["sm110"])
                ts("dve", VG[:], VG[:], sm[:, 108:109], sm[:, 110:111], ALU.subtract, ALU.mult, ["T0", "sm108", "sm110"], ["T0"])
                tt("pool", VG[:], VG[:], bc[LNG][:], ALU.mult, ["T0", "bc5"], ["T0"])
                tt("pool", VG[:], VG[:], bc[LNB][:], ALU.add, ["T0", "bc6"], ["T0"])
                if tl["samp"] is not None:
                    j = tl["samp"]
                    ld(vs[j * 128:(j + 1) * 128, :], VG[:], ["T0"], [], eng="pool")
                cp("act", vnb[:], VG[:], ["T0"], ["B0"])

            def fm_unit(ubase, b, srcT, srcn):
                ru, run_ = load_unit(ubase + b)
                wu_ = v3(ru[:], 512)
                for cc in range(4):
                    for kc in range(8):
                        mm(PC[:, b * 512 + cc * 128:b * 512 + (cc + 1) * 128], wu_[:, kc, cc * 128:(cc + 1) * 128], srcT[:, kc, :],
                           kc == 0, kc == 7, [run_, srcn], ["PC%d" % b])

            def u_evac():
                cp("act", T[2][:], PC[:], ["PC0", "PC1"], ["T2"])

            def spatial():
                for g in range(8):
                    o = PC[:, g * 128:(g + 1) * 128]
                    bn = "PC%d" % (g // 4)
                    mm(o, vnb[:, g * 128:(g + 1) * 128], wst[v][:, g, :], True, False, ["B0", "wst%d" % v], [bn])
                    mm(o, osel[:, v, :], w0hl[:, 2, g * 128:(g + 1) * 128], False, True, ["osel", "w0hl"], [bn])
                tt("dve", B[1][:], PC[:], T[2][:], ALU.mult, ["PC0", "PC1", "T2"], ["B1"])

            obT = v3(B[1][:], 128)
            ch.append(lambda: vg_unit(0))
            ch.append(lambda: (vg_unit(1), vg_post()))
            ch.append(lambda: fm_unit(U_U, 0, hTc, hTn))
            ch.append(lambda: (fm_unit(U_U, 1, hTc, hTn), u_evac()))
            ch.append(lambda: fm_unit(U_GA, 0, hTc, hTn))
            ch.append(lambda: (fm_unit(U_GA, 1, hTc, hTn), act(T[3][:], PC[:], AF.Sigmoid, ["PC0", "PC1"], ["T3"])))
            ch.append(lambda: spatial())
            ch.append(lambda: fm_unit(U_GA + 2, 0, hTc, hTn))
            ch.append(lambda: (fm_unit(U_GA + 2, 1, hTc, hTn), act(T[4][:], PC[:], AF.Sigmoid, ["PC0", "PC1"], ["T4"])))
            ch.append(lambda: fm_unit(U_PA + 2, 0, obT, "B1"))
            ch.append(lambda: (fm_unit(U_PA + 2, 1, obT, "B1"),
                               tt("dve", T[6][:], PC[:], T[4][:], ALU.mult, ["PC0", "PC1", "T4"], ["T6"])))
            return ch

        def make_ffn_chunks(xb, xbn, ydst):
            ch = []
            gbank, gbn = BANKS[4]
            ubank, ubn = BANKS[5]
            dbank0, dbn0 = BANKS[3]
            for q in range(6):
                nfc = min(4, NFC - 4 * q)
                nun = (nfc + 1) // 2
                for jj in range(nun):
                    def c(q=q, jj=jj, nfc=nfc, lastu=(jj == nun - 1)):
                        rf, rfn = load_unit(U_FFN + 2 * q + jj)
                        wf_ = rf[:].rearrange("p (m k c) -> p m k c", m=2, k=8)
                        for cc in range(2):
                            fc = jj * 2 + cc
                            for kc in range(8):
                                mm(gbank[:, fc * 128:(fc + 1) * 128], wf_[:, 0, kc, cc * 128:(cc + 1) * 128], fTb[:, kc, :],
                                   kc == 0, kc == 7, [rfn, "fTb"], [gbn])
                            for kc in range(8):
                                mm(ubank[:, fc * 128:(fc + 1) * 128], wf_[:, 1, kc, cc * 128:(cc + 1) * 128], fTb[:, kc, :],
                                   kc == 0, kc == 7, [rfn, "fTb"], [ubn])
                        if lastu:
                            tq = T[2 + (q % 2)]
                            tqn = "T%d" % (2 + (q % 2))
                            act(tq[:, 0:nfc * 128], gbank[:, 0:nfc * 128], AF.Silu, [gbn], [tqn])
                            tt("dve", actT[:, 4 * q:4 * q + nfc, :].rearrange("p a t -> p (a t)"), tq[:, 0:nfc * 128],
                               ubank[:, 0:nfc * 128], ALU.mult, [tqn, ubn], ["actT"])
                    c.kind = "ffn"
                    ch.append(c)
            for q in range(6):
                def c(q=q):
                    rd, rdn = load_unit(U_DOWN + q, min(4, NFC - 4 * q))
                    wd_ = v3(rd[:], 1024)
                    for rc in range(4):
                        fc = 4 * q + rc
                        if fc >= NFC:
                            continue
                        mm(dbank0, actT[:, fc, :], wd_[:, rc, 0:512], fc == 0, fc == NFC - 1, ["actT", rdn], [dbn0])
                        mm(PT32, actT[:, fc, :], wd_[:, rc, 512:1024], fc == 0, fc == NFC - 1, ["actT", rdn], ["PT"])
                    if q == 5:
                        cp("dve", T[0][:, 0:512], dbank0, [dbn0], ["T0"])
                        cp("dve", T[0][:, 512:1024], PT32, ["PT"], ["T0"])
                        rstd = rms_rstd(T[0][:], ["T0"], T[4][:], "T4", RMS_EPS)
                        stt(T[0][:], T[0][:], rstd, bc[GPF][:], ALU.mult, ALU.mult, ["T0", "sm1", "bc8"], ["T0"])
                        tt("dve", T[0][:], T[0][:], xb[:], ALU.add, ["T0", xbn], ["T0"])
                        ld(ydst, T[0][:], ["T0"], [], eng="pool")
                c.kind = "ffn"
                ch.append(c)
            return ch

        def st_A(cx):
            tl = cx["tl"]
            ti = cx["ti"]
            v = tl["v"]
            segs = tl["segs"]
            nseg = len(segs)
            xb = xt[ti % 2]
            xbn = "xt%d" % (ti % 2)
            hTc = hT[ti % 2]
            hTn = "hT%d" % (ti % 2)
            hTprev = hT[(ti + 1) % 2]
            hTprevn = "hT%d" % ((ti + 1) % 2)
            R_, K_, V_ = Rk[0], Rk[1], Rk[2]
            oaT = v3(B[1][:], 128)
            mT = v3(B[3][:], 128)
            ld(xb[:], tl["x"], (), [xbn])
            if tl["samp"] is not None:
                for sl in range(2):
                    seq = 2 * tl["samp"] + sl
                    ld(T[0][0:64, :].rearrange("v (h k) -> v h k", k=64), swkv[seq].rearrange("h v k -> v h k"), (), ["T0"])
                    for c in range(8):
                        tr(PD[:, c * 64:(c + 1) * 64], T[0][0:64, c * 128:(c + 1) * 128], idf[0:64, 0:64], ["T0", "csb"], ["PD"])
                    cp("dve", Hf[sl][:], v3(PD[:], 64), ["PD"], ["Hf%d" % sl])
                    cp("act", Hb[sl][:], Hf[sl][:], ["Hf%d" % sl], ["Hb%d" % sl])
            elif tl["first"]:
                mset("dve", Hf[0][:], 0.0, ["Hf0"])
                mset("dve", Hb[0][:], 0.0, ["Hb0"])
            rstd = rms_rstd(xb[:], [xbn], hb[:], "hb", RMS_EPS)
            ts("dve", hb[:], xb[:], rstd, None, ALU.mult, None, [xbn, "sm1"], ["hb"])

        def st_A2(cx):
            tl = cx["tl"]
            ti = cx["ti"]
            v = tl["v"]
            segs = tl["segs"]
            nseg = len(segs)
            xb = xt[ti % 2]
            xbn = "xt%d" % (ti % 2)
            hTc = hT[ti % 2]
            hTn = "hT%d" % (ti % 2)
            hTprev = hT[(ti + 1) % 2]
            hTprevn = "hT%d" % ((ti + 1) % 2)
            for kc in range(8):
                tr(PT[:, kc * 128:(kc + 1) * 128], hb[:, kc * 128:(kc + 1) * 128], idb[:], ["hb", "idb"], ["PT"])
            ptv = PT[:].rearrange("p (k t) -> p k t", t=128)
            cp("dve", hTc[:].rearrange("p k t -> p (k t)"), PT[:], ["PT"], [hTn])
            cp("dve", hTp[:, :, 1:128], ptv[:, :, 0:127], ["PT"], ["hTp"])
            if tl["first"]:
                mset("dve", hTp[:, :, 0:1], 0.0, ["hTp"])
                if v == 1:
                    mset("dve", hTp[:, :, 64:65], 0.0, ["hTp"])
            else:
                cp("dve", hTp[:, :, 0:1], hTprev[:, :, 127:128], [hTprevn], ["hTp"])

        def st_B1(cx, blocks):
            tl = cx["tl"]
            ti = cx["ti"]
            v = tl["v"]
            segs = tl["segs"]
            nseg = len(segs)
            xb = xt[ti % 2]
            xbn = "xt%d" % (ti % 2)
            hTc = hT[ti % 2]
            hTn = "hT%d" % (ti % 2)
            hTprev = hT[(ti + 1) % 2]
            hTprevn = "hT%d" % ((ti + 1) % 2)
            R_, K_, V_ = Rk[0], Rk[1], Rk[2]
            oaT = v3(B[1][:], 128)
            mT = v3(B[3][:], 128)
            for b in blocks:
                r1, r1n = load_unit(U_RKV + 2 * b)
                r2, r2n = load_unit(U_RKV + 2 * b + 1)
                w1 = v3(r1[:], 512)
                w2 = v3(r2[:], 512)
                bank, bn = next_bank()
                if v == 1:
                    ld(ssb[:], ssmu_d[tl["samp"], :, b * 512:(b + 1) * 512], ["ssmu_d"], ["ssb"])
                for kc in range(8):
                    mm(bank, hTc[:, kc, :], w1[:, kc, :], kc == 0, False, [hTn, r1n], [bn])
                for kc in range(8):
                    mm(bank, hTp[:, kc, :], w2[:, kc, :], False, (kc == 7 and v == 0), ["hTp", r2n], [bn])
                if v == 1:
                    mm(bank, sel[:], ssb[:], False, True, ["sel", "ssb"], [bn])
                dstt = Rk[b // 2]
                cp("act", dstt[:, (b % 2) * 512:(b % 2 + 1) * 512], bank, [bn], ["rkv%d" % (b // 2)])
                if tl["last"]:
                    lh = hTc[:, :, 63::64]
                    for kc in range(8):
                        mm(PD[0:2, :], lh[:, kc, :], w1[:, kc, :], kc == 0, False, [hTn, r1n], ["PD"])
                    for kc in range(8):
                        mm(PD[0:2, :], lh[:, kc, :], w2[:, kc, :], False, kc == 7, [hTn, r2n], ["PD"])
                    tq = T[b // 2]
                    cp("dve", tq[0:2, (b % 2) * 512:(b % 2 + 1) * 512], PD[0:2, :], ["PD"], ["T%d" % (b // 2)])
                    if v == 0:
                        ld(shp[0:1, b * 512:(b + 1) * 512], tq[1:2, (b % 2) * 512:(b % 2 + 1) * 512], ["T%d" % (b // 2)], [], eng="pool")
                    else:
                        j = tl["samp"]
                        ld(shs[2 * j:2 * j + 2, b * 512:(b + 1) * 512], tq[0:2, (b % 2) * 512:(b % 2 + 1) * 512], ["T%d" % (b // 2)], [], eng="pool")

        def st_B2(cx):
            tl = cx["tl"]
            ti = cx["ti"]
            v = tl["v"]
            segs = tl["segs"]
            nseg = len(segs)
            xb = xt[ti % 2]
            xbn = "xt%d" % (ti % 2)
            hTc = hT[ti % 2]
            hTn = "hT%d" % (ti % 2)
            hTprev = hT[(ti + 1) % 2]
            hTprevn = "hT%d" % ((ti + 1) % 2)
            R_, K_, V_ = Rk[0], Rk[1], Rk[2]
            oaT = v3(B[1][:], 128)
            mT = v3(B[3][:], 128)
            rl, rln = load_unit(U_LORA)
            wl = rl[:].rearrange("p (a k c) -> p a k c", a=2, k=8)
            if v == 1:
                ld(ssb[:, 0:256], ssmu_d[tl["samp"], :, 3072:3328], ["ssmu_d"], ["ssb"])
            for cc in range(2):
                o = PD[:, cc * 128:(cc + 1) * 128]
                for kc in range(8):
                    mm(o, wl[:, 0, kc, cc * 128:(cc + 1) * 128], hTc[:, kc, :], kc == 0, False, [rln, hTn], ["PD"])
                for kc in range(8):
                    mm(o, wl[:, 1, kc, cc * 128:(cc + 1) * 128], hTp[:, kc, :], False, (kc == 7 and v == 0), [rln, "hTp"], ["PD"])
                if v == 1:
                    mm(o, ssb[:, cc * 128:(cc + 1) * 128], sel[:], False, True, ["ssb", "sel"], ["PD"])
            act(lin[0:64, 0, :], PD[0:64, 0:128], AF.Tanh, ["PD"], ["lin"])
            cp("act", lin[64:128, 0, :], PD[64:128, 0:128], ["PD"], ["lin"])
            act(lin[:, 1, :], PD[:, 128:256], AF.Sigmoid, ["PD"], ["lin"])
            if tl["last"]:
                lh = hTc[:, :, 63::64]
                o = PD[0:2, 256:512]
                for kc in range(8):
                    mm(o, lh[:, kc, :], wl[:, 0, kc, :], kc == 0, False, [hTn, rln], ["PD"])
                for kc in range(8):
                    mm(o, lh[:, kc, :], wl[:, 1, kc, :], False, kc == 7, [hTn, rln], ["PD"])
                cp("dve", T[3][0:2, 0:256], o, ["PD"], ["T3"])
                if v == 0:
                    ld(shp[0:1, 3072:3328], T[3][1:2, 0:256], ["T3"], [], eng="pool")
                else:
                    j = tl["samp"]
                    ld(shs[2 * j:2 * j + 2, 3072:3328], T[3][0:2, 0:256], ["T3"], [], eng="pool")

        def st_CD(cx):
            tl = cx["tl"]
            ti = cx["ti"]
            v = tl["v"]
            segs = tl["segs"]
            nseg = len(segs)
            xb = xt[ti % 2]
            xbn = "xt%d" % (ti % 2)
            hTc = hT[ti % 2]
            hTn = "hT%d" % (ti % 2)
            hTprev = hT[(ti + 1) % 2]
            hTprevn = "hT%d" % ((ti + 1) % 2)
            R_, K_, V_ = Rk[0], Rk[1], Rk[2]
            oaT = v3(B[1][:], 128)
            mT = v3(B[3][:], 128)
            R_, K_, V_ = Rk[0], Rk[1], Rk[2]
            for blk in range(2):
                cs = slice(blk * 512, (blk + 1) * 512)
                mm(PA[:, cs], lin[:, 0, :], lw[0][:, cs], True, False, ["lin", "lw0"], ["PA%d" % blk])
                mm(PA[:, cs], ones2[:], w0hl[:, 0, cs], False, True, ["ones2", "w0hl"], ["PA%d" % blk])
                mm(PB[:, cs], lin[:, 0, :], lw[1][:, cs], True, False, ["lin", "lw1"], ["PB%d" % blk])
                mm(PB[:, cs], ones2[:], w0hl[:, 1, cs], False, True, ["ones2", "w0hl"], ["PB%d" % blk])
            act(T[0][:], PA[:], AF.Sigmoid, ["PA0", "PA1"], ["T0"])
            act(T[4][:], PB[:], AF.Sigmoid, ["PB0", "PB1"], ["T4"])
            for blk in range(2):
                cs = slice(blk * 512, (blk + 1) * 512)
                mm(PA[:, cs], TRI[v], T[0][:, cs], True, True, ["csb", "T0"], ["PA%d" % blk])
            for c in range(8):
                mm(PD[:, 2 * c:2 * c + 2], T[0][:, c * 128:(c + 1) * 128], SEGI[v], True, True, ["T0", "csb"], ["PD"])
            act(T[1][:], PA[:], AF.Exp, ["PA0", "PA1"], ["T1"], scale=-C0)
            act(T[2][:], PA[:], AF.Exp, ["PA0", "PA1"], ["T2"], scale=C0)
            for blk in range(2):
                cs = slice(blk * 512, (blk + 1) * 512)
                mm(PB[:, cs], MT2[v][:, 0:128], T[0][:, cs], True, True, ["csb", "T0"], ["PB%d" % blk])
            act(T[3][:], PB[:], AF.Exp, ["PB0", "PB1"], ["T3"], scale=-C0)
            act(gC[:].rearrange("p c s -> p (c s)"), PD[:, 0:16], AF.Exp, ["PD"], ["gC"], scale=-C0)
            tt("dve", T[5][:], K_[:], bc[KK][:], ALU.mult, ["rkv1", "bc0"], ["T5"])
            act(T[7][:], T[5][:], AF.Square, ["T5"], ["T7"])
            red(sm[:, 16:32], v3(T[7][:], 64), ["T7"], ["sm16"])
            act(sm[:, 16:32], sm[:, 16:32], AF.Sqrt, ["sm16"], ["sm16"])
            ts("dve", sm[:, 16:32], sm[:, 16:32], 1e-12, None, ALU.max, None, ["sm16"], ["sm16"])
            recip(sm[:, 16:32], sm[:, 16:32], ["sm16"], ["sm16"])
            tt("dve", v3(T[5][:], 64), v3(T[5][:], 64), sm[:, 16:32].unsqueeze(2).to_broadcast([128, 16, 64]), ALU.mult,
               ["T5", "sm16"], ["T5"])
            stt(T[6][:], T[4][:], -1.0, bc[KA][:], ALU.add, ALU.mult, ["T4", "bc1"], ["T6"])
            stt(T[6][:], T[6][:], 1.0, K_[:], ALU.add, ALU.mult, ["T6", "rkv1"], ["T6"])
            tt("dve", B[0][:], R_[:], T[1][:], ALU.mult, ["rkv0", "T1"], ["B0"])
            stt(B[1][:], T[5][:], -1.0, T[3][:], ALU.mult, ALU.mult, ["T5", "T3"], ["B1"])
            tt("pool", T[7][:], T[5][:], T[4][:], ALU.mult, ["T5", "T4"], ["T7"])
            tt("dve", B[2][:], T[7][:], T[2][:], ALU.mult, ["T7", "T2"], ["B2"])
            tt("pool", B[3][:], T[6][:], T[2][:], ALU.mult, ["T6", "T2"], ["B3"])
            cp("act", B[4][:], V_[:], ["rkv2"], ["B4"])
            transpose_tm_to_fm(B[1], "B1", arT[:, :, 0, :], "arT", "act")
            transpose_tm_to_fm(B[0], "B0", arT[:, :, 1, :], "arT", "dve")
            bT = v3(B[5][:], 128)
            kT = v3(B[6][:], 128)
            transpose_tm_to_fm(B[2], "B2", bT, "B5", "act")
            transpose_tm_to_fm(B[3], "B3", kT, "B6", "dve")

            Vb = B[4]
            bank_pool[0] = 3
            pending.extend(make_side_chunks(tl, v, hTc, hTn))
            tt("pool", T[7][:], R_[:], T[6][:], ALU.mult, ["rkv0", "T6"], ["T7"])
            tt("pool", T[7][:], T[7][:], bc[RK][:], ALU.mult, ["T7", "bc2"], ["T7"])
            red(sm[:, 32:48], v3(T[7][:], 64), ["T7"], ["sm32"])
            for h in range(16):
                ts("pool", T[5][:, h * 64:(h + 1) * 64], V_[:, h * 64:(h + 1) * 64], sm[:, 32 + h:33 + h], None, ALU.mult, None,
                   ["rkv2", "sm32"], ["T5"])
            tt("pool", T[5][:], T[5][:], bc[GNB][:], ALU.add, ["T5", "bc4"], ["T5"])
            for hg in range(2):
                if hg == 1 and cx.get("mid_hook") is not None:
                    while pending and getattr(pending[0], "kind", None) == "ffn":
                        pending.pop(0)()
                    cx["mid_hook"]()
                heads = list(range(hg * 8, hg * 8 + 8))
                mask2 = MT2[v].unsqueeze(1).to_broadcast([128, 2, 256])
                for lhsbuf, lhn, dst, dstn in ((bT, "B5", M2a, "M2a"), (kT, "B6", M2b, "M2b")):
                    for (ja, jb) in ((0, 2), (4, 6), (1, 3), (5, 7)):
                        bank, bn = next_bank()
                        for jj, j in enumerate((ja, jb)):
                            h = heads[j]
                            c, pb = h // 2, 64 * (h % 2)
                            mm(bank[:, jj * 256:(jj + 1) * 256], lhsbuf[pb:pb + 64, c, :],
                               arT[pb:pb + 64, c, :, :].rearrange("p a t -> p (a t)"), True, True, [lhn, "arT"], [bn])
                        tt("dve", dst[:, ja:jb + 1:2, :], v3(bank, 256), mask2, ALU.mult, [bn, "csb"], [dstn])
                        pump(force=(ja in (4, 5)))
                maskn = MSN[v].unsqueeze(1).to_broadcast([128, 4, 128])
                for par in range(2):
                    bank, bn = next_bank()
                    for jj in range(4):
                        j = 2 * jj + par
                        h = heads[j]
                        c, pb = h // 2, 64 * (h % 2)
                        mm(bank[:, jj * 128:(jj + 1) * 128], arT[pb:pb + 64, c, 0, :], bT[pb:pb + 64, c, :], True, True,
                           ["arT", "B5"], [bn])
                    tt("dve", Xb[0][:, par::2, :], v3(bank, 128), maskn, ALU.mult, [bn, "csb"], ["X0"])
                    pump(force=(par == 1))
                tt("dve", PTb[0][:], M2a[:, :, 0:128], idb[:].unsqueeze(1).to_broadcast([128, 8, 128]), ALU.add,
                   ["M2a", "idb"], ["PT0"])
                for k in range(6):
                    cur, nxt = k % 2, (k + 1) % 2
                    Xc, Xcn = Xb[cur], "X%d" % cur
                    if k == 0:
                        XTc, XTcn = M2a[:, :, 0:128], "M2a"
                    else:
                        XTc, XTcn = XTb[cur][:], "XT%d" % cur
                    for jq in range(2):
                        bank, bn = next_bank()
                        for jj in range(4):
                            j = 4 * jq + jj
                            mm(bank[:, jj * 128:(jj + 1) * 128], XTc[:, j, :], Xc[:, j, :], True, True, [XTcn, Xcn], [bn])
                        cp("act", Xb[nxt][:, 4 * jq:4 * jq + 4, :], v3(bank, 128), [bn], ["X%d" % nxt])
                        pump(force=(jq == 1))
                    if k < 5:
                        for jq in range(2):
                            bank, bn = next_bank()
                            for jj in range(4):
                                j = 4 * jq + jj
                                mm(bank[:, jj * 128:(jj + 1) * 128], Xc[:, j, :], XTc[:, j, :], True, True, [Xcn, XTcn], [bn])
                            cp("act" if jq == 0 else "dve", XTb[nxt][:, 4 * jq:4 * jq + 4, :], v3(bank, 128), [bn], ["XT%d" % nxt])
                            pump()
                    for jq in range(2):
                        bank, bn = next_bank()
                        for jj in range(4):
                            j = 4 * jq + jj
                            mm(bank[:, jj * 128:(jj + 1) * 128], Xb[nxt][:, j, :], PTb[cur][:, j, :], True, True,
                               ["X%d" % nxt, "PT%d" % cur], [bn])
                        tt("dve", PTb[nxt][:, 4 * jq:4 * jq + 4, :], v3(bank, 128), PTb[cur][:, 4 * jq:4 * jq + 4, :], ALU.add,
                           [bn, "PT%d" % cur], ["PT%d" % nxt])
                        pump(force=(jq == 1))
                PTf, PTfn = PTb[0], "PT0"
                for par in range(2):
                    bank, bn = next_bank()
                    for jj in range(4):
                        j = 2 * jj + par
                        h = heads[j]
                        c, pb = h // 2, 64 * (h % 2)
                        o = bank[:, jj * 64:(jj + 1) * 64]
                        for (r0, nr, slot) in segs:
                            mm(bank[r0:r0 + nr, jj * 64:(jj + 1) * 64], arT[pb:pb + 64, c, 0, r0:r0 + nr], Hb[slot][pb:pb + 64, c, :],
                               True, False, ["arT", "Hb%d" % slot], [bn])
                        mm(o, M2b[:, j, 0:128], Vb[:, h * 64:(h + 1) * 64], False, True, ["M2b", "B4"], [bn])
                    cp("dve", W1b[:, par::2, :], v3(bank[:, 0:256], 64), [bn], ["W1b"])
                bank, bn = next_bank()
                for j in range(8):
                    mm(bank[:, j * 64:(j + 1) * 64], PTf[:, j, :], W1b[:, j, :], True, True, [PTfn, "W1b"], [bn])
                cp("act", Ub[:], v3(bank, 64), [bn], ["Ub"])
                for par in range(2):
                    bank, bn = next_bank()
                    for jj in range(4):
                        j = 2 * jj + par
                        h = heads[j]
                        c, pb = h // 2, 64 * (h % 2)
                        o = bank[:, jj * 64:(jj + 1) * 64]
                        for (r0, nr, slot) in segs:
                            mm(bank[r0:r0 + nr, jj * 64:(jj + 1) * 64], arT[pb:pb + 64, c, 1, r0:r0 + nr], Hb[slot][pb:pb + 64, c, :],
                               True, False, ["arT", "Hb%d" % slot], [bn])
                        mm(o, M2a[:, j, 128:256], Ub[:, j, :], False, False, ["M2a", "Ub"], [bn])
                        mm(o, M2b[:, j, 128:256], Vb[:, h * 64:(h + 1) * 64], False, True, ["M2b", "B4"], [bn])
                    cp("dve", v3(T[1][:], 64)[:, hg * 8 + par:hg * 8 + 8:2, :], v3(bank[:, 0:256], 64), [bn], ["T1"])
                hbanks = []
                for si, (r0, nr, slot) in enumerate(segs):
                    hbk, hbn = (PD[:], "PD") if si == 0 else next_bank()
                    if si > 0 and hbn == "PD":
                        hbk, hbn = next_bank()
                    hbanks.append((hbk, hbn))
                    for j in range(8):
                        h = heads[j]
                        c, pb = h // 2, 64 * (h % 2)
                        cl = c - hg * 4
                        o = hbk[pb:pb + 64, cl * 64:(cl + 1) * 64]
                        mm(o, B[2][r0:r0 + nr, h * 64:(h + 1) * 64], Ub[r0:r0 + nr, j, :], True, False, ["B2", "Ub"], [hbn])
                        mm(o, B[3][r0:r0 + nr, h * 64:(h + 1) * 64], Vb[r0:r0 + nr, h * 64:(h + 1) * 64], False, True,
                           ["B3", "B4"], [hbn])
                for si, (r0, nr, slot) in enumerate(segs):
                    hsl = Hf[slot][:, hg * 4:hg * 4 + 4, :]
                    hn = "Hf%d" % slot
                    hbk, hbn = hbanks[si]
                    tt("dve", hsl, v3(hbk[:, 0:256], 64), hsl, ALU.add, [hbn, hn], [hn])
                    tt("dve", hsl, hsl, gC[:, hg * 4:hg * 4 + 4, si:si + 1].to_broadcast([128, 4, 64]), ALU.mult,
                       [hn, "gC"], [hn])
                    cp("act", Hb[slot][:, hg * 4:hg * 4 + 4, :], hsl, [hn], ["Hb%d" % slot])

            drain()
            bank_pool[0] = 6
            if tl["last"]:
                if v == 0:
                    emit_state_out(0, wkvp)
                else:
                    for sl in range(2):
                        emit_state_out(sl, wkvs[2 * tl["samp"] + sl])

        def st_E(cx):
            tl = cx["tl"]
            ti = cx["ti"]
            v = tl["v"]
            segs = tl["segs"]
            nseg = len(segs)
            xb = xt[ti % 2]
            xbn = "xt%d" % (ti % 2)
            hTc = hT[ti % 2]
            hTn = "hT%d" % (ti % 2)
            hTprev = hT[(ti + 1) % 2]
            hTprevn = "hT%d" % ((ti + 1) % 2)
            R_, K_, V_ = Rk[0], Rk[1], Rk[2]
            oaT = v3(B[1][:], 128)
            mT = v3(B[3][:], 128)
            Y = T[1]
            for blk in range(2):
                cs = slice(blk * 512, (blk + 1) * 512)
                mm(PC[:, cs], lin[:, 1, :], lw[2][:, cs], True, True, ["lin", "lw2"], ["PC%d" % blk])
            red(sm[:, 48:64], v3(Y[:], 64), ["T1"], ["sm48"])
            act(T[7][:], Y[:], AF.Square, ["T1"], ["T7"])
            red(sm[:, 64:80], v3(T[7][:], 64), ["T7"], ["sm64"])
            ts("dve", sm[:, 48:64], sm[:, 48:64], 1.0 / 64, None, ALU.mult, None, ["sm48"], ["sm48"])
            tt("dve", sm[:, 80:96], sm[:, 48:64], sm[:, 48:64], ALU.mult, ["sm48"], ["sm80"])
            stt(sm[:, 64:80], sm[:, 64:80], 1.0 / 64, sm[:, 80:96], ALU.mult, ALU.subtract, ["sm64", "sm80"], ["sm64"])
            act(sm[:, 64:80], sm[:, 64:80], AF.Sqrt, ["sm64"], ["sm64"], bias=GN_EPS)
            recip(sm[:, 64:80], sm[:, 64:80], ["sm64"], ["sm64"])
            bch = lambda a: a.unsqueeze(2).to_broadcast([128, 16, 64])
            tt("dve", v3(Y[:], 64), v3(Y[:], 64), bch(sm[:, 48:64]), ALU.subtract, ["T1", "sm48"], ["T1"])
            tt("dve", v3(Y[:], 64), v3(Y[:], 64), bch(sm[:, 64:80]), ALU.mult, ["T1", "sm64"], ["T1"])
            tt("dve", Y[:], Y[:], bc[GNG][:], ALU.mult, ["T1", "bc3"], ["T1"])
            tt("dve", Y[:], Y[:], T[5][:], ALU.add, ["T1", "T5"], ["T1"])
            tt("dve", B[0][:], Y[:], PC[:], ALU.mult, ["T1", "PC0", "PC1"], ["B0"])
            oaT = v3(B[1][:], 128)
            transpose_tm_to_fm(B[0], "B0", oaT, "B1", "act")

        def st_G(cx):
            tl = cx["tl"]
            ti = cx["ti"]
            v = tl["v"]
            segs = tl["segs"]
            nseg = len(segs)
            xb = xt[ti % 2]
            xbn = "xt%d" % (ti % 2)
            hTc = hT[ti % 2]
            hTn = "hT%d" % (ti % 2)
            hTprev = hT[(ti + 1) % 2]
            hTprevn = "hT%d" % ((ti + 1) % 2)
            R_, K_, V_ = Rk[0], Rk[1], Rk[2]
            oaT = v3(B[1][:], 128)
            mT = v3(B[3][:], 128)
            for b in range(2):
                rp, rpn = load_unit(U_PA + b)
                wp_ = v3(rp[:], 512)
                for cc in range(4):
                    for kc in range(8):
                        mm(PC[:, b * 512 + cc * 128:b * 512 + (cc + 1) * 128], wp_[:, kc, cc * 128:(cc + 1) * 128], oaT[:, kc, :],
                           kc == 0, kc == 7, [rpn, "B1"], ["PC%d" % b])
            tt("dve", T[5][:], PC[:], T[3][:], ALU.mult, ["PC0", "PC1", "T3"], ["T5"])
            mT = v3(B[3][:], 128)
            tt("pool", B[3][:], T[5][:], T[6][:], ALU.add, ["T5", "T6"], ["B3"])

        def st_H(cx):
            tl = cx["tl"]
            ti = cx["ti"]
            v = tl["v"]
            segs = tl["segs"]
            nseg = len(segs)
            xb = xt[ti % 2]
            xbn = "xt%d" % (ti % 2)
            hTc = hT[ti % 2]
            hTn = "hT%d" % (ti % 2)
            hTprev = hT[(ti + 1) % 2]
            hTprevn = "hT%d" % ((ti + 1) % 2)
            R_, K_, V_ = Rk[0], Rk[1], Rk[2]
            oaT = v3(B[1][:], 128)
            mT = v3(B[3][:], 128)
            for b in range(2):
                ro, ron = load_unit(U_PA + 4 + b)
                wo_ = v3(ro[:], 512)
                for kc in range(8):
                    mm(PB[:, b * 512:(b + 1) * 512], mT[:, kc, :], wo_[:, kc, :], kc == 0, kc == 7, ["B3", ron], ["PB%d" % b])
            cp("dve", T[0][:], PB[:], ["PB0", "PB1"], ["T0"])
            rstd = rms_rstd(T[0][:], ["T0"], T[2][:], "T2", RMS_EPS)
            stt(T[0][:], T[0][:], rstd, bc[GPM][:], ALU.mult, ALU.mult, ["T0", "sm1", "bc7"], ["T0"])
            tt("dve", xb[:], xb[:], T[0][:], ALU.add, [xbn, "T0"], [xbn])

        def st_I(cx):
            tl = cx["tl"]
            ti = cx["ti"]
            v = tl["v"]
            segs = tl["segs"]
            nseg = len(segs)
            xb = xt[ti % 2]
            xbn = "xt%d" % (ti % 2)
            hTc = hT[ti % 2]
            hTn = "hT%d" % (ti % 2)
            hTprev = hT[(ti + 1) % 2]
            hTprevn = "hT%d" % ((ti + 1) % 2)
            R_, K_, V_ = Rk[0], Rk[1], Rk[2]
            oaT = v3(B[1][:], 128)
            mT = v3(B[3][:], 128)
            rstd = rms_rstd(xb[:], [xbn], T[0][:], "T0", RMS_EPS)
            ts("dve", hb[:], xb[:], rstd, None, ALU.mult, None, [xbn, "sm1"], ["hb"])
            transpose_tm_to_fm(hb, "hb", fTb[:], "fTb", "act")
            pending.extend(make_ffn_chunks(xb, xbn, tl["y"]))


        cxs = [dict(tl=tl, ti=ti) for ti, tl in enumerate(tiles)]
        st_A(cxs[0])
        st_A2(cxs[0])
        st_B1(cxs[0], range(6))
        st_B2(cxs[0])
        for ti, cx in enumerate(cxs):
            nx = cxs[ti + 1] if ti + 1 < len(cxs) else None
            piped = PIPELINE and nx is not None and not nx["tl"]["last"]
            if piped:
                cx["mid_hook"] = (lambda nx=nx: st_A(nx))
            st_CD(cx)
            if piped:
                st_A2(nx)
                bank_pool[0] = 2
                st_B1(nx, [0, 1, 2, 3])
                st_E(cx)
                st_G(cx)
                st_B1(nx, [4])
                st_H(cx)
                st_B1(nx, [5])
                st_I(cx)
                bank_pool[0] = 6
                st_B2(nx)
            else:
                st_E(cx)
                st_G(cx)
                st_H(cx)
                st_I(cx)
                if nx is not None:
                    st_A(nx)
                    st_A2(nx)
                    st_B1(nx, range(6))
                    st_B2(nx)

        drain()
        S.finish("sp")
        with nc.Block() as block:
            S.replay(block)
    return nc


_CACHE = {}


def _consts():
    s = np.arange(128)[:, None]
    t = np.arange(128)[None, :]
    same = (s // 64) == (t // 64)
    ms_p = (s < t).astype(np.float32)
    mi_p = (s <= t).astype(np.float32)
    ms_s = ((s < t) & same).astype(np.float32)
    mi_s = ((s <= t) & same).astype(np.float32)
    msn_p = ms_p.T.copy()
    msn_s = ms_s.T.copy()
    seg_p = np.stack([np.ones(128, np.float32), np.zeros(128, np.float32)], axis=1)
    seg_s = np.stack([(np.arange(128) < 64).astype(np.float32), (np.arange(128) >= 64).astype(np.float32)], axis=1)
    cst = np.concatenate([np.eye(128, dtype=np.float32), ms_p, mi_p, ms_s, mi_s, msn_p, msn_s, seg_p, seg_s], axis=1)
    sel = np.zeros((2, 384), np.float32)
    sel[0, 0] = 1.0
    sel[1, 64] = 1.0
    sel[0, 128:256] = 1.0
    sel[1, 256:384] = 1.0
    return np.ascontiguousarray(cst), sel


def make_in_maps(inp, n_ptiles=N_PTILES, n_stiles=N_STILES, ncores=NCORE):
    f = lambda a: np.ascontiguousarray(np.asarray(a, dtype=np.float32))
    bcv = lambda a: np.broadcast_to(f(a).reshape(1, -1), (128, f(a).size))
    cst, sel = _consts()
    L = 0
    bcp = np.ascontiguousarray(np.stack([
        bcv(inp["k_k"][L]), bcv(inp["k_a"][L]), bcv(inp["r_k"][L]), bcv(inp["gn_g"][L]), bcv(inp["gn_b"][L]),
        bcv(inp["ln_v_g"][L]), bcv(inp["ln_v_b"][L]), bcv(inp["g_post_mix"][L]), bcv(inp["g_post_ffn"][L])]))
    mu_bc = np.ascontiguousarray(bcv(inp["mu_shift"][L]))
    gvec = np.ascontiguousarray(np.concatenate([f(inp["g_pre_mix"][L]).reshape(8, 128).T,
                                                f(inp["g_pre_ffn"][L]).reshape(8, 128).T], axis=1))
    lw = np.zeros((3, 128, D), np.float32)
    lw[0, 0:64] = f(inp["w_lora_w"][L])
    lw[1, 64:128] = f(inp["w_lora_a"][L])
    lw[2] = f(inp["w_lora_g"][L])
    wa0 = np.ascontiguousarray(np.stack([f(inp["w0"][L]), f(inp["a0"][L])]))
    wsp = f(inp["w_spatial"][L])
    wsT_p = np.transpose(wsp, (2, 0, 1))
    idx = np.arange(128) % 64
    wsT_s = np.transpose(wsp[:, idx][:, :, idx], (2, 0, 1))
    wsT = np.ascontiguousarray(np.stack([wsT_p, wsT_s]))
    bs = f(inp["b_spatial"][L])
    bspv = np.ascontiguousarray(np.stack([bs, bs[:, idx]]))
    shared = dict(w_in=f(inp["w_in"][L]), w_pa=f(inp["w_proj_a"][L]), w_pb=f(inp["w_proj_b"][L]), w_o=f(inp["w_o"][L]),
                  w_gate=f(inp["w_gate"][L]), w_up=f(inp["w_up"][L]), w_down=f(inp["w_down"][L]),
                  bcp=bcp, mu_bc=mu_bc, gvec=gvec, lora=lw, wa0=wa0, wsT=wsT, bspv=bspv, cst=cst, selc=sel)
    xpr = f(inp["x_prompt"])
    xsm = f(inp["x_sample"])
    ssh = f(inp["state_shift"][L])[:, 0, :]
    swk = f(inp["state_wkv"][L])
    nseq = 2 * n_stiles
    maps = []
    for c in range(ncores):
        m = dict(shared)
        m["xp"] = np.ascontiguousarray(xpr[c, :n_ptiles * 128])
        m["xs"] = np.ascontiguousarray(xsm[c * nseq:(c + 1) * nseq].reshape(nseq * 64, D))
        m["sshift"] = np.ascontiguousarray(ssh[c * nseq:(c + 1) * nseq])
        m["swkv"] = np.ascontiguousarray(swk[c * nseq:(c + 1) * nseq])
        maps.append(m)
    return maps


def assemble(results, n_ptiles=N_PTILES, n_stiles=N_STILES):
    nseq = 2 * n_stiles
    yp = np.stack([r["yp"] for r in results])
    ys = np.concatenate([r["ys"].reshape(nseq, 64, D) for r in results])
    shp = np.stack([r["shp"] for r in results])[None]
    wkvp = np.stack([r["wkvp"] for r in results])[None]
    shs = np.concatenate([r["shs"] for r in results])[None, :, None, :]
    wkvs = np.concatenate([r["wkvs"] for r in results])[None]
    vs = np.concatenate([r["vs"].reshape(nseq, 64, D) for r in results])[None]
    f = lambda a: np.ascontiguousarray(a, dtype=np.float32)
    return (f(yp), f(ys), f(shp), f(wkvp), f(shs), f(wkvs), f(vs))


def kernel(**inputs):
    if "nc" not in _CACHE:
        _CACHE["nc"] = build_program()
    nc = _CACHE["nc"]
    in_maps = make_in_maps(inputs)
    res = run_bass_kernel_spmd(nc, in_maps, core_ids=list(range(NCORE)))
    return assemble(res.results)
```

```python
import contextlib
import numpy as np
import concourse.bass as bass
import concourse.mybir as mybir
from concourse.bass_utils import run_bass_kernel_spmd

F32 = mybir.dt.float32
BF16 = mybir.dt.bfloat16
AF = mybir.ActivationFunctionType
ALU = mybir.AluOpType
AX = mybir.AxisListType

D = 1024
SEQ = 4096
NCORE = 8
SHIFT_W = 3328
IN_W = 7424
DFF = 2816
NFC = 22
RMS_EPS = 1e-6
GN_EPS = 64e-5
LN_EPS = 1e-5
C0 = float(np.exp(-0.5))
NSLOT = 4
NUNIT = 44
N_PTILES = SEQ // 128
N_STILES = 2
PIPELINE = True
PUMP_EVERY = 3
DMA_ENG_MAP = {"pool": "act"}


class Sched:
    ENGS = ("pe", "dve", "act", "pool", "sp")

    def __init__(self, nc, stack, n_dma_sems=28):
        self.nc = nc
        self.ops = {e: [] for e in self.ENGS}
        self.cnt = {e: 0 for e in self.ENGS}
        self.waited = {e: {} for e in self.ENGS}
        self.res = {}
        self.dummy = {}
        self.excl = set()
        self.semobj = {}
        for e in self.ENGS:
            self.semobj["c_" + e] = stack.enter_context(nc.semaphore("c_" + e))
        for i in range(n_dma_sems):
            self.semobj["d%d" % i] = stack.enter_context(nc.semaphore("d%d" % i))
        self.duse = [0] * n_dma_sems
        self.dpool = {"sp": list(range(0, 14)), "act": list(range(14, 22)), "pool": list(range(22, n_dma_sems))}
        self.drr = {"sp": 0, "act": 0, "pool": 0}

    def _deps(self, eng, reads, writes, xreads=()):
        need = {}

        def add(tok, raw):
            if tok is None:
                return
            key, val, teng = tok
            if teng == eng and eng == "pe":
                return
            if need.get(key, 0) < val:
                need[key] = val

        for r in reads:
            st = self.res.get(r)
            if st is not None:
                add(st["w"], True)
        for w in writes:
            st = self.res.get(w)
            if st is not None:
                add(st["w"], False)
                for key, (val, teng) in st["r"].items():
                    add((key, val, teng), False)
        for r in xreads:
            st = self.res.get(r)
            if st is not None:
                add(st["w"], True)
                for key, (val, teng) in st["r"].items():
                    if teng != eng:
                        add((key, val, teng), False)
        out = []
        wd = self.waited[eng]
        for key, val in need.items():
            if wd.get(key, 0) < val:
                wd[key] = val
                out.append((key, val))
        return out

    def _commit(self, tok, reads, writes):
        key, val, teng = tok
        for r in reads:
            st = self.res.setdefault(r, {"w": None, "r": {}})
            old = st["r"].get(key)
            if old is None or old[0] < val:
                st["r"][key] = (val, teng)
        for w in writes:
            self.res[w] = {"w": tok, "r": {}}

    def op(self, eng, fn, reads=(), writes=()):
        xreads = [r for r in reads if r in self.excl]
        reads = [r for r in reads if r not in self.excl]
        waits = self._deps(eng, reads, writes, xreads)
        reads = list(reads) + xreads
        dummy = self.dummy.get(eng)
        if dummy is not None and any(k == "c_pe" for k, _ in waits):
            self.cnt[eng] += 1
            self.ops[eng].append((waits, dummy, "c_" + eng, 1))
            waits = []
        self.cnt[eng] += 1
        tok = ("c_" + eng, self.cnt[eng], eng)
        self.ops[eng].append((waits, fn, "c_" + eng, 1))
        self._commit(tok, reads, writes)

    def dma(self, eng, fn, reads=(), writes=()):
        eng = DMA_ENG_MAP.get(eng, eng)
        waits = self._deps(eng, reads, writes)
        pl = self.dpool[eng]
        j = pl[self.drr[eng] % len(pl)]
        self.drr[eng] += 1
        key = "d%d" % j
        if self.duse[j] > 0:
            prev = 16 * self.duse[j]
            if self.waited[eng].get(key, 0) < prev:
                self.waited[eng][key] = prev
                waits.append((key, prev))
        self.duse[j] += 1
        tok = (key, 16 * self.duse[j], None)
        self.ops[eng].append((waits, fn, key, 16))
        self._commit(tok, reads, writes)

    def finish(self, eng="sp"):
        waits = []
        for j, u in enumerate(self.duse):
            if u > 0:
                waits.append(("d%d" % j, 16 * u))
        for e in self.ENGS:
            if e != eng and self.cnt[e] > 0:
                waits.append(("c_" + e, self.cnt[e]))
        self.ops[eng].append((waits, None, None, 0))

    def replay(self, block):
        def run(e, handle):
            for waits, fn, key, inc in self.ops[e]:
                for k, v in waits:
                    handle.wait_ge(self.semobj[k], v)
                if fn is not None:
                    fn(handle).then_inc(self.semobj[key], inc)

        @block.sync
        def _(sync):
            run("sp", sync)

        @block.tensor
        def _(tensor):
            run("pe", tensor)

        @block.vector
        def _(vector):
            run("dve", vector)

        @block.scalar
        def _(scalar):
            run("act", scalar)

        @block.gpsimd
        def _(gpsimd):
            run("pool", gpsimd)


def build_program(n_ptiles=N_PTILES, n_stiles=N_STILES):
    nc = bass.Bass("TRN2", target_bir_lowering=False)
    npt, nst = n_ptiles, n_stiles
    nseq_s = 2 * nst

    def din(name, shape, dt=F32):
        return nc.dram_tensor(name, list(shape), dt, kind="ExternalInput").ap()

    def dout(name, shape):
        return nc.dram_tensor(name, list(shape), F32, kind="ExternalOutput").ap()

    xp = din("xp", [npt * 128, D])
    xs = din("xs", [nst * 128, D])
    sshift = din("sshift", [nseq_s, SHIFT_W])
    swkv = din("swkv", [nseq_s, 16, 64, 64])
    w_in = din("w_in", [D, IN_W])
    w_pa = din("w_pa", [D, D])
    w_pb = din("w_pb", [D, D])
    w_o = din("w_o", [D, D])
    w_gate = din("w_gate", [D, DFF])
    w_up = din("w_up", [D, DFF])
    w_down = din("w_down", [DFF, D])
    bcp = din("bcp", [9, 128, D])
    mu_bc = din("mu_bc", [128, SHIFT_W])
    gvec = din("gvec", [128, 16])
    lora = din("lora", [3, 128, D])
    wa0 = din("wa0", [2, D])
    wsT = din("wsT", [2, 128, 8, 128])
    bspv = din("bspv", [2, 8, 128])
    cst = din("cst", [128, 128 * 7 + 4])
    selc = din("selc", [2, 384])

    yp = dout("yp", [npt * 128, D])
    ys = dout("ys", [nst * 128, D])
    shp = dout("shp", [1, SHIFT_W])
    wkvp = dout("wkvp", [16, 64, 64])
    shs = dout("shs", [nseq_s, SHIFT_W])
    wkvs = dout("wkvs", [nseq_s, 16, 64, 64])
    vs = dout("vs", [nst * 128, D])

    scr = nc.dram_tensor("scr", [NUNIT, 128, 4096], BF16, kind="Internal").ap()
    ssmu_d = nc.dram_tensor("ssmu_d", [nst, 2, SHIFT_W], BF16, kind="Internal").ap()

    with contextlib.ExitStack() as st:
        S = Sched(nc, st)

        def sb(name, shape, dt=F32):
            return st.enter_context(nc.sbuf_tensor(name, list(shape), dt))

        def ps(name, shape, dt=F32):
            return st.enter_context(nc.psum_tensor(name, list(shape), dt))

        def mm(out, lhsT, rhs, start, stop, r, w):
            S.op("pe", lambda e: e.matmul(out, lhsT=lhsT, rhs=rhs, start=start, stop=stop), r, w)

        def tr(out, in_, ident, r, w):
            S.op("pe", lambda e: e.transpose(out=out, in_=in_, identity=ident), r, w)

        def act(out, in_, func, r, w, scale=None, bias=None, accum=None):
            kw = {}
            if scale is not None:
                kw["scale"] = scale
            if bias is not None:
                kw["bias"] = bias
            if accum is not None:
                kw["accum_out"] = accum
            S.op("act", lambda e: e.activation(out=out, in_=in_, func=func, **kw), r, w)

        def tt(eng, out, in0, in1, op, r, w):
            S.op(eng, lambda e: e.tensor_tensor(out=out, in0=in0, in1=in1, op=op), r, w)

        def ts(eng, out, in0, s1, s2, op0, op1, r, w):
            if op1 is None:
                S.op(eng, lambda e: e.tensor_scalar(out=out, in0=in0, scalar1=s1, scalar2=None, op0=op0), r, w)
            else:
                S.op(eng, lambda e: e.tensor_scalar(out=out, in0=in0, scalar1=s1, scalar2=s2, op0=op0, op1=op1), r, w)

        def stt(out, in0, scalar, in1, op0, op1, r, w):
            S.op("dve", lambda e: e.scalar_tensor_tensor(out=out, in0=in0, scalar=scalar, in1=in1, op0=op0, op1=op1), r, w)

        def cp(eng, out, in_, r, w):
            if eng == "act":
                S.op("act", lambda e: e.activation(out=out, in_=in_, func=AF.Copy), r, w)
            else:
                S.op(eng, lambda e: e.tensor_copy(out=out, in_=in_), r, w)

        def red(out, in_, r, w):
            S.op("dve", lambda e: e.tensor_reduce(out=out, in_=in_, axis=AX.X, op=ALU.add), r, w)

        def recip(out, in_, r, w):
            S.op("dve", lambda e: e.reciprocal(out=out, in_=in_), r, w)

        def mset(eng, ap, val, w):
            S.op(eng, lambda e: e.memset(ap, val), (), w)

        def ld(out, in_, r, w, eng="sp"):
            S.dma(eng, lambda e: e.dma_start(out=out, in_=in_), r, w)

        def ldnc(out, in_, r, w, eng="sp"):
            S.dma(eng, lambda e: e.dma_start(out=out, in_=in_, allow_slow_non_contiguous=True), r, w)

        bc = [sb("bc%d" % i, [128, D]) for i in range(9)]
        KK, KA, RK, GNG, GNB, LNG, LNB, GPM, GPF = range(9)
        lw = [sb("lw%d" % i, [128, D], BF16) for i in range(3)]
        w0hl = sb("w0hl", [2, 3, D], BF16)
        ones2 = sb("ones2", [2, 128], BF16)
        wst = [sb("wst%d" % v, [128, 8, 128], BF16) for v in range(2)]
        osel = sb("osel", [2, 2, 128], BF16)
        csb = sb("csb", [128, 128 * 7 + 4])
        idb = sb("idb", [128, 128], BF16)
        sel = sb("sel", [2, 128], BF16)
        gv = sb("gv", [128, 16])
        idf = csb[:, 0:128]
        MT2 = [csb[:, 128:384], csb[:, 384:640]]
        TRI = [csb[:, 256:384], csb[:, 512:640]]
        MSN = [csb[:, 640:768], csb[:, 768:896]]
        SEGI = [csb[:, 896:898], csb[:, 898:900]]

        xt = [sb("xt%d" % i, [128, D]) for i in range(2)]
        hb = sb("hb", [128, D], BF16)
        hT = [sb("hT%d" % i, [128, 8, 128], BF16) for i in range(2)]
        hTp = sb("hTp", [128, 8, 128], BF16)
        Rk = [sb("rkv%d" % i, [128, D]) for i in range(3)]
        T = [sb("T%d" % i, [128, D]) for i in range(8)]
        fTb = sb("fTb", [128, 8, 128], BF16)
        B = [sb("B%d" % i, [128, D], BF16) for i in range(7)]
        arT = sb("arT", [128, 8, 2, 128], BF16)
        Xb = [sb("X%d" % i, [128, 8, 128], BF16) for i in range(2)]
        XTb = [sb("XT%d" % i, [128, 8, 128], BF16) for i in range(2)]
        PTb = [sb("PT%d" % i, [128, 8, 128], BF16) for i in range(2)]
        M2a = sb("M2a", [128, 8, 256], BF16)
        M2b = sb("M2b", [128, 8, 256], BF16)
        W1b = sb("W1b", [128, 8, 64], BF16)
        Ub = sb("Ub", [128, 8, 64], BF16)
        actT = sb("actT", [128, NFC, 128], BF16)
        Hf = [sb("Hf%d" % i, [128, 8, 64]) for i in range(2)]
        Hb = [sb("Hb%d" % i, [128, 8, 64], BF16) for i in range(2)]
        lin = sb("lin", [128, 2, 128], BF16)
        ssb = sb("ssb", [2, 512], BF16)
        sm = sb("sm", [128, 112])
        gC = sb("gC", [128, 8, 2])
        ring = [sb("ring%d" % i, [128, 4096], BF16) for i in range(NSLOT)]

        dmy = sb("dmy", [128, 8])
        PA = ps("PA", [128, 1024])
        PB = ps("PB", [128, 1024])
        PC = ps("PC", [128, 1024])
        PD = ps("PD", [128, 512])
        PT = ps("PT", [128, 1024], BF16)
        S.excl = {"PA0", "PA1", "PB0", "PB1", "PC0", "PC1", "PD", "PT"}
        BANKS = [(PA[:, 0:512], "PA0"), (PA[:, 512:1024], "PA1"), (PB[:, 0:512], "PB0"),
                 (PB[:, 512:1024], "PB1"), (PC[:, 0:512], "PC0"), (PC[:, 512:1024], "PC1")]

        def v3(ap, d):
            return ap.rearrange("p (a d) -> p a d", d=d)

        mset("dve", dmy[:], 0.0, ["dmy"])
        for i in range(9):
            ld(bc[i][:], bcp[i], (), ["bc%d" % i])
        ld(csb[:], cst, (), ["csb"])
        ld(gv[:], gvec, (), ["gv"])
        cp("dve", idb[:], idf, ["csb"], ["idb"])
        mset("dve", ones2[:], 1.0, ["ones2"])
        ld(T[0][0:2, 0:384], selc, (), ["T0"])
        cp("dve", sel[:], T[0][0:2, 0:128], ["T0"], ["sel"])
        cp("dve", osel[:].rearrange("o a t -> o (a t)"), T[0][0:2, 128:384], ["T0"], ["osel"])
        ld(T[7][0:2, :], bspv.rearrange("v g t -> v (g t)"), (), ["T7"])
        cp("dve", w0hl[:, 2, :], T[7][0:2, :], ["T7"], ["w0hl"])
        for i in range(3):
            ld(T[1 + (i % 2)][:], lora[i], (), ["T%d" % (1 + (i % 2))])
            cp("dve", lw[i][:], T[1 + (i % 2)][:], ["T%d" % (1 + (i % 2))], ["lw%d" % i])
        for j in range(2):
            tj = T[3 + j]
            nm = "T%d" % (3 + j)
            ld(tj[0:1, :], wa0[j:j + 1, :], (), [nm])
            ld(tj[1:2, :], wa0[j:j + 1, :], (), [nm])
            cp("dve", w0hl[:, j, :], tj[0:2, :], [nm], ["w0hl"])
            tt("dve", tj[0:2, :], tj[0:2, :], w0hl[:, j, :], ALU.subtract, [nm, "w0hl"], [nm])
            cp("dve", B[0][0:2, j * D:(j + 1) * D] if False else B[j][0:2, :], tj[0:2, :], [nm], ["B%d" % j])
            ld(w0hl[1:2, j, :], B[j][0:1, :], ["B%d" % j], ["w0hl"])
        for v in range(2):
            tv = T[5 + v]
            nm = "T%d" % (5 + v)
            ld(v3(tv[:], 128), wsT[v], (), [nm])
            tt("dve", wst[v][:], v3(tv[:], 128), TRI[v].unsqueeze(1).to_broadcast([128, 8, 128]), ALU.mult,
               [nm, "csb"], ["wst%d" % v])
        for j in range(nst):
            for q in range(4):
                c0 = q * 1024
                c1 = min(SHIFT_W, c0 + 1024)
                n = c1 - c0
                ld(T[0][0:2, 0:n], sshift[2 * j:2 * j + 2, c0:c1], (), ["T0"])
                ld(T[1][0:2, 0:n], mu_bc[0:2, c0:c1], (), ["T1"])
                tt("dve", B[2][0:2, 0:n], T[0][0:2, 0:n], T[1][0:2, 0:n], ALU.mult, ["T0", "T1"], ["B2"])
                ld(ssmu_d[j, :, c0:c1], B[2][0:2, 0:n], ["B2"], ["ssmu_d"], eng="pool")

        w_in_v = w_in.rearrange("(kc p) c -> p kc c", p=128)
        w_pa_v = w_pa.rearrange("(kc p) c -> p kc c", p=128)
        w_pb_v = w_pb.rearrange("(kc p) c -> p kc c", p=128)
        w_o_v = w_o.rearrange("(kc p) c -> p kc c", p=128)
        w_gate_v = w_gate.rearrange("(kc p) c -> p kc c", p=128)
        w_up_v = w_up.rearrange("(kc p) c -> p kc c", p=128)
        w_down_v = w_down.rearrange("(rc p) c -> p rc c", p=128)

        pp_cnt = [0]
        ew_rr = [0]

        def ew_eng():
            ew_rr[0] += 1
            return ("dve", "act")[ew_rr[0] % 2]

        def pp_piece(u, q, src, nk, ncol, gcol, mu_cols, mu_inv):
            i = pp_cnt[0] % 4
            pp_cnt[0] += 1
            stg = T[i]
            snm = "T%d" % i
            ob = B[i]
            onm = "B%d" % i
            n = nk * ncol
            sv = stg[:, 0:n].rearrange("p (k c) -> p k c", c=ncol)
            ov = ob[:, 0:n].rearrange("p (k c) -> p k c", c=ncol)
            ld(sv, src, (), [snm])
            if mu_cols is not None:
                mt = T[4 + i]
                mnm = "T%d" % (4 + i)
                ld(mt[:, 0:ncol], mu_bc[:, mu_cols:mu_cols + ncol], (), [mnm])
                if mu_inv:
                    ts("dve", mt[:, 0:ncol], mt[:, 0:ncol], -1.0, 1.0, ALU.mult, ALU.add, [mnm], [mnm])
                for k in range(nk):
                    stt(ov[:, k, :], sv[:, k, :], gv[:, gcol + k:gcol + k + 1], mt[:, 0:ncol], ALU.mult, ALU.mult,
                        [snm, mnm, "gv"], [onm])
            elif gcol is not None:
                for k in range(nk):
                    e = ew_eng()
                    if e == "act":
                        act(ov[:, k, :], sv[:, k, :], AF.Copy, [snm, "gv"], [onm], scale=gv[:, gcol + k:gcol + k + 1])
                    else:
                        ts(e, ov[:, k, :], sv[:, k, :], gv[:, gcol + k:gcol + k + 1], None, ALU.mult, None,
                           [snm, "gv"], [onm])
            else:
                e = ew_eng()
                cp(e, ob[:, 0:n], stg[:, 0:n], [snm], [onm])
            ld(scr[u][:, q * 1024:q * 1024 + n], ob[:, 0:n], [onm], [("scr", u, q)], eng="pool")

        units = []
        u = 0
        U_RKV = u
        for b in range(6):
            for var in range(2):
                for q in range(4):
                    pp_piece(u, q, w_in_v[:, 2 * q:2 * q + 2, b * 512:(b + 1) * 512], 2, 512, 2 * q, b * 512, var == 0)
                u += 1
        U_LORA = u
        for var in range(2):
            for hh in range(2):
                pp_piece(u, var * 2 + hh, w_in_v[:, 4 * hh:4 * hh + 4, 3072:3328], 4, 256, 4 * hh, 3072, var == 0)
        u += 1
        U_VG = u
        for (base, col0) in ((0, 4352),):
            for b in range(2):
                for q in range(4):
                    pp_piece(u, q, w_in_v[:, 2 * q:2 * q + 2, col0 + b * 512:col0 + (b + 1) * 512], 2, 512, 2 * q, None, False)
                u += 1
        U_U = u
        for b in range(2):
            for q in range(4):
                pp_piece(u, q, w_in_v[:, 2 * q:2 * q + 2, 3328 + b * 512:3328 + (b + 1) * 512], 2, 512, 2 * q, None, False)
            u += 1
        U_GA = u
        for col0 in (5376, 6400):
            for b in range(2):
                for q in range(4):
                    pp_piece(u, q, w_in_v[:, 2 * q:2 * q + 2, col0 + b * 512:col0 + (b + 1) * 512], 2, 512, 2 * q, None, False)
                u += 1
        U_PA = u
        for wv in (w_pa_v, w_pb_v, w_o_v):
            for b in range(2):
                for q in range(4):
                    pp_piece(u, q, wv[:, 2 * q:2 * q + 2, b * 512:(b + 1) * 512], 2, 512, None, None, False)
                u += 1
        U_FFN = u
        for j in range(11):
            for m, wv in enumerate((w_gate_v, w_up_v)):
                for hh in range(2):
                    pp_piece(u, m * 2 + hh, wv[:, 4 * hh:4 * hh + 4, j * 256:(j + 1) * 256], 4, 256, 8 + 4 * hh, None, False)
            u += 1
        U_DOWN = u
        for qd in range(6):
            for rc in range(4):
                if 4 * qd + rc < NFC:
                    pp_piece(u, rc, w_down_v[:, 4 * qd + rc:4 * qd + rc + 1, :], 1, 1024, None, None, False)
            u += 1
        assert u == NUNIT

        ring_ctr = [0]

        def load_unit(uidx, npc=4):
            s = ring_ctr[0] % NSLOT
            ring_ctr[0] += 1
            ld(ring[s][:, 0:npc * 1024], scr[uidx][:, 0:npc * 1024], [("scr", uidx, q) for q in range(npc)], [("ring", s)])
            return ring[s], ("ring", s)

        bank_ctr = [0]

        bank_pool = [6]

        SCAN_BANKS = [BANKS[0], BANKS[1], BANKS[2], (PD[:], "PD")]

        def next_bank():
            if bank_pool[0] == 3:
                b = SCAN_BANKS[bank_ctr[0] % 4]
            else:
                b = BANKS[bank_ctr[0] % bank_pool[0]]
            bank_ctr[0] += 1
            return b

        ptb_ctr = [0]
        PT_ALT = [(PT[:], "PT"), (PD[:].bitcast(BF16), "PD")]

        def transpose_tm_to_fm(src, srcn, dst, dstn, evac_eng):
            pt, ptn = PT_ALT[ptb_ctr[0] % 2]
            ptb_ctr[0] += 1
            for kc in range(8):
                tr(pt[:, kc * 128:(kc + 1) * 128], src[:, kc * 128:(kc + 1) * 128], idb[:], [srcn, "idb"], [ptn])
            cp("dve", dst, pt.rearrange("p (k t) -> p k t", t=128), [ptn], [dstn])

        def rms_rstd(src_ap, srcn, junk, junkn, eps):
            act(junk, src_ap, AF.Square, srcn, [junkn, "sm0"], accum=sm[:, 0:1])
            act(sm[:, 1:2], sm[:, 0:1], AF.Sqrt, ["sm0"], ["sm1"], scale=1.0 / D, bias=eps)
            recip(sm[:, 1:2], sm[:, 1:2], ["sm1"], ["sm1"])
            return sm[:, 1:2]

        tiles = []
        for i in range(npt):
            tiles.append(dict(v=0, x=xp[i * 128:(i + 1) * 128, :], y=yp[i * 128:(i + 1) * 128, :],
                              segs=[(0, 128, 0)], first=(i == 0), last=(i == npt - 1), samp=None))
        for j in range(nst):
            tiles.append(dict(v=1, x=xs[j * 128:(j + 1) * 128, :], y=ys[j * 128:(j + 1) * 128, :],
                              segs=[(0, 64, 0), (64, 64, 1)], first=True, last=True, samp=j))

        def emit_state_out(slot, dst):
            for c in range(8):
                tr(PC[0:64, c * 128:(c + 1) * 128], Hf[slot][:, c, :], idf, ["Hf%d" % slot, "csb"], ["PC0", "PC1"])
            cp("act", T[0][0:64, :], PC[0:64, :], ["PC0", "PC1"], ["T0"])
            ld(dst.rearrange("h v k -> v h k"), T[0][0:64, :].rearrange("v (h k) -> v h k", k=64), ["T0"], [], eng="pool")

        pending = []
        pump_ctr = [0]
        PT32 = PT[:].bitcast(F32)[:, 0:512]

        def pump(force=False):
            pump_ctr[0] += 1
            if pending and force:
                pending.pop(0)()

        def drain():
            while pending:
                pending.pop(0)()

        def make_side_chunks(tl, v, hTc, hTn):
            ch = []
            VG = T[0]
            vnb = B[0]

            def vg_unit(b):
                rv, rvn = load_unit(U_VG + b)
                wv_ = v3(rv[:], 512)
                for kc in range(8):
                    mm(PC[:, b * 512:(b + 1) * 512], hTc[:, kc, :], wv_[:, kc, :], kc == 0, kc == 7, [hTn, rvn], ["PC%d" % b])

            def vg_post():
                cp("act", VG[:], PC[:], ["PC0", "PC1"], ["T0"])
                S.op("dve", lambda e: e.bn_stats(out=sm[:, 96:102], in_=T[0][:, 0:512]), ["T0"], ["sm96"])
                S.op("dve", lambda e: e.bn_stats(out=sm[:, 102:108], in_=T[0][:, 512:1024]), ["T0"], ["sm96"])
                S.op("dve", lambda e: e.bn_aggr(out=sm[:, 108:110], in_=sm[:, 96:108]), ["sm96"], ["sm108"])
                act(sm[:, 110:111], sm[:, 109:110], AF.Sqrt, ["sm108"], ["sm110"], bias=LN_EPS)
                recip(sm[:, 110:111], sm[:, 110:111], ["sm110"], ["sm110"])
                ts("dve", VG[:], VG[:], sm[:, 108:109], sm[:, 110:111], ALU.subtract, ALU.mult, ["T0", "sm108", "sm110"], ["T0"])
                tt("pool", VG[:], VG[:], bc[LNG][:], ALU.mult, ["T0", "bc5"], ["T0"])
                tt("pool", VG[:], VG[:], bc[LNB][:], ALU.add, ["T0", "bc6"], ["T0"])
                if tl["samp"] is not None:
                    j = tl["samp"]
                    ld(vs[j * 128:(j + 1) * 128, :], VG[:], ["T0"], [], eng="pool")
                cp("act", vnb[:], VG[:], ["T0"], ["B0"])

            def fm_unit(ubase, b, srcT, srcn):
                ru, run_ = load_unit(ubase + b)
                wu_ = v3(ru[:], 512)
                for cc in range(4):
                    for kc in range(8):
                        mm(PC[:, b * 512 + cc * 128:b * 512 + (cc + 1) * 128], wu_[:, kc, cc * 128:(cc + 1) * 128], srcT[:, kc, :],
                           kc == 0, kc == 7, [run_, srcn], ["PC%d" % b])

            def u_evac():
                cp("act", T[2][:], PC[:], ["PC0", "PC1"], ["T2"])

            def spatial():
                for g in range(8):
                    o = PC[:, g * 128:(g + 1) * 128]
                    bn = "PC%d" % (g // 4)
                    mm(o, vnb[:, g * 128:(g + 1) * 128], wst[v][:, g, :], True, False, ["B0", "wst%d" % v], [bn])
                    mm(o, osel[:, v, :], w0hl[:, 2, g * 128:(g + 1) * 128], False, True, ["osel", "w0hl"], [bn])
                tt("dve", B[1][:], PC[:], T[2][:], ALU.mult, ["PC0", "PC1", "T2"], ["B1"])

            obT = v3(B[1][:], 128)
            ch.append(lambda: vg_unit(0))
            ch.append(lambda: (vg_unit(1), vg_post()))
            ch.append(lambda: fm_unit(U_U, 0, hTc, hTn))
            ch.append(lambda: (fm_unit(U_U, 1, hTc, hTn), u_evac()))
            ch.append(lambda: fm_unit(U_GA, 0, hTc, hTn))
            ch.append(lambda: (fm_unit(U_GA, 1, hTc, hTn), act(T[3][:], PC[:], AF.Sigmoid, ["PC0", "PC1"], ["T3"])))
            ch.append(lambda: spatial())
            ch.append(lambda: fm_unit(U_GA + 2, 0, hTc, hTn))
            ch.append(lambda: (fm_unit(U_GA + 2, 1, hTc, hTn), act(T[4][:], PC[:], AF.Sigmoid, ["PC0", "PC1"], ["T4"])))
            ch.append(lambda: fm_unit(U_PA + 2, 0, obT, "B1"))
            ch.append(lambda: (fm_unit(U_PA + 2, 1, obT, "B1"),
                               tt("dve", T[6][:], PC[:], T[4][:], ALU.mult, ["PC0", "PC1", "T4"], ["T6"])))
            return ch

        def make_ffn_chunks(xb, xbn, ydst):
            ch = []
            gbank, gbn = BANKS[4]
            ubank, ubn = BANKS[5]
            dbank0, dbn0 = BANKS[3]
            for q in range(6):
                nfc = min(4, NFC - 4 * q)
                nun = (nfc + 1) // 2
                for jj in range(nun):
                    def c(q=q, jj=jj, nfc=nfc, lastu=(jj == nun - 1)):
                        rf, rfn = load_unit(U_FFN + 2 * q + jj)
                        wf_ = rf[:].rearrange("p (m k c) -> p m k c", m=2, k=8)
                        for cc in range(2):
                            fc = jj * 2 + cc
                            for kc in range(8):
                                mm(gbank[:, fc * 128:(fc + 1) * 128], wf_[:, 0, kc, cc * 128:(cc + 1) * 128], fTb[:, kc, :],
                                   kc == 0, kc == 7, [rfn, "fTb"], [gbn])
                            for kc in range(8):
                                mm(ubank[:, fc * 128:(fc + 1) * 128], wf_[:, 1, kc, cc * 128:(cc + 1) * 128], fTb[:, kc, :],
                                   kc == 0, kc == 7, [rfn, "fTb"], [ubn])
                        if lastu:
                            tq = T[2 + (q % 2)]
                            tqn = "T%d" % (2 + (q % 2))
                            act(tq[:, 0:nfc * 128], gbank[:, 0:nfc * 128], AF.Silu, [gbn], [tqn])
                            tt("dve", actT[:, 4 * q:4 * q + nfc, :].rearrange("p a t -> p (a t)"), tq[:, 0:nfc * 128],
                               ubank[:, 0:nfc * 128], ALU.mult, [tqn, ubn], ["actT"])
                    c.kind = "ffn"
                    ch.append(c)
            for q in range(6):
                def c(q=q):
                    rd, rdn = load_unit(U_DOWN + q, min(4, NFC - 4 * q))
                    wd_ = v3(rd[:], 1024)
                    for rc in range(4):
                        fc = 4 * q + rc
                        if fc >= NFC:
                            continue
                        mm(dbank0, actT[:, fc, :], wd_[:, rc, 0:512], fc == 0, fc == NFC - 1, ["actT", rdn], [dbn0])
                        mm(PT32, actT[:, fc, :], wd_[:, rc, 512:1024], fc == 0, fc == NFC - 1, ["actT", rdn], ["PT"])
                    if q == 5:
                        cp("dve", T[0][:, 0:512], dbank0, [dbn0], ["T0"])
                        cp("dve", T[0][:, 512:1024], PT32, ["PT"], ["T0"])
                        rstd = rms_rstd(T[0][:], ["T0"], T[4][:], "T4", RMS_EPS)
                        stt(T[0][:], T[0][:], rstd, bc[GPF][:], ALU.mult, ALU.mult, ["T0", "sm1", "bc8"], ["T0"])
                        tt("dve", T[0][:], T[0][:], xb[:], ALU.add, ["T0", xbn], ["T0"])
                        ld(ydst, T[0][:], ["T0"], [], eng="pool")
                c.kind = "ffn"
                ch.append(c)
            return ch

        def st_A(cx):
            tl = cx["tl"]
            ti = cx["ti"]
            v = tl["v"]
            segs = tl["segs"]
            nseg = len(segs)
            xb = xt[ti % 2]
            xbn = "xt%d" % (ti % 2)
            hTc = hT[ti % 2]
            hTn = "hT%d" % (ti % 2)
            hTprev = hT[(ti + 1) % 2]
            hTprevn = "hT%d" % ((ti + 1) % 2)
            R_, K_, V_ = Rk[0], Rk[1], Rk[2]
            oaT = v3(B[1][:], 128)
            mT = v3(B[3][:], 128)
            ld(xb[:], tl["x"], (), [xbn])
            if tl["samp"] is not None:
                for sl in range(2):
                    seq = 2 * tl["samp"] + sl
                    ld(T[0][0:64, :].rearrange("v (h k) -> v h k", k=64), swkv[seq].rearrange("h v k -> v h k"), (), ["T0"])
                    for c in range(8):
                        tr(PD[:, c * 64:(c + 1) * 64], T[0][0:64, c * 128:(c + 1) * 128], idf[0:64, 0:64], ["T0", "csb"], ["PD"])
                    cp("dve", Hf[sl][:], v3(PD[:], 64), ["PD"], ["Hf%d" % sl])
                    cp("act", Hb[sl][:], Hf[sl][:], ["Hf%d" % sl], ["Hb%d" % sl])
            elif tl["first"]:
                mset("dve", Hf[0][:], 0.0, ["Hf0"])
                mset("dve", Hb[0][:], 0.0, ["Hb0"])
            rstd = rms_rstd(xb[:], [xbn], hb[:], "hb", RMS_EPS)
            ts("dve", hb[:], xb[:], rstd, None, ALU.mult, None, [xbn, "sm1"], ["hb"])

        def st_A2(cx):
            tl = cx["tl"]
            ti = cx["ti"]
            v = tl["v"]
            segs = tl["segs"]
            nseg = len(segs)
            xb = xt[ti % 2]
            xbn = "xt%d" % (ti % 2)
            hTc = hT[ti % 2]
            hTn = "hT%d" % (ti % 2)
            hTprev = hT[(ti + 1) % 2]
            hTprevn = "hT%d" % ((ti + 1) % 2)
            for kc in range(8):
                tr(PT[:, kc * 128:(kc + 1) * 128], hb[:, kc * 128:(kc + 1) * 128], idb[:], ["hb", "idb"], ["PT"])
            ptv = PT[:].rearrange("p (k t) -> p k t", t=128)
            cp("dve", hTc[:].rearrange("p k t -> p (k t)"), PT[:], ["PT"], [hTn])
            cp("dve", hTp[:, :, 1:128], ptv[:, :, 0:127], ["PT"], ["hTp"])
            if tl["first"]:
                mset("dve", hTp[:, :, 0:1], 0.0, ["hTp"])
                if v == 1:
                    mset("dve", hTp[:, :, 64:65], 0.0, ["hTp"])
            else:
                cp("dve", hTp[:, :, 0:1], hTprev[:, :, 127:128], [hTprevn], ["hTp"])

        def st_B1(cx, blocks):
            tl = cx["tl"]
            ti = cx["ti"]
            v = tl["v"]
            segs = tl["segs"]
            nseg = len(segs)
            xb = xt[ti % 2]
            xbn = "xt%d" % (ti % 2)
            hTc = hT[ti % 2]
            hTn = "hT%d" % (ti % 2)
            hTprev = hT[(ti + 1) % 2]
            hTprevn = "hT%d" % ((ti + 1) % 2)
            R_, K_, V_ = Rk[0], Rk[1], Rk[2]
            oaT = v3(B[1][:], 128)
            mT = v3(B[3][:], 128)
            for b in blocks:
                r1, r1n = load_unit(U_RKV + 2 * b)
                r2, r2n = load_unit(U_RKV + 2 * b + 1)
                w1 = v3(r1[:], 512)
                w2 = v3(r2[:], 512)
                bank, bn = next_bank()
                if v == 1:
                    ld(ssb[:], ssmu_d[tl["samp"], :, b * 512:(b + 1) * 512], ["ssmu_d"], ["ssb"])
                for kc in range(8):
                    mm(bank, hTc[:, kc, :], w1[:, kc, :], kc == 0, False, [hTn, r1n], [bn])
                for kc in range(8):
                    mm(bank, hTp[:, kc, :], w2[:, kc, :], False, (kc == 7 and v == 0), ["hTp", r2n], [bn])
                if v == 1:
                    mm(bank, sel[:], ssb[:], False, True, ["sel", "ssb"], [bn])
                dstt = Rk[b // 2]
                cp("act", dstt[:, (b % 2) * 512:(b % 2 + 1) * 512], bank, [bn], ["rkv%d" % (b // 2)])
                if tl["last"]:
                    lh = hTc[:, :, 63::64]
                    for kc in range(8):
                        mm(PD[0:2, :], lh[:, kc, :], w1[:, kc, :], kc == 0, False, [hTn, r1n], ["PD"])
                    for kc in range(8):
                        mm(PD[0:2, :], lh[:, kc, :], w2[:, kc, :], False, kc == 7, [hTn, r2n], ["PD"])
                    tq = T[b // 2]
                    cp("dve", tq[0:2, (b % 2) * 512:(b % 2 + 1) * 512], PD[0:2, :], ["PD"], ["T%d" % (b // 2)])
                    if v == 0:
                        ld(shp[0:1, b * 512:(b + 1) * 512], tq[1:2, (b % 2) * 512:(b % 2 + 1) * 512], ["T%d" % (b // 2)], [], eng="pool")
                    else:
                        j = tl["samp"]
                        ld(shs[2 * j:2 * j + 2, b * 512:(b + 1) * 512], tq[0:2, (b % 2) * 512:(b % 2 + 1) * 512], ["T%d" % (b // 2)], [], eng="pool")

        def st_B2(cx):
            tl = cx["tl"]
            ti = cx["ti"]
            v = tl["v"]
            segs = tl["segs"]
            nseg = len(segs)
            xb = xt[ti % 2]
            xbn = "xt%d" % (ti % 2)
            hTc = hT[ti % 2]
            hTn = "hT%d" % (ti % 2)
            hTprev = hT[(ti + 1) % 2]
            hTprevn = "hT%d" % ((ti + 1) % 2)
            R_, K_, V_ = Rk[0], Rk[1], Rk[2]
            oaT = v3(B[1][:], 128)
            mT = v3(B[3][:], 128)
            rl, rln = load_unit(U_LORA)
            wl = rl[:].rearrange("p (a k c) -> p a k c", a=2, k=8)
            if v == 1:
                ld(ssb[:, 0:256], ssmu_d[tl["samp"], :, 3072:3328], ["ssmu_d"], ["ssb"])
            for cc in range(2):
                o = PD[:, cc * 128:(cc + 1) * 128]
                for kc in range(8):
                    mm(o, wl[:, 0, kc, cc * 128:(cc + 1) * 128], hTc[:, kc, :], kc == 0, False, [rln, hTn], ["PD"])
                for kc in range(8):
                    mm(o, wl[:, 1, kc, cc * 128:(cc + 1) * 128], hTp[:, kc, :], False, (kc == 7 and v == 0), [rln, "hTp"], ["PD"])
                if v == 1:
                    mm(o, ssb[:, cc * 128:(cc + 1) * 128], sel[:], False, True, ["ssb", "sel"], ["PD"])
            act(lin[0:64, 0, :], PD[0:64, 0:128], AF.Tanh, ["PD"], ["lin"])
            cp("act", lin[64:128, 0, :], PD[64:128, 0:128], ["PD"], ["lin"])
            act(lin[:, 1, :], PD[:, 128:256], AF.Sigmoid, ["PD"], ["lin"])
            if tl["last"]:
                lh = hTc[:, :, 63::64]
                o = PD[0:2, 256:512]
                for kc in range(8):
                    mm(o, lh[:, kc, :], wl[:, 0, kc, :], kc == 0, False, [hTn, rln], ["PD"])
                for kc in range(8):
                    mm(o, lh[:, kc, :], wl[:, 1, kc, :], False, kc == 7, [hTn, rln], ["PD"])
                cp("dve", T[3][0:2, 0:256], o, ["PD"], ["T3"])
                if v == 0:
                    ld(shp[0:1, 3072:3328], T[3][1:2, 0:256], ["T3"], [], eng="pool")
                else:
                    j = tl["samp"]
                    ld(shs[2 * j:2 * j + 2, 3072:3328], T[3][0:2, 0:256], ["T3"], [], eng="pool")

        def st_CD(cx):
            tl = cx["tl"]
            ti = cx["ti"]
            v = tl["v"]
            segs = tl["segs"]
            nseg = len(segs)
            xb = xt[ti % 2]
            xbn = "xt%d" % (ti % 2)
            hTc = hT[ti % 2]
            hTn = "hT%d" % (ti % 2)
            hTprev = hT[(ti + 1) % 2]
            hTprevn = "hT%d" % ((ti + 1) % 2)
            R_, K_, V_ = Rk[0], Rk[1], Rk[2]
            oaT = v3(B[1][:], 128)
            mT = v3(B[3][:], 128)
            R_, K_, V_ = Rk[0], Rk[1], Rk[2]
            tt("dve", T[5][:], K_[:], bc[KK][:], ALU.mult, ["rkv1", "bc0"], ["T5"])
            act(T[7][:], T[5][:], AF.Square, ["T5"], ["T7"])
            red(sm[:, 16:32], v3(T[7][:], 64), ["T7"], ["sm16"])
            act(sm[:, 16:32], sm[:, 16:32], AF.Sqrt, ["sm16"], ["sm16"])
            ts("dve", sm[:, 16:32], sm[:, 16:32], 1e-12, None, ALU.max, None, ["sm16"], ["sm16"])
            recip(sm[:, 16:32], sm[:, 16:32], ["sm16"], ["sm16"])
            tt("dve", v3(T[5][:], 64), v3(T[5][:], 64), sm[:, 16:32].unsqueeze(2).to_broadcast([128, 16, 64]), ALU.mult,
               ["T5", "sm16"], ["T5"])
            for blk in range(2):
                cs = slice(blk * 512, (blk + 1) * 512)
                mm(PA[:, cs], lin[:, 0, :], lw[0][:, cs], True, False, ["lin", "lw0"], ["PA%d" % blk])
                mm(PA[:, cs], ones2[:], w0hl[:, 0, cs], False, True, ["ones2", "w0hl"], ["PA%d" % blk])
                mm(PB[:, cs], lin[:, 0, :], lw[1][:, cs], True, False, ["lin", "lw1"], ["PB%d" % blk])
                mm(PB[:, cs], ones2[:], w0hl[:, 1, cs], False, True, ["ones2", "w0hl"], ["PB%d" % blk])
            act(T[0][:], PA[:], AF.Sigmoid, ["PA0", "PA1"], ["T0"])
            act(T[4][:], PB[:], AF.Sigmoid, ["PB0", "PB1"], ["T4"])
            for blk in range(2):
                cs = slice(blk * 512, (blk + 1) * 512)
                mm(PA[:, cs], TRI[v], T[0][:, cs], True, True, ["csb", "T0"], ["PA%d" % blk])
            for c in range(8):
                mm(PD[:, 2 * c:2 * c + 2], T[0][:, c * 128:(c + 1) * 128], SEGI[v], True, True, ["T0", "csb"], ["PD"])
            act(T[1][:], PA[:], AF.Exp, ["PA0", "PA1"], ["T1"], scale=-C0)
            act(T[2][:], PA[:], AF.Exp, ["PA0", "PA1"], ["T2"], scale=C0)
            for blk in range(2):
                cs = slice(blk * 512, (blk + 1) * 512)
                mm(PB[:, cs], MT2[v][:, 0:128], T[0][:, cs], True, True, ["csb", "T0"], ["PB%d" % blk])
            act(T[3][:], PB[:], AF.Exp, ["PB0", "PB1"], ["T3"], scale=-C0)
            act(gC[:].rearrange("p c s -> p (c s)"), PD[:, 0:16], AF.Exp, ["PD"], ["gC"], scale=-C0)
            stt(T[6][:], T[4][:], -1.0, bc[KA][:], ALU.add, ALU.mult, ["T4", "bc1"], ["T6"])
            stt(T[6][:], T[6][:], 1.0, K_[:], ALU.add, ALU.mult, ["T6", "rkv1"], ["T6"])
            tt("dve", B[0][:], R_[:], T[1][:], ALU.mult, ["rkv0", "T1"], ["B0"])
            stt(B[1][:], T[5][:], -1.0, T[3][:], ALU.mult, ALU.mult, ["T5", "T3"], ["B1"])
            tt("pool", T[7][:], T[5][:], T[4][:], ALU.mult, ["T5", "T4"], ["T7"])
            tt("dve", B[2][:], T[7][:], T[2][:], ALU.mult, ["T7", "T2"], ["B2"])
            tt("pool", B[3][:], T[6][:], T[2][:], ALU.mult, ["T6", "T2"], ["B3"])
            cp("act", B[4][:], V_[:], ["rkv2"], ["B4"])
            transpose_tm_to_fm(B[1], "B1", arT[:, :, 0, :], "arT", "act")
            transpose_tm_to_fm(B[0], "B0", arT[:, :, 1, :], "arT", "dve")
            bT = v3(B[5][:], 128)
            kT = v3(B[6][:], 128)
            transpose_tm_to_fm(B[2], "B2", bT, "B5", "act")
            transpose_tm_to_fm(B[3], "B3", kT, "B6", "dve")

            Vb = B[4]
            bank_pool[0] = 3
            pending.extend(make_side_chunks(tl, v, hTc, hTn))
            tt("pool", T[7][:], R_[:], T[6][:], ALU.mult, ["rkv0", "T6"], ["T7"])
            tt("pool", T[7][:], T[7][:], bc[RK][:], ALU.mult, ["T7", "bc2"], ["T7"])
            red(sm[:, 32:48], v3(T[7][:], 64), ["T7"], ["sm32"])
            for h in range(16):
                ts("pool", T[5][:, h * 64:(h + 1) * 64], V_[:, h * 64:(h + 1) * 64], sm[:, 32 + h:33 + h], None, ALU.mult, None,
                   ["rkv2", "sm32"], ["T5"])
            tt("pool", T[5][:], T[5][:], bc[GNB][:], ALU.add, ["T5", "bc4"], ["T5"])
            for hg in range(2):
                if hg == 1 and cx.get("mid_hook") is not None:
                    while pending and getattr(pending[0], "kind", None) == "ffn":
                        pending.pop(0)()
                    cx["mid_hook"]()
                heads = list(range(hg * 8, hg * 8 + 8))
                mask2 = MT2[v].unsqueeze(1).to_broadcast([128, 2, 256])
                for lhsbuf, lhn, dst, dstn in ((bT, "B5", M2a, "M2a"), (kT, "B6", M2b, "M2b")):
                    for (ja, jb) in ((0, 2), (4, 6), (1, 3), (5, 7)):
                        bank, bn = next_bank()
                        for jj, j in enumerate((ja, jb)):
                            h = heads[j]
                            c, pb = h // 2, 64 * (h % 2)
                            mm(bank[:, jj * 256:(jj + 1) * 256], lhsbuf[pb:pb + 64, c, :],
                               arT[pb:pb + 64, c, :, :].rearrange("p a t -> p (a t)"), True, True, [lhn, "arT"], [bn])
                        tt("dve", dst[:, ja:jb + 1:2, :], v3(bank, 256), mask2, ALU.mult, [bn, "csb"], [dstn])
                        pump(force=(ja in (4, 5)))
                maskn = MSN[v].unsqueeze(1).to_broadcast([128, 4, 128])
                for par in range(2):
                    bank, bn = next_bank()
                    for jj in range(4):
                        j = 2 * jj + par
                        h = heads[j]
                        c, pb = h // 2, 64 * (h % 2)
                        mm(bank[:, jj * 128:(jj + 1) * 128], arT[pb:pb + 64, c, 0, :], bT[pb:pb + 64, c, :], True, True,
                           ["arT", "B5"], [bn])
                    tt("dve", Xb[0][:, par::2, :], v3(bank, 128), maskn, ALU.mult, [bn, "csb"], ["X0"])
                    pump(force=(par == 1))
                tt("dve", PTb[0][:], M2a[:, :, 0:128], idb[:].unsqueeze(1).to_broadcast([128, 8, 128]), ALU.add,
                   ["M2a", "idb"], ["PT0"])
                for k in range(6):
                    cur, nxt = k % 2, (k + 1) % 2
                    Xc, Xcn = Xb[cur], "X%d" % cur
                    if k == 0:
                        XTc, XTcn = M2a[:, :, 0:128], "M2a"
                    else:
                        XTc, XTcn = XTb[cur][:], "XT%d" % cur
                    for jq in range(2):
                        bank, bn = next_bank()
                        for jj in range(4):
                            j = 4 * jq + jj
                            mm(bank[:, jj * 128:(jj + 1) * 128], XTc[:, j, :], Xc[:, j, :], True, True, [XTcn, Xcn], [bn])
                        cp("act", Xb[nxt][:, 4 * jq:4 * jq + 4, :], v3(bank, 128), [bn], ["X%d" % nxt])
                        pump(force=(jq == 1))
                    if k < 5:
                        for jq in range(2):
                            bank, bn = next_bank()
                            for jj in range(4):
                                j = 4 * jq + jj
                                mm(bank[:, jj * 128:(jj + 1) * 128], Xc[:, j, :], XTc[:, j, :], True, True, [Xcn, XTcn], [bn])
                            cp("act" if jq == 0 else "dve", XTb[nxt][:, 4 * jq:4 * jq + 4, :], v3(bank, 128), [bn], ["XT%d" % nxt])
                            pump()
                    for jq in range(2):
                        bank, bn = next_bank()
                        for jj in range(4):
                            j = 4 * jq + jj
                            mm(bank[:, jj * 128:(jj + 1) * 128], Xb[nxt][:, j, :], PTb[cur][:, j, :], True, True,
                               ["X%d" % nxt, "PT%d" % cur], [bn])
                        tt("dve", PTb[nxt][:, 4 * jq:4 * jq + 4, :], v3(bank, 128), PTb[cur][:, 4 * jq:4 * jq + 4, :], ALU.add,
                           [bn, "PT%d" % cur], ["PT%d" % nxt])
                        pump(force=(jq == 1))
                PTf, PTfn = PTb[0], "PT0"
                for par in range(2):
                    bank, bn = next_bank()
                    for jj in range(4):
                        j = 2 * jj + par
                        h = heads[j]
                        c, pb = h // 2, 64 * (h % 2)
                        o = bank[:, jj * 64:(jj + 1) * 64]
                        for (r0, nr, slot) in segs:
                            mm(bank[r0:r0 + nr, jj * 64:(jj + 1) * 64], arT[pb:pb + 64, c, 0, r0:r0 + nr], Hb[slot][pb:pb + 64, c, :],
                               True, False, ["arT", "Hb%d" % slot], [bn])
                        mm(o, M2b[:, j, 0:128], Vb[:, h * 64:(h + 1) * 64], False, True, ["M2b", "B4"], [bn])
                    cp("dve", W1b[:, par::2, :], v3(bank[:, 0:256], 64), [bn], ["W1b"])
                bank, bn = next_bank()
                for j in range(8):
                    mm(bank[:, j * 64:(j + 1) * 64], PTf[:, j, :], W1b[:, j, :], True, True, [PTfn, "W1b"], [bn])
                cp("act", Ub[:], v3(bank, 64), [bn], ["Ub"])
                for par in range(2):
                    bank, bn = next_bank()
                    for jj in range(4):
                        j = 2 * jj + par
                        h = heads[j]
                        c, pb = h // 2, 64 * (h % 2)
                        o = bank[:, jj * 64:(jj + 1) * 64]
                        for (r0, nr, slot) in segs:
                            mm(bank[r0:r0 + nr, jj * 64:(jj + 1) * 64], arT[pb:pb + 64, c, 1, r0:r0 + nr], Hb[slot][pb:pb + 64, c, :],
                               True, False, ["arT", "Hb%d" % slot], [bn])
                        mm(o, M2a[:, j, 128:256], Ub[:, j, :], False, False, ["M2a", "Ub"], [bn])
                        mm(o, M2b[:, j, 128:256], Vb[:, h * 64:(h + 1) * 64], False, True, ["M2b", "B4"], [bn])
                    cp("dve", v3(T[1][:], 64)[:, hg * 8 + par:hg * 8 + 8:2, :], v3(bank[:, 0:256], 64), [bn], ["T1"])
                hbanks = []
                for si, (r0, nr, slot) in enumerate(segs):
                    hbk, hbn = (PD[:], "PD") if si == 0 else next_bank()
                    if si > 0 and hbn == "PD":
                        hbk, hbn = next_bank()
                    hbanks.append((hbk, hbn))
                    for j in range(8):
                        h = heads[j]
                        c, pb = h // 2, 64 * (h % 2)
                        cl = c - hg * 4
                        o = hbk[pb:pb + 64, cl * 64:(cl + 1) * 64]
                        mm(o, B[2][r0:r0 + nr, h * 64:(h + 1) * 64], Ub[r0:r0 + nr, j, :], True, False, ["B2", "Ub"], [hbn])
                        mm(o, B[3][r0:r0 + nr, h * 64:(h + 1) * 64], Vb[r0:r0 + nr, h * 64:(h + 1) * 64], False, True,
                           ["B3", "B4"], [hbn])
                for si, (r0, nr, slot) in enumerate(segs):
                    hsl = Hf[slot][:, hg * 4:hg * 4 + 4, :]
                    hn = "Hf%d" % slot
                    hbk, hbn = hbanks[si]
                    tt("dve", hsl, v3(hbk[:, 0:256], 64), hsl, ALU.add, [hbn, hn], [hn])
                    tt("dve", hsl, hsl, gC[:, hg * 4:hg * 4 + 4, si:si + 1].to_broadcast([128, 4, 64]), ALU.mult,
                       [hn, "gC"], [hn])
                    cp("act", Hb[slot][:, hg * 4:hg * 4 + 4, :], hsl, [hn], ["Hb%d" % slot])

            drain()
            bank_pool[0] = 6
            if tl["last"]:
                if v == 0:
                    emit_state_out(0, wkvp)
                else:
                    for sl in range(2):
                        emit_state_out(sl, wkvs[2 * tl["samp"] + sl])

        def st_E(cx):
            tl = cx["tl"]
            ti = cx["ti"]
            v = tl["v"]
            segs = tl["segs"]
            nseg = len(segs)
            xb = xt[ti % 2]
            xbn = "xt%d" % (ti % 2)
            hTc = hT[ti % 2]
            hTn = "hT%d" % (ti % 2)
            hTprev = hT[(ti + 1) % 2]
            hTprevn = "hT%d" % ((ti + 1) % 2)
            R_, K_, V_ = Rk[0], Rk[1], Rk[2]
            oaT = v3(B[1][:], 128)
            mT = v3(B[3][:], 128)
            Y = T[1]
            for blk in range(2):
                cs = slice(blk * 512, (blk + 1) * 512)
                mm(PC[:, cs], lin[:, 1, :], lw[2][:, cs], True, True, ["lin", "lw2"], ["PC%d" % blk])
            red(sm[:, 48:64], v3(Y[:], 64), ["T1"], ["sm48"])
            act(T[7][:], Y[:], AF.Square, ["T1"], ["T7"])
            red(sm[:, 64:80], v3(T[7][:], 64), ["T7"], ["sm64"])
            ts("dve", sm[:, 48:64], sm[:, 48:64], 1.0 / 64, None, ALU.mult, None, ["sm48"], ["sm48"])
            tt("dve", sm[:, 80:96], sm[:, 48:64], sm[:, 48:64], ALU.mult, ["sm48"], ["sm80"])
            stt(sm[:, 64:80], sm[:, 64:80], 1.0 / 64, sm[:, 80:96], ALU.mult, ALU.subtract, ["sm64", "sm80"], ["sm64"])
            act(sm[:, 64:80], sm[:, 64:80], AF.Sqrt, ["sm64"], ["sm64"], bias=GN_EPS)
            recip(sm[:, 64:80], sm[:, 64:80], ["sm64"], ["sm64"])
            bch = lambda a: a.unsqueeze(2).to_broadcast([128, 16, 64])
            tt("dve", v3(Y[:], 64), v3(Y[:], 64), bch(sm[:, 48:64]), ALU.subtract, ["T1", "sm48"], ["T1"])
            tt("dve", v3(Y[:], 64), v3(Y[:], 64), bch(sm[:, 64:80]), ALU.mult, ["T1", "sm64"], ["T1"])
            tt("dve", Y[:], Y[:], bc[GNG][:], ALU.mult, ["T1", "bc3"], ["T1"])
            tt("dve", Y[:], Y[:], T[5][:], ALU.add, ["T1", "T5"], ["T1"])
            tt("dve", B[0][:], Y[:], PC[:], ALU.mult, ["T1", "PC0", "PC1"], ["B0"])
            oaT = v3(B[1][:], 128)
            transpose_tm_to_fm(B[0], "B0", oaT, "B1", "act")

        def st_G(cx):
            tl = cx["tl"]
            ti = cx["ti"]
            v = tl["v"]
            segs = tl["segs"]
            nseg = len(segs)
            xb = xt[ti % 2]
            xbn = "xt%d" % (ti % 2)
            hTc = hT[ti % 2]
            hTn = "hT%d" % (ti % 2)
            hTprev = hT[(ti + 1) % 2]
            hTprevn = "hT%d" % ((ti + 1) % 2)
            R_, K_, V_ = Rk[0], Rk[1], Rk[2]
            oaT = v3(B[1][:], 128)
            mT = v3(B[3][:], 128)
            for b in range(2):
                rp, rpn = load_unit(U_PA + b)
                wp_ = v3(rp[:], 512)
                for cc in range(4):
                    for kc in range(8):
                        mm(PC[:, b * 512 + cc * 128:b * 512 + (cc + 1) * 128], wp_[:, kc, cc * 128:(cc + 1) * 128], oaT[:, kc, :],
                           kc == 0, kc == 7, [rpn, "B1"], ["PC%d" % b])
            tt("dve", T[5][:], PC[:], T[3][:], ALU.mult, ["PC0", "PC1", "T3"], ["T5"])
            mT = v3(B[3][:], 128)
            tt("pool", B[3][:], T[5][:], T[6][:], ALU.add, ["T5", "T6"], ["B3"])

        def st_H(cx):
            tl = cx["tl"]
            ti = cx["ti"]
            v = tl["v"]
            segs = tl["segs"]
            nseg = len(segs)
            xb = xt[ti % 2]
            xbn = "xt%d" % (ti % 2)
            hTc = hT[ti % 2]
            hTn = "hT%d" % (ti % 2)
            hTprev = hT[(ti + 1) % 2]
            hTprevn = "hT%d" % ((ti + 1) % 2)
            R_, K_, V_ = Rk[0], Rk[1], Rk[2]
            oaT = v3(B[1][:], 128)
            mT = v3(B[3][:], 128)
            for b in range(2):
                ro, ron = load_unit(U_PA + 4 + b)
                wo_ = v3(ro[:], 512)
                for kc in range(8):
                    mm(PB[:, b * 512:(b + 1) * 512], mT[:, kc, :], wo_[:, kc, :], kc == 0, kc == 7, ["B3", ron], ["PB%d" % b])
            cp("dve", T[0][:], PB[:], ["PB0", "PB1"], ["T0"])
            rstd = rms_rstd(T[0][:], ["T0"], T[2][:], "T2", RMS_EPS)
            stt(T[0][:], T[0][:], rstd, bc[GPM][:], ALU.mult, ALU.mult, ["T0", "sm1", "bc7"], ["T0"])
            tt("dve", xb[:], xb[:], T[0][:], ALU.add, [xbn, "T0"], [xbn])

        def st_I(cx):
            tl = cx["tl"]
            ti = cx["ti"]
            v = tl["v"]
            segs = tl["segs"]
            nseg = len(segs)
            xb = xt[ti % 2]
            xbn = "xt%d" % (ti % 2)
            hTc = hT[ti % 2]
            hTn = "hT%d" % (ti % 2)
            hTprev = hT[(ti + 1) % 2]
            hTprevn = "hT%d" % ((ti + 1) % 2)
            R_, K_, V_ = Rk[0], Rk[1], Rk[2]
            oaT = v3(B[1][:], 128)
            mT = v3(B[3][:], 128)
            rstd = rms_rstd(xb[:], [xbn], T[0][:], "T0", RMS_EPS)
            ts("dve", hb[:], xb[:], rstd, None, ALU.mult, None, [xbn, "sm1"], ["hb"])
            transpose_tm_to_fm(hb, "hb", fTb[:], "fTb", "act")
            pending.extend(make_ffn_chunks(xb, xbn, tl["y"]))


        cxs = [dict(tl=tl, ti=ti) for ti, tl in enumerate(tiles)]
        st_A(cxs[0])
        st_A2(cxs[0])
        st_B1(cxs[0], range(6))
        st_B2(cxs[0])
        for ti, cx in enumerate(cxs):
            nx = cxs[ti + 1] if ti + 1 < len(cxs) else None
            piped = PIPELINE and nx is not None and not nx["tl"]["last"]
            if piped:
                cx["mid_hook"] = (lambda nx=nx: st_A(nx))
            st_CD(cx)
            if piped:
                st_A2(nx)
                bank_pool[0] = 2
                st_B1(nx, [0, 1, 2, 3])
                st_E(cx)
                st_G(cx)
                st_B1(nx, [4])
                st_H(cx)
                st_B1(nx, [5])
                st_I(cx)
                bank_pool[0] = 6
                st_B2(nx)
            else:
                st_E(cx)
                st_G(cx)
                st_H(cx)
                st_I(cx)
                if nx is not None:
                    st_A(nx)
                    st_A2(nx)
                    st_B1(nx, range(6))
                    st_B2(nx)

        drain()
        S.finish("sp")
        with nc.Block() as block:
            S.replay(block)
    return nc


_CACHE = {}


def _consts():
    s = np.arange(128)[:, None]
    t = np.arange(128)[None, :]
    same = (s // 64) == (t // 64)
    ms_p = (s < t).astype(np.float32)
    mi_p = (s <= t).astype(np.float32)
    ms_s = ((s < t) & same).astype(np.float32)
    mi_s = ((s <= t) & same).astype(np.float32)
    msn_p = ms_p.T.copy()
    msn_s = ms_s.T.copy()
    seg_p = np.stack([np.ones(128, np.float32), np.zeros(128, np.float32)], axis=1)
    seg_s = np.stack([(np.arange(128) < 64).astype(np.float32), (np.arange(128) >= 64).astype(np.float32)], axis=1)
    cst = np.concatenate([np.eye(128, dtype=np.float32), ms_p, mi_p, ms_s, mi_s, msn_p, msn_s, seg_p, seg_s], axis=1)
    sel = np.zeros((2, 384), np.float32)
    sel[0, 0] = 1.0
    sel[1, 64] = 1.0
    sel[0, 128:256] = 1.0
    sel[1, 256:384] = 1.0
    return np.ascontiguousarray(cst), sel


def make_in_maps(inp, n_ptiles=N_PTILES, n_stiles=N_STILES, ncores=NCORE):
    f = lambda a: np.ascontiguousarray(np.asarray(a, dtype=np.float32))
    bcv = lambda a: np.broadcast_to(f(a).reshape(1, -1), (128, f(a).size))
    cst, sel = _consts()
    L = 0
    bcp = np.ascontiguousarray(np.stack([
        bcv(inp["k_k"][L]), bcv(inp["k_a"][L]), bcv(inp["r_k"][L]), bcv(inp["gn_g"][L]), bcv(inp["gn_b"][L]),
        bcv(inp["ln_v_g"][L]), bcv(inp["ln_v_b"][L]), bcv(inp["g_post_mix"][L]), bcv(inp["g_post_ffn"][L])]))
    mu_bc = np.ascontiguousarray(bcv(inp["mu_shift"][L]))
    gvec = np.ascontiguousarray(np.concatenate([f(inp["g_pre_mix"][L]).reshape(8, 128).T,
                                                f(inp["g_pre_ffn"][L]).reshape(8, 128).T], axis=1))
    lw = np.zeros((3, 128, D), np.float32)
    lw[0, 0:64] = f(inp["w_lora_w"][L])
    lw[1, 64:128] = f(inp["w_lora_a"][L])
    lw[2] = f(inp["w_lora_g"][L])
    wa0 = np.ascontiguousarray(np.stack([f(inp["w0"][L]), f(inp["a0"][L])]))
    wsp = f(inp["w_spatial"][L])
    wsT_p = np.transpose(wsp, (2, 0, 1))
    idx = np.arange(128) % 64
    wsT_s = np.transpose(wsp[:, idx][:, :, idx], (2, 0, 1))
    wsT = np.ascontiguousarray(np.stack([wsT_p, wsT_s]))
    bs = f(inp["b_spatial"][L])
    bspv = np.ascontiguousarray(np.stack([bs, bs[:, idx]]))
    shared = dict(w_in=f(inp["w_in"][L]), w_pa=f(inp["w_proj_a"][L]), w_pb=f(inp["w_proj_b"][L]), w_o=f(inp["w_o"][L]),
                  w_gate=f(inp["w_gate"][L]), w_up=f(inp["w_up"][L]), w_down=f(inp["w_down"][L]),
                  bcp=bcp, mu_bc=mu_bc, gvec=gvec, lora=lw, wa0=wa0, wsT=wsT, bspv=bspv, cst=cst, selc=sel)
    xpr = f(inp["x_prompt"])
    xsm = f(inp["x_sample"])
    ssh = f(inp["state_shift"][L])[:, 0, :]
    swk = f(inp["state_wkv"][L])
    nseq = 2 * n_stiles
    maps = []
    for c in range(ncores):
        m = dict(shared)
        m["xp"] = np.ascontiguousarray(xpr[c, :n_ptiles * 128])
        m["xs"] = np.ascontiguousarray(xsm[c * nseq:(c + 1) * nseq].reshape(nseq * 64, D))
        m["sshift"] = np.ascontiguousarray(ssh[c * nseq:(c + 1) * nseq])
        m["swkv"] = np.ascontiguousarray(swk[c * nseq:(c + 1) * nseq])
        maps.append(m)
    return maps


def assemble(results, n_ptiles=N_PTILES, n_stiles=N_STILES):
    nseq = 2 * n_stiles
    yp = np.stack([r["yp"] for r in results])
    ys = np.concatenate([r["ys"].reshape(nseq, 64, D) for r in results])
    shp = np.stack([r["shp"] for r in results])[None]
    wkvp = np.stack([r["wkvp"] for r in results])[None]
    shs = np.concatenate([r["shs"] for r in results])[None, :, None, :]
    wkvs = np.concatenate([r["wkvs"] for r in results])[None]
    vs = np.concatenate([r["vs"].reshape(nseq, 64, D) for r in results])[None]
    f = lambda a: np.ascontiguousarray(a, dtype=np.float32)
    return (f(yp), f(ys), f(shp), f(wkvp), f(shs), f(wkvs), f(vs))


def kernel(**inputs):
    if "nc" not in _CACHE:
        _CACHE["nc"] = build_program()
    nc = _CACHE["nc"]
    in_maps = make_in_maps(inputs)
    res = run_bass_kernel_spmd(nc, in_maps, core_ids=list(range(NCORE)))
    return assemble(res.results)
```

```python
import contextlib
import numpy as np
import concourse.bass as bass
import concourse.mybir as mybir
from concourse.bass_utils import run_bass_kernel_spmd

F32 = mybir.dt.float32
BF16 = mybir.dt.bfloat16
AF = mybir.ActivationFunctionType
ALU = mybir.AluOpType
AX = mybir.AxisListType

D = 1024
SEQ = 4096
NCORE = 8
SHIFT_W = 3328
IN_W = 7424
DFF = 2816
NFC = 22
RMS_EPS = 1e-6
GN_EPS = 64e-5
LN_EPS = 1e-5
C0 = float(np.exp(-0.5))
NSLOT = 4
NUNIT = 44
N_PTILES = SEQ // 128
N_STILES = 2
PIPELINE = True
PUMP_EVERY = 3
DMA_ENG_MAP = {"pool": "act"}


class Sched:
    ENGS = ("pe", "dve", "act", "pool", "sp")

    def __init__(self, nc, stack, n_dma_sems=28):
        self.nc = nc
        self.ops = {e: [] for e in self.ENGS}
        self.cnt = {e: 0 for e in self.ENGS}
        self.waited = {e: {} for e in self.ENGS}
        self.res = {}
        self.dummy = {}
        self.excl = set()
        self.semobj = {}
        for e in self.ENGS:
            self.semobj["c_" + e] = stack.enter_context(nc.semaphore("c_" + e))
        for i in range(n_dma_sems):
            self.semobj["d%d" % i] = stack.enter_context(nc.semaphore("d%d" % i))
        self.duse = [0] * n_dma_sems
        self.dpool = {"sp": list(range(0, 14)), "act": list(range(14, 22)), "pool": list(range(22, n_dma_sems))}
        self.drr = {"sp": 0, "act": 0, "pool": 0}

    def _deps(self, eng, reads, writes, xreads=()):
        need = {}

        def add(tok, raw):
            if tok is None:
                return
            key, val, teng = tok
            if teng == eng and eng == "pe":
                return
            if need.get(key, 0) < val:
                need[key] = val

        for r in reads:
            st = self.res.get(r)
            if st is not None:
                add(st["w"], True)
        for w in writes:
            st = self.res.get(w)
            if st is not None:
                add(st["w"], False)
                for key, (val, teng) in st["r"].items():
                    add((key, val, teng), False)
        for r in xreads:
            st = self.res.get(r)
            if st is not None:
                add(st["w"], True)
                for key, (val, teng) in st["r"].items():
                    if teng != eng:
                        add((key, val, teng), False)
        out = []
        wd = self.waited[eng]
        for key, val in need.items():
            if wd.get(key, 0) < val:
                wd[key] = val
                out.append((key, val))
        return out

    def _commit(self, tok, reads, writes):
        key, val, teng = tok
        for r in reads:
            st = self.res.setdefault(r, {"w": None, "r": {}})
            old = st["r"].get(key)
            if old is None or old[0] < val:
                st["r"][key] = (val, teng)
        for w in writes:
            self.res[w] = {"w": tok, "r": {}}

    def op(self, eng, fn, reads=(), writes=()):
        xreads = [r for r in reads if r in self.excl]
        reads = [r for r in reads if r not in self.excl]
        waits = self._deps(eng, reads, writes, xreads)
        reads = list(reads) + xreads
        dummy = self.dummy.get(eng)
        if dummy is not None and any(k == "c_pe" for k, _ in waits):
            self.cnt[eng] += 1
            self.ops[eng].append((waits, dummy, "c_" + eng, 1))
            waits = []
        self.cnt[eng] += 1
        tok = ("c_" + eng, self.cnt[eng], eng)
        self.ops[eng].append((waits, fn, "c_" + eng, 1))
        self._commit(tok, reads, writes)

    def dma(self, eng, fn, reads=(), writes=()):
        eng = DMA_ENG_MAP.get(eng, eng)
        waits = self._deps(eng, reads, writes)
        pl = self.dpool[eng]
        j = pl[self.drr[eng] % len(pl)]
        self.drr[eng] += 1
        key = "d%d" % j
        if self.duse[j] > 0:
            prev = 16 * self.duse[j]
            if self.waited[eng].get(key, 0) < prev:
                self.waited[eng][key] = prev
                waits.append((key, prev))
        self.duse[j] += 1
        tok = (key, 16 * self.duse[j], None)
        self.ops[eng].append((waits, fn, key, 16))
        self._commit(tok, reads, writes)

    def finish(self, eng="sp"):
        waits = []
        for j, u in enumerate(self.duse):
            if u > 0:
                waits.append(("d%d" % j, 16 * u))
        for e in self.ENGS:
            if e != eng and self.cnt[e] > 0:
                waits.append(("c_" + e, self.cnt[e]))
        self.ops[eng].append((waits, None, None, 0))

    def replay(self, block):
        def run(e, handle):
            for waits, fn, key, inc in self.ops[e]:
                for k, v in waits:
                    handle.wait_ge(self.semobj[k], v)
                if fn is not None:
                    fn(handle).then_inc(self.semobj[key], inc)

        @block.sync
        def _(sync):
            run("sp", sync)

        @block.tensor
        def _(tensor):
            run("pe", tensor)

        @block.vector
        def _(vector):
            run("dve", vector)

        @block.scalar
        def _(scalar):
            run("act", scalar)

        @block.gpsimd
        def _(gpsimd):
            run("pool", gpsimd)


def build_program(n_ptiles=N_PTILES, n_stiles=N_STILES):
    nc = bass.Bass("TRN2", target_bir_lowering=False)
    npt, nst = n_ptiles, n_stiles
    nseq_s = 2 * nst

    def din(name, shape, dt=F32):
        return nc.dram_tensor(name, list(shape), dt, kind="ExternalInput").ap()

    def dout(name, shape):
        return nc.dram_tensor(name, list(shape), F32, kind="ExternalOutput").ap()

    xp = din("xp", [npt * 128, D])
    xs = din("xs", [nst * 128, D])
    sshift = din("sshift", [nseq_s, SHIFT_W])
    swkv = din("swkv", [nseq_s, 16, 64, 64])
    w_in = din("w_in", [D, IN_W])
    w_pa = din("w_pa", [D, D])
    w_pb = din("w_pb", [D, D])
    w_o = din("w_o", [D, D])
    w_gate = din("w_gate", [D, DFF])
    w_up = din("w_up", [D, DFF])
    w_down = din("w_down", [DFF, D])
    bcp = din("bcp", [9, 128, D])
    mu_bc = din("mu_bc", [128, SHIFT_W])
    gvec = din("gvec", [128, 16])
    lora = din("lora", [3, 128, D])
    wa0 = din("wa0", [2, D])
    wsT = din("wsT", [2, 128, 8, 128])
    bspv = din("bspv", [2, 8, 128])
    cst = din("cst", [128, 128 * 7 + 4])
    selc = din("selc", [2, 384])

    yp = dout("yp", [npt * 128, D])
    ys = dout("ys", [nst * 128, D])
    shp = dout("shp", [1, SHIFT_W])
    wkvp = dout("wkvp", [16, 64, 64])
    shs = dout("shs", [nseq_s, SHIFT_W])
    wkvs = dout("wkvs", [nseq_s, 16, 64, 64])
    vs = dout("vs", [nst * 128, D])

    scr = nc.dram_tensor("scr", [NUNIT, 128, 4096], BF16, kind="Internal").ap()
    ssmu_d = nc.dram_tensor("ssmu_d", [nst, 2, SHIFT_W], BF16, kind="Internal").ap()

    with contextlib.ExitStack() as st:
        S = Sched(nc, st)

        def sb(name, shape, dt=F32):
            return st.enter_context(nc.sbuf_tensor(name, list(shape), dt))

        def ps(name, shape, dt=F32):
            return st.enter_context(nc.psum_tensor(name, list(shape), dt))

        def mm(out, lhsT, rhs, start, stop, r, w):
            S.op("pe", lambda e: e.matmul(out, lhsT=lhsT, rhs=rhs, start=start, stop=stop), r, w)

        def tr(out, in_, ident, r, w):
            S.op("pe", lambda e: e.transpose(out=out, in_=in_, identity=ident), r, w)

        def act(out, in_, func, r, w, scale=None, bias=None, accum=None):
            kw = {}
            if scale is not None:
                kw["scale"] = scale
            if bias is not None:
                kw["bias"] = bias
            if accum is not None:
                kw["accum_out"] = accum
            S.op("act", lambda e: e.activation(out=out, in_=in_, func=func, **kw), r, w)

        def tt(eng, out, in0, in1, op, r, w):
            S.op(eng, lambda e: e.tensor_tensor(out=out, in0=in0, in1=in1, op=op), r, w)

        def ts(eng, out, in0, s1, s2, op0, op1, r, w):
            if op1 is None:
                S.op(eng, lambda e: e.tensor_scalar(out=out, in0=in0, scalar1=s1, scalar2=None, op0=op0), r, w)
            else:
                S.op(eng, lambda e: e.tensor_scalar(out=out, in0=in0, scalar1=s1, scalar2=s2, op0=op0, op1=op1), r, w)

        def stt(out, in0, scalar, in1, op0, op1, r, w):
            S.op("dve", lambda e: e.scalar_tensor_tensor(out=out, in0=in0, scalar=scalar, in1=in1, op0=op0, op1=op1), r, w)

        def cp(eng, out, in_, r, w):
            if eng == "act":
                S.op("act", lambda e: e.activation(out=out, in_=in_, func=AF.Copy), r, w)
            else:
                S.op(eng, lambda e: e.tensor_copy(out=out, in_=in_), r, w)

        def red(out, in_, r, w):
            S.op("dve", lambda e: e.tensor_reduce(out=out, in_=in_, axis=AX.X, op=ALU.add), r, w)

        def recip(out, in_, r, w):
            S.op("dve", lambda e: e.reciprocal(out=out, in_=in_), r, w)

        def mset(eng, ap, val, w):
            S.op(eng, lambda e: e.memset(ap, val), (), w)

        def ld(out, in_, r, w, eng="sp"):
            S.dma(eng, lambda e: e.dma_start(out=out, in_=in_), r, w)

        def ldnc(out, in_, r, w, eng="sp"):
            S.dma(eng, lambda e: e.dma_start(out=out, in_=in_, allow_slow_non_contiguous=True), r, w)

        bc = [sb("bc%d" % i, [128, D]) for i in range(9)]
        KK, KA, RK, GNG, GNB, LNG, LNB, GPM, GPF = range(9)
        lw = [sb("lw%d" % i, [128, D], BF16) for i in range(3)]
        w0hl = sb("w0hl", [2, 3, D], BF16)
        ones2 = sb("ones2", [2, 128], BF16)
        wst = [sb("wst%d" % v, [128, 8, 128], BF16) for v in range(2)]
        osel = sb("osel", [2, 2, 128], BF16)
        csb = sb("csb", [128, 128 * 7 + 4])
        idb = sb("idb", [128, 128], BF16)
        sel = sb("sel", [2, 128], BF16)
        gv = sb("gv", [128, 16])
        idf = csb[:, 0:128]
        MT2 = [csb[:, 128:384], csb[:, 384:640]]
        TRI = [csb[:, 256:384], csb[:, 512:640]]
        MSN = [csb[:, 640:768], csb[:, 768:896]]
        SEGI = [csb[:, 896:898], csb[:, 898:900]]

        xt = [sb("xt%d" % i, [128, D]) for i in range(2)]
        hb = sb("hb", [128, D], BF16)
        hT = [sb("hT%d" % i, [128, 8, 128], BF16) for i in range(2)]
        hTp = sb("hTp", [128, 8, 128], BF16)
        Rk = [sb("rkv%d" % i, [128, D]) for i in range(3)]
        T = [sb("T%d" % i, [128, D]) for i in range(8)]
        fTb = sb("fTb", [128, 8, 128], BF16)
        B = [sb("B%d" % i, [128, D], BF16) for i in range(7)]
        arT = sb("arT", [128, 8, 2, 128], BF16)
        Xb = [sb("X%d" % i, [128, 8, 128], BF16) for i in range(2)]
        XTb = [sb("XT%d" % i, [128, 8, 128], BF16) for i in range(2)]
        PTb = [sb("PT%d" % i, [128, 8, 128], BF16) for i in range(2)]
        M2a = sb("M2a", [128, 8, 256], BF16)
        M2b = sb("M2b", [128, 8, 256], BF16)
        W1b = sb("W1b", [128, 8, 64], BF16)
        Ub = sb("Ub", [128, 8, 64], BF16)
        actT = sb("actT", [128, NFC, 128], BF16)
        Hf = [sb("Hf%d" % i, [128, 8, 64]) for i in range(2)]
        Hb = [sb("Hb%d" % i, [128, 8, 64], BF16) for i in range(2)]
        lin = sb("lin", [128, 2, 128], BF16)
        ssb = sb("ssb", [2, 512], BF16)
        sm = sb("sm", [128, 112])
        gC = sb("gC", [128, 8, 2])
        ring = [sb("ring%d" % i, [128, 4096], BF16) for i in range(NSLOT)]

        dmy = sb("dmy", [128, 8])
        PA = ps("PA", [128, 1024])
        PB = ps("PB", [128, 1024])
        PC = ps("PC", [128, 1024])
        PD = ps("PD", [128, 512])
        PT = ps("PT", [128, 1024], BF16)
        S.excl = {"PA0", "PA1", "PB0", "PB1", "PC0", "PC1", "PD", "PT"}
        BANKS = [(PA[:, 0:512], "PA0"), (PA[:, 512:1024], "PA1"), (PB[:, 0:512], "PB0"),
                 (PB[:, 512:1024], "PB1"), (PC[:, 0:512], "PC0"), (PC[:, 512:1024], "PC1")]

        def v3(ap, d):
            return ap.rearrange("p (a d) -> p a d", d=d)

        mset("dve", dmy[:], 0.0, ["dmy"])
        for i in range(9):
            ld(bc[i][:], bcp[i], (), ["bc%d" % i])
        ld(csb[:], cst, (), ["csb"])
        ld(gv[:], gvec, (), ["gv"])
        cp("dve", idb[:], idf, ["csb"], ["idb"])
        mset("dve", ones2[:], 1.0, ["ones2"])
        ld(T[0][0:2, 0:384], selc, (), ["T0"])
        cp("dve", sel[:], T[0][0:2, 0:128], ["T0"], ["sel"])
        cp("dve", osel[:].rearrange("o a t -> o (a t)"), T[0][0:2, 128:384], ["T0"], ["osel"])
        ld(T[7][0:2, :], bspv.rearrange("v g t -> v (g t)"), (), ["T7"])
        cp("dve", w0hl[:, 2, :], T[7][0:2, :], ["T7"], ["w0hl"])
        for i in range(3):
            ld(T[1 + (i % 2)][:], lora[i], (), ["T%d" % (1 + (i % 2))])
            cp("dve", lw[i][:], T[1 + (i % 2)][:], ["T%d" % (1 + (i % 2))], ["lw%d" % i])
        for j in range(2):
            tj = T[3 + j]
            nm = "T%d" % (3 + j)
            ld(tj[0:1, :], wa0[j:j + 1, :], (), [nm])
            ld(tj[1:2, :], wa0[j:j + 1, :], (), [nm])
            cp("dve", w0hl[:, j, :], tj[0:2, :], [nm], ["w0hl"])
            tt("dve", tj[0:2, :], tj[0:2, :], w0hl[:, j, :], ALU.subtract, [nm, "w0hl"], [nm])
            cp("dve", B[0][0:2, j * D:(j + 1) * D] if False else B[j][0:2, :], tj[0:2, :], [nm], ["B%d" % j])
            ld(w0hl[1:2, j, :], B[j][0:1, :], ["B%d" % j], ["w0hl"])
        for v in range(2):
            tv = T[5 + v]
            nm = "T%d" % (5 + v)
            ld(v3(tv[:], 128), wsT[v], (), [nm])
            tt("dve", wst[v][:], v3(tv[:], 128), TRI[v].unsqueeze(1).to_broadcast([128, 8, 128]), ALU.mult,
               [nm, "csb"], ["wst%d" % v])
        for j in range(nst):
            for q in range(4):
                c0 = q * 1024
                c1 = min(SHIFT_W, c0 + 1024)
                n = c1 - c0
                ld(T[0][0:2, 0:n], sshift[2 * j:2 * j + 2, c0:c1], (), ["T0"])
                ld(T[1][0:2, 0:n], mu_bc[0:2, c0:c1], (), ["T1"])
                tt("dve", B[2][0:2, 0:n], T[0][0:2, 0:n], T[1][0:2, 0:n], ALU.mult, ["T0", "T1"], ["B2"])
                ld(ssmu_d[j, :, c0:c1], B[2][0:2, 0:n], ["B2"], ["ssmu_d"], eng="pool")

        w_in_v = w_in.rearrange("(kc p) c -> p kc c", p=128)
        w_pa_v = w_pa.rearrange("(kc p) c -> p kc c", p=128)
        w_pb_v = w_pb.rearrange("(kc p) c -> p kc c", p=128)
        w_o_v = w_o.rearrange("(kc p) c -> p kc c", p=128)
        w_gate_v = w_gate.rearrange("(kc p) c -> p kc c", p=128)
        w_up_v = w_up.rearrange("(kc p) c -> p kc c", p=128)
        w_down_v = w_down.rearrange("(rc p) c -> p rc c", p=128)

        pp_cnt = [0]
        ew_rr = [0]

        def ew_eng():
            ew_rr[0] += 1
            return ("dve", "act")[ew_rr[0] % 2]

        def pp_piece(u, q, src, nk, ncol, gcol, mu_cols, mu_inv):
            i = pp_cnt[0] % 4
            pp_cnt[0] += 1
            stg = T[i]
            snm = "T%d" % i
            ob = B[i]
            onm = "B%d" % i
            n = nk * ncol
            sv = stg[:, 0:n].rearrange("p (k c) -> p k c", c=ncol)
            ov = ob[:, 0:n].rearrange("p (k c) -> p k c", c=ncol)
            ld(sv, src, (), [snm])
            if mu_cols is not None:
                mt = T[4 + i]
                mnm = "T%d" % (4 + i)
                ld(mt[:, 0:ncol], mu_bc[:, mu_cols:mu_cols + ncol], (), [mnm])
                if mu_inv:
                    ts("dve", mt[:, 0:ncol], mt[:, 0:ncol], -1.0, 1.0, ALU.mult, ALU.add, [mnm], [mnm])
                for k in range(nk):
                    stt(ov[:, k, :], sv[:, k, :], gv[:, gcol + k:gcol + k + 1], mt[:, 0:ncol], ALU.mult, ALU.mult,
                        [snm, mnm, "gv"], [onm])
            elif gcol is not None:
                for k in range(nk):
                    e = ew_eng()
                    if e == "act":
                        act(ov[:, k, :], sv[:, k, :], AF.Copy, [snm, "gv"], [onm], scale=gv[:, gcol + k:gcol + k + 1])
                    else:
                        ts(e, ov[:, k, :], sv[:, k, :], gv[:, gcol + k:gcol + k + 1], None, ALU.mult, None,
                           [snm, "gv"], [onm])
            else:
                e = ew_eng()
                cp(e, ob[:, 0:n], stg[:, 0:n], [snm], [onm])
            ld(scr[u][:, q * 1024:q * 1024 + n], ob[:, 0:n], [onm], [("scr", u, q)], eng="pool")

        units = []
        u = 0
        U_RKV = u
        for b in range(6):
            for var in range(2):
                for q in range(4):
                    pp_piece(u, q, w_in_v[:, 2 * q:2 * q + 2, b * 512:(b + 1) * 512], 2, 512, 2 * q, b * 512, var == 0)
                u += 1
        U_LORA = u
        for var in range(2):
            for hh in range(2):
                pp_piece(u, var * 2 + hh, w_in_v[:, 4 * hh:4 * hh + 4, 3072:3328], 4, 256, 4 * hh, 3072, var == 0)
        u += 1
        U_VG = u
        for (base, col0) in ((0, 4352),):
            for b in range(2):
                for q in range(4):
                    pp_piece(u, q, w_in_v[:, 2 * q:2 * q + 2, col0 + b * 512:col0 + (b + 1) * 512], 2, 512, 2 * q, None, False)
                u += 1
        U_U = u
        for b in range(2):
            for q in range(4):
                pp_piece(u, q, w_in_v[:, 2 * q:2 * q + 2, 3328 + b * 512:3328 + (b + 1) * 512], 2, 512, 2 * q, None, False)
            u += 1
        U_GA = u
        for col0 in (5376, 6400):
            for b in range(2):
                for q in range(4):
                    pp_piece(u, q, w_in_v[:, 2 * q:2 * q + 2, col0 + b * 512:col0 + (b + 1) * 512], 2, 512, 2 * q, None, False)
                u += 1
        U_PA = u
        for wv in (w_pa_v, w_pb_v, w_o_v):
            for b in range(2):
                for q in range(4):
                    pp_piece(u, q, wv[:, 2 * q:2 * q + 2, b * 512:(b + 1) * 512], 2, 512, None, None, False)
                u += 1
        U_FFN = u
        for j in range(11):
            for m, wv in enumerate((w_gate_v, w_up_v)):
                for hh in range(2):
                    pp_piece(u, m * 2 + hh, wv[:, 4 * hh:4 * hh + 4, j * 256:(j + 1) * 256], 4, 256, 8 + 4 * hh, None, False)
            u += 1
        U_DOWN = u
        for qd in range(6):
            for rc in range(4):
                if 4 * qd + rc < NFC:
                    pp_piece(u, rc, w_down_v[:, 4 * qd + rc:4 * qd + rc + 1, :], 1, 1024, None, None, False)
            u += 1
        assert u == NUNIT

        ring_ctr = [0]

        def load_unit(uidx, npc=4):
            s = ring_ctr[0] % NSLOT
            ring_ctr[0] += 1
            ld(ring[s][:, 0:npc * 1024], scr[uidx][:, 0:npc * 1024], [("scr", uidx, q) for q in range(npc)], [("ring", s)])
            return ring[s], ("ring", s)

        bank_ctr = [0]

        bank_pool = [6]

        SCAN_BANKS = [BANKS[0], BANKS[1], BANKS[2], (PD[:], "PD")]

        def next_bank():
            if bank_pool[0] == 3:
                b = SCAN_BANKS[bank_ctr[0] % 4]
            else:
                b = BANKS[bank_ctr[0] % bank_pool[0]]
            bank_ctr[0] += 1
            return b

        ptb_ctr = [0]
        PT_ALT = [(PT[:], "PT"), (PD[:].bitcast(BF16), "PD")]

        def transpose_tm_to_fm(src, srcn, dst, dstn, evac_eng):
            pt, ptn = PT_ALT[ptb_ctr[0] % 2]
            ptb_ctr[0] += 1
            for kc in range(8):
                tr(pt[:, kc * 128:(kc + 1) * 128], src[:, kc * 128:(kc + 1) * 128], idb[:], [srcn, "idb"], [ptn])
            cp("dve", dst, pt.rearrange("p (k t) -> p k t", t=128), [ptn], [dstn])

        def rms_rstd(src_ap, srcn, junk, junkn, eps):
            act(junk, src_ap, AF.Square, srcn, [junkn, "sm0"], accum=sm[:, 0:1])
            act(sm[:, 1:2], sm[:, 0:1], AF.Sqrt, ["sm0"], ["sm1"], scale=1.0 / D, bias=eps)
            recip(sm[:, 1:2], sm[:, 1:2], ["sm1"], ["sm1"])
            return sm[:, 1:2]

        tiles = []
        for i in range(npt):
            tiles.append(dict(v=0, x=xp[i * 128:(i + 1) * 128, :], y=yp[i * 128:(i + 1) * 128, :],
                              segs=[(0, 128, 0)], first=(i == 0), last=(i == npt - 1), samp=None))
        for j in range(nst):
            tiles.append(dict(v=1, x=xs[j * 128:(j + 1) * 128, :], y=ys[j * 128:(j + 1) * 128, :],
                              segs=[(0, 64, 0), (64, 64, 1)], first=True, last=True, samp=j))

        def emit_state_out(slot, dst):
            for c in range(8):
                tr(PC[0:64, c * 128:(c + 1) * 128], Hf[slot][:, c, :], idf, ["Hf%d" % slot, "csb"], ["PC0", "PC1"])
            cp("act", T[0][0:64, :], PC[0:64, :], ["PC0", "PC1"], ["T0"])
            ld(dst.rearrange("h v k -> v h k"), T[0][0:64, :].rearrange("v (h k) -> v h k", k=64), ["T0"], [], eng="pool")

        pending = []
        pump_ctr = [0]
        PT32 = PT[:].bitcast(F32)[:, 0:512]

        def pump(force=False):
            pump_ctr[0] += 1
            if pending and force:
                pending.pop(0)()

        def drain():
            while pending:
                pending.pop(0)()

        def make_side_chunks(tl, v, hTc, hTn):
            ch = []
            VG = T[0]
            vnb = B[0]

            def vg_unit(b):
                rv, rvn = load_unit(U_VG + b)
                wv_ = v3(rv[:], 512)
                for kc in range(8):
                    mm(PC[:, b * 512:(b + 1) * 512], hTc[:, kc, :], wv_[:, kc, :], kc == 0, kc == 7, [hTn, rvn], ["PC%d" % b])

            def vg_post():
                cp("act", VG[:], PC[:], ["PC0", "PC1"], ["T0"])
                S.op("dve", lambda e: e.bn_stats(out=sm[:, 96:102], in_=T[0][:, 0:512]), ["T0"], ["sm96"])
                S.op("dve", lambda e: e.bn_stats(out=sm[:, 102:108], in_=T[0][:, 512:1024]), ["T0"], ["sm96"])
                S.op("dve", lambda e: e.bn_aggr(out=sm[:, 108:110], in_=sm[:, 96:108]), ["sm96"], ["sm108"])
                act(sm[:, 110:111], sm[:, 109:110], AF.Sqrt, ["sm108"], ["sm110"], bias=LN_EPS)
                recip(sm[:, 110:111], sm[:, 110:111], ["sm110"], ["sm110"])
                ts("dve", VG[:], VG[:], sm[:, 108:109], sm[:, 110:111], ALU.subtract, ALU.mult, ["T0", "sm108", "sm110"], ["T0"])
                tt("pool", VG[:], VG[:], bc[LNG][:], ALU.mult, ["T0", "bc5"], ["T0"])
                tt("pool", VG[:], VG[:], bc[LNB][:], ALU.add, ["T0", "bc6"], ["T0"])
                if tl["samp"] is not None:
                    j = tl["samp"]
                    ld(vs[j * 128:(j + 1) * 128, :], VG[:], ["T0"], [], eng="pool")
                cp("act", vnb[:], VG[:], ["T0"], ["B0"])

            def fm_unit(ubase, b, srcT, srcn):
                ru, run_ = load_unit(ubase + b)
                wu_ = v3(ru[:], 512)
                for cc in range(4):
                    for kc in range(8):
                        mm(PC[:, b * 512 + cc * 128:b * 512 + (cc + 1) * 128], wu_[:, kc, cc * 128:(cc + 1) * 128], srcT[:, kc, :],
                           kc == 0, kc == 7, [run_, srcn], ["PC%d" % b])

            def u_evac():
                cp("act", T[2][:], PC[:], ["PC0", "PC1"], ["T2"])

            def spatial():
                for g in range(8):
                    o = PC[:, g * 128:(g + 1) * 128]
                    bn = "PC%d" % (g // 4)
                    mm(o, vnb[:, g * 128:(g + 1) * 128], wst[v][:, g, :], True, False, ["B0", "wst%d" % v], [bn])
                    mm(o, osel[:, v, :], w0hl[:, 2, g * 128:(g + 1) * 128], False, True, ["osel", "w0hl"], [bn])
                tt("dve", B[1][:], PC[:], T[2][:], ALU.mult, ["PC0", "PC1", "T2"], ["B1"])

            obT = v3(B[1][:], 128)
            ch.append(lambda: vg_unit(0))
            ch.append(lambda: (vg_unit(1), vg_post()))
            ch.append(lambda: fm_unit(U_U, 0, hTc, hTn))
            ch.append(lambda: (fm_unit(U_U, 1, hTc, hTn), u_evac()))
            ch.append(lambda: fm_unit(U_GA, 0, hTc, hTn))
            ch.append(lambda: (fm_unit(U_GA, 1, hTc, hTn), act(T[3][:], PC[:], AF.Sigmoid, ["PC0", "PC1"], ["T3"])))
            ch.append(lambda: spatial())
            ch.append(lambda: fm_unit(U_GA + 2, 0, hTc, hTn))
            ch.append(lambda: (fm_unit(U_GA + 2, 1, hTc, hTn), act(T[4][:], PC[:], AF.Sigmoid, ["PC0", "PC1"], ["T4"])))
            ch.append(lambda: fm_unit(U_PA + 2, 0, obT, "B1"))
            ch.append(lambda: (fm_unit(U_PA + 2, 1, obT, "B1"),
                               tt("dve", T[6][:], PC[:], T[4][:], ALU.mult, ["PC0", "PC1", "T4"], ["T6"])))
            return ch

        def make_ffn_chunks(xb, xbn, ydst):
            ch = []
            gbank, gbn = BANKS[4]
            ubank, ubn = BANKS[5]
            dbank0, dbn0 = BANKS[3]
            for q in range(6):
                nfc = min(4, NFC - 4 * q)
                nun = (nfc + 1) // 2
                for jj in range(nun):
                    def c(q=q, jj=jj, nfc=nfc, lastu=(jj == nun - 1)):
                        rf, rfn = load_unit(U_FFN + 2 * q + jj)
                        wf_ = rf[:].rearrange("p (m k c) -> p m k c", m=2, k=8)
                        for cc in range(2):
                            fc = jj * 2 + cc
                            for kc in range(8):
                                mm(gbank[:, fc * 128:(fc + 1) * 128], wf_[:, 0, kc, cc * 128:(cc + 1) * 128], fTb[:, kc, :],
                                   kc == 0, kc == 7, [rfn, "fTb"], [gbn])
                            for kc in range(8):
                                mm(ubank[:, fc * 128:(fc + 1) * 128], wf_[:, 1, kc, cc * 128:(cc + 1) * 128], fTb[:, kc, :],
                                   kc == 0, kc == 7, [rfn, "fTb"], [ubn])
                        if lastu:
                            tq = T[2 + (q % 2)]
                            tqn = "T%d" % (2 + (q % 2))
                            act(tq[:, 0:nfc * 128], gbank[:, 0:nfc * 128], AF.Silu, [gbn], [tqn])
                            tt("dve", actT[:, 4 * q:4 * q + nfc, :].rearrange("p a t -> p (a t)"), tq[:, 0:nfc * 128],
                               ubank[:, 0:nfc * 128], ALU.mult, [tqn, ubn], ["actT"])
                    c.kind = "ffn"
                    ch.append(c)
            for q in range(6):
                def c(q=q):
                    rd, rdn = load_unit(U_DOWN + q, min(4, NFC - 4 * q))
                    wd_ = v3(rd[:], 1024)
                    for rc in range(4):
                        fc = 4 * q + rc
                        if fc >= NFC:
                            continue
                        mm(dbank0, actT[:, fc, :], wd_[:, rc, 0:512], fc == 0, fc == NFC - 1, ["actT", rdn], [dbn0])
                        mm(PT32, actT[:, fc, :], wd_[:, rc, 512:1024], fc == 0, fc == NFC - 1, ["actT", rdn], ["PT"])
                    if q == 5:
                        cp("dve", T[0][:, 0:512], dbank0, [dbn0], ["T0"])
                        cp("dve", T[0][:, 512:1024], PT32, ["PT"], ["T0"])
                        rstd = rms_rstd(T[0][:], ["T0"], T[4][:], "T4", RMS_EPS)
                        stt(T[0][:], T[0][:], rstd, bc[GPF][:], ALU.mult, ALU.mult, ["T0", "sm1", "bc8"], ["T0"])
                        tt("dve", T[0][:], T[0][:], xb[:], ALU.add, ["T0", xbn], ["T0"])
                        ld(ydst, T[0][:], ["T0"], [], eng="pool")
                c.kind = "ffn"
                ch.append(c)
            return ch

        def st_A(cx):
            tl = cx["tl"]
            ti = cx["ti"]
            v = tl["v"]
            segs = tl["segs"]
            nseg = len(segs)
            xb = xt[ti % 2]
            xbn = "xt%d" % (ti % 2)
            hTc = hT[ti % 2]
            hTn = "hT%d" % (ti % 2)
            hTprev = hT[(ti + 1) % 2]
            hTprevn = "hT%d" % ((ti + 1) % 2)
            R_, K_, V_ = Rk[0], Rk[1], Rk[2]
            oaT = v3(B[1][:], 128)
            mT = v3(B[3][:], 128)
            ld(xb[:], tl["x"], (), [xbn])
            if tl["samp"] is not None:
                for sl in range(2):
                    seq = 2 * tl["samp"] + sl
                    ld(T[0][0:64, :].rearrange("v (h k) -> v h k", k=64), swkv[seq].rearrange("h v k -> v h k"), (), ["T0"])
                    for c in range(8):
                        tr(PD[:, c * 64:(c + 1) * 64], T[0][0:64, c * 128:(c + 1) * 128], idf[0:64, 0:64], ["T0", "csb"], ["PD"])
                    cp("dve", Hf[sl][:], v3(PD[:], 64), ["PD"], ["Hf%d" % sl])
                    cp("act", Hb[sl][:], Hf[sl][:], ["Hf%d" % sl], ["Hb%d" % sl])
            elif tl["first"]:
                mset("dve", Hf[0][:], 0.0, ["Hf0"])
                mset("dve", Hb[0][:], 0.0, ["Hb0"])
            rstd = rms_rstd(xb[:], [xbn], hb[:], "hb", RMS_EPS)
            ts("dve", hb[:], xb[:], rstd, None, ALU.mult, None, [xbn, "sm1"], ["hb"])

        def st_A2(cx):
            tl = cx["tl"]
            ti = cx["ti"]
            v = tl["v"]
            segs = tl["segs"]
            nseg = len(segs)
            xb = xt[ti % 2]
            xbn = "xt%d" % (ti % 2)
            hTc = hT[ti % 2]
            hTn = "hT%d" % (ti % 2)
            hTprev = hT[(ti + 1) % 2]
            hTprevn = "hT%d" % ((ti + 1) % 2)
            for kc in range(8):
                tr(PT[:, kc * 128:(kc + 1) * 128], hb[:, kc * 128:(kc + 1) * 128], idb[:], ["hb", "idb"], ["PT"])
            ptv = PT[:].rearrange("p (k t) -> p k t", t=128)
            cp("dve", hTc[:].rearrange("p k t -> p (k t)"), PT[:], ["PT"], [hTn])
            cp("dve", hTp[:, :, 1:128], ptv[:, :, 0:127], ["PT"], ["hTp"])
            if tl["first"]:
                mset("dve", hTp[:, :, 0:1], 0.0, ["hTp"])
                if v == 1:
                    mset("dve", hTp[:, :, 64:65], 0.0, ["hTp"])
            else:
                cp("dve", hTp[:, :, 0:1], hTprev[:, :, 127:128], [hTprevn], ["hTp"])

        def st_B1(cx, blocks, defer_evac=False):
            tl = cx["tl"]
            ti = cx["ti"]
            v = tl["v"]
            segs = tl["segs"]
            nseg = len(segs)
            xb = xt[ti % 2]
            xbn = "xt%d" % (ti % 2)
            hTc = hT[ti % 2]
            hTn = "hT%d" % (ti % 2)
            hTprev = hT[(ti + 1) % 2]
            hTprevn = "hT%d" % ((ti + 1) % 2)
            R_, K_, V_ = Rk[0], Rk[1], Rk[2]
            oaT = v3(B[1][:], 128)
            mT = v3(B[3][:], 128)
            for b in blocks:
                r1, r1n = load_unit(U_RKV + 2 * b)
                r2, r2n = load_unit(U_RKV + 2 * b + 1)
                w1 = v3(r1[:], 512)
                w2 = v3(r2[:], 512)
                bank, bn = next_bank()
                if v == 1:
                    ld(ssb[:], ssmu_d[tl["samp"], :, b * 512:(b + 1) * 512], ["ssmu_d"], ["ssb"])
                for kc in range(8):
                    mm(bank, hTc[:, kc, :], w1[:, kc, :], kc == 0, False, [hTn, r1n], [bn])
                for kc in range(8):
                    mm(bank, hTp[:, kc, :], w2[:, kc, :], False, (kc == 7 and v == 0), ["hTp", r2n], [bn])
                if v == 1:
                    mm(bank, sel[:], ssb[:], False, True, ["sel", "ssb"], [bn])
                dstt = Rk[b // 2]
                if defer_evac:
                    cx.setdefault("b1ev", []).append((dstt[:, (b % 2) * 512:(b % 2 + 1) * 512], bank, bn, "rkv%d" % (b // 2)))
                else:
                    cp("act", dstt[:, (b % 2) * 512:(b % 2 + 1) * 512], bank, [bn], ["rkv%d" % (b // 2)])
                if tl["last"]:
                    lh = hTc[:, :, 63::64]
                    for kc in range(8):
                        mm(PD[0:2, :], lh[:, kc, :], w1[:, kc, :], kc == 0, False, [hTn, r1n], ["PD"])
                    for kc in range(8):
                        mm(PD[0:2, :], lh[:, kc, :], w2[:, kc, :], False, kc == 7, [hTn, r2n], ["PD"])
                    tq = T[b // 2]
                    cp("dve", tq[0:2, (b % 2) * 512:(b % 2 + 1) * 512], PD[0:2, :], ["PD"], ["T%d" % (b // 2)])
                    if v == 0:
                        ld(shp[0:1, b * 512:(b + 1) * 512], tq[1:2, (b % 2) * 512:(b % 2 + 1) * 512], ["T%d" % (b // 2)], [], eng="pool")
                    else:
                        j = tl["samp"]
                        ld(shs[2 * j:2 * j + 2, b * 512:(b + 1) * 512], tq[0:2, (b % 2) * 512:(b % 2 + 1) * 512], ["T%d" % (b // 2)], [], eng="pool")

        def st_B1ev(cx):
            for (dst, bank, bn, rn) in cx.pop("b1ev", []):
                cp("act", dst, bank, [bn], [rn])

        def st_B2(cx):
            tl = cx["tl"]
            ti = cx["ti"]
            v = tl["v"]
            segs = tl["segs"]
            nseg = len(segs)
            xb = xt[ti % 2]
            xbn = "xt%d" % (ti % 2)
            hTc = hT[ti % 2]
            hTn = "hT%d" % (ti % 2)
            hTprev = hT[(ti + 1) % 2]
            hTprevn = "hT%d" % ((ti + 1) % 2)
            R_, K_, V_ = Rk[0], Rk[1], Rk[2]
            oaT = v3(B[1][:], 128)
            mT = v3(B[3][:], 128)
            rl, rln = load_unit(U_LORA)
            wl = rl[:].rearrange("p (a k c) -> p a k c", a=2, k=8)
            if v == 1:
                ld(ssb[:, 0:256], ssmu_d[tl["samp"], :, 3072:3328], ["ssmu_d"], ["ssb"])
            for cc in range(2):
                o = PD[:, cc * 128:(cc + 1) * 128]
                for kc in range(8):
                    mm(o, wl[:, 0, kc, cc * 128:(cc + 1) * 128], hTc[:, kc, :], kc == 0, False, [rln, hTn], ["PD"])
                for kc in range(8):
                    mm(o, wl[:, 1, kc, cc * 128:(cc + 1) * 128], hTp[:, kc, :], False, (kc == 7 and v == 0), [rln, "hTp"], ["PD"])
                if v == 1:
                    mm(o, ssb[:, cc * 128:(cc + 1) * 128], sel[:], False, True, ["ssb", "sel"], ["PD"])
            act(lin[0:64, 0, :], PD[0:64, 0:128], AF.Tanh, ["PD"], ["lin"])
            cp("act", lin[64:128, 0, :], PD[64:128, 0:128], ["PD"], ["lin"])
            act(lin[:, 1, :], PD[:, 128:256], AF.Sigmoid, ["PD"], ["lin"])
            if tl["last"]:
                lh = hTc[:, :, 63::64]
                o = PD[0:2, 256:512]
                for kc in range(8):
                    mm(o, lh[:, kc, :], wl[:, 0, kc, :], kc == 0, False, [hTn, rln], ["PD"])
                for kc in range(8):
                    mm(o, lh[:, kc, :], wl[:, 1, kc, :], False, kc == 7, [hTn, rln], ["PD"])
                cp("dve", T[3][0:2, 0:256], o, ["PD"], ["T3"])
                if v == 0:
                    ld(shp[0:1, 3072:3328], T[3][1:2, 0:256], ["T3"], [], eng="pool")
                else:
                    j = tl["samp"]
                    ld(shs[2 * j:2 * j + 2, 3072:3328], T[3][0:2, 0:256], ["T3"], [], eng="pool")

        def st_CD(cx):
            tl = cx["tl"]
            ti = cx["ti"]
            v = tl["v"]
            segs = tl["segs"]
            nseg = len(segs)
            xb = xt[ti % 2]
            xbn = "xt%d" % (ti % 2)
            hTc = hT[ti % 2]
            hTn = "hT%d" % (ti % 2)
            hTprev = hT[(ti + 1) % 2]
            hTprevn = "hT%d" % ((ti + 1) % 2)
            R_, K_, V_ = Rk[0], Rk[1], Rk[2]
            oaT = v3(B[1][:], 128)
            mT = v3(B[3][:], 128)
            R_, K_, V_ = Rk[0], Rk[1], Rk[2]
            tt("dve", T[5][:], K_[:], bc[KK][:], ALU.mult, ["rkv1", "bc0"], ["T5"])
            act(T[7][:], T[5][:], AF.Square, ["T5"], ["T7"])
            red(sm[:, 16:32], v3(T[7][:], 64), ["T7"], ["sm16"])
            act(sm[:, 16:32], sm[:, 16:32], AF.Sqrt, ["sm16"], ["sm16"])
            ts("dve", sm[:, 16:32], sm[:, 16:32], 1e-12, None, ALU.max, None, ["sm16"], ["sm16"])
            recip(sm[:, 16:32], sm[:, 16:32], ["sm16"], ["sm16"])
            tt("dve", v3(T[5][:], 64), v3(T[5][:], 64), sm[:, 16:32].unsqueeze(2).to_broadcast([128, 16, 64]), ALU.mult,
               ["T5", "sm16"], ["T5"])
            for blk in range(2):
                cs = slice(blk * 512, (blk + 1) * 512)
                mm(PA[:, cs], lin[:, 0, :], lw[0][:, cs], True, False, ["lin", "lw0"], ["PA%d" % blk])
                mm(PA[:, cs], ones2[:], w0hl[:, 0, cs], False, True, ["ones2", "w0hl"], ["PA%d" % blk])
                mm(PB[:, cs], lin[:, 0, :], lw[1][:, cs], True, False, ["lin", "lw1"], ["PB%d" % blk])
                mm(PB[:, cs], ones2[:], w0hl[:, 1, cs], False, True, ["ones2", "w0hl"], ["PB%d" % blk])
            act(T[0][:], PA[:], AF.Sigmoid, ["PA0", "PA1"], ["T0"])
            act(T[4][:], PB[:], AF.Sigmoid, ["PB0", "PB1"], ["T4"])
            for blk in range(2):
                cs = slice(blk * 512, (blk + 1) * 512)
                mm(PA[:, cs], TRI[v], T[0][:, cs], True, True, ["csb", "T0"], ["PA%d" % blk])
            for c in range(8):
                mm(PD[:, 2 * c:2 * c + 2], T[0][:, c * 128:(c + 1) * 128], SEGI[v], True, True, ["T0", "csb"], ["PD"])
            act(T[1][:], PA[:], AF.Exp, ["PA0", "PA1"], ["T1"], scale=-C0)
            act(T[2][:], PA[:], AF.Exp, ["PA0", "PA1"], ["T2"], scale=C0)
            for blk in range(2):
                cs = slice(blk * 512, (blk + 1) * 512)
                mm(PB[:, cs], MT2[v][:, 0:128], T[0][:, cs], True, True, ["csb", "T0"], ["PB%d" % blk])
            act(T[3][:], PB[:], AF.Exp, ["PB0", "PB1"], ["T3"], scale=-C0)
            act(gC[:].rearrange("p c s -> p (c s)"), PD[:, 0:16], AF.Exp, ["PD"], ["gC"], scale=-C0)
            stt(T[6][:], T[4][:], -1.0, bc[KA][:], ALU.add, ALU.mult, ["T4", "bc1"], ["T6"])
            stt(T[6][:], T[6][:], 1.0, K_[:], ALU.add, ALU.mult, ["T6", "rkv1"], ["T6"])
            tt("dve", B[0][:], R_[:], T[1][:], ALU.mult, ["rkv0", "T1"], ["B0"])
            stt(B[1][:], T[5][:], -1.0, T[3][:], ALU.mult, ALU.mult, ["T5", "T3"], ["B1"])
            tt("pool", T[7][:], T[5][:], T[4][:], ALU.mult, ["T5", "T4"], ["T7"])
            tt("dve", B[2][:], T[7][:], T[2][:], ALU.mult, ["T7", "T2"], ["B2"])
            tt("pool", B[3][:], T[6][:], T[2][:], ALU.mult, ["T6", "T2"], ["B3"])
            cp("act", B[4][:], V_[:], ["rkv2"], ["B4"])
            transpose_tm_to_fm(B[1], "B1", arT[:, :, 0, :], "arT", "act")
            transpose_tm_to_fm(B[0], "B0", arT[:, :, 1, :], "arT", "dve")
            bT = v3(B[5][:], 128)
            kT = v3(B[6][:], 128)
            transpose_tm_to_fm(B[2], "B2", bT, "B5", "act")
            transpose_tm_to_fm(B[3], "B3", kT, "B6", "dve")

            Vb = B[4]
            bank_pool[0] = 3
            pending.extend(make_side_chunks(tl, v, hTc, hTn))
            tt("pool", T[7][:], R_[:], T[6][:], ALU.mult, ["rkv0", "T6"], ["T7"])
            tt("pool", T[7][:], T[7][:], bc[RK][:], ALU.mult, ["T7", "bc2"], ["T7"])
            red(sm[:, 32:48], v3(T[7][:], 64), ["T7"], ["sm32"])
            for h in range(16):
                ts("pool", T[5][:, h * 64:(h + 1) * 64], V_[:, h * 64:(h + 1) * 64], sm[:, 32 + h:33 + h], None, ALU.mult, None,
                   ["rkv2", "sm32"], ["T5"])
            tt("pool", T[5][:], T[5][:], bc[GNB][:], ALU.add, ["T5", "bc4"], ["T5"])
            for hg in range(2):
                if hg == 1 and cx.get("mid_hook") is not None:
                    while pending and getattr(pending[0], "kind", None) == "ffn":
                        pending.pop(0)()
                    cx["mid_hook"]()
                heads = list(range(hg * 8, hg * 8 + 8))
                mask2 = MT2[v].unsqueeze(1).to_broadcast([128, 2, 256])
                for lhsbuf, lhn, dst, dstn in ((bT, "B5", M2a, "M2a"), (kT, "B6", M2b, "M2b")):
                    for (ja, jb) in ((0, 2), (4, 6), (1, 3), (5, 7)):
                        bank, bn = next_bank()
                        for jj, j in enumerate((ja, jb)):
                            h = heads[j]
                            c, pb = h // 2, 64 * (h % 2)
                            mm(bank[:, jj * 256:(jj + 1) * 256], lhsbuf[pb:pb + 64, c, :],
                               arT[pb:pb + 64, c, :, :].rearrange("p a t -> p (a t)"), True, True, [lhn, "arT"], [bn])
                        tt("dve", dst[:, ja:jb + 1:2, :], v3(bank, 256), mask2, ALU.mult, [bn, "csb"], [dstn])
                        pump(force=(ja in (4, 5)))
                maskn = MSN[v].unsqueeze(1).to_broadcast([128, 4, 128])
                for par in range(2):
                    bank, bn = next_bank()
                    for jj in range(4):
                        j = 2 * jj + par
                        h = heads[j]
                        c, pb = h // 2, 64 * (h % 2)
                        mm(bank[:, jj * 128:(jj + 1) * 128], arT[pb:pb + 64, c, 0, :], bT[pb:pb + 64, c, :], True, True,
                           ["arT", "B5"], [bn])
                    tt("dve", Xb[0][:, par::2, :], v3(bank, 128), maskn, ALU.mult, [bn, "csb"], ["X0"])
                    pump(force=(par == 1))
                tt("dve", PTb[0][:], M2a[:, :, 0:128], idb[:].unsqueeze(1).to_broadcast([128, 8, 128]), ALU.add,
                   ["M2a", "idb"], ["PT0"])
                for k in range(6):
                    cur, nxt = k % 2, (k + 1) % 2
                    Xc, Xcn = Xb[cur], "X%d" % cur
                    if k == 0:
                        XTc, XTcn = M2a[:, :, 0:128], "M2a"
                    else:
                        XTc, XTcn = XTb[cur][:], "XT%d" % cur
                    for jq in range(2):
                        bank, bn = next_bank()
                        for jj in range(4):
                            j = 4 * jq + jj
                            mm(bank[:, jj * 128:(jj + 1) * 128], XTc[:, j, :], Xc[:, j, :], True, True, [XTcn, Xcn], [bn])
                        cp("act", Xb[nxt][:, 4 * jq:4 * jq + 4, :], v3(bank, 128), [bn], ["X%d" % nxt])
                        pump(force=(jq == 1))
                    if k < 5:
                        for jq in range(2):
                            bank, bn = next_bank()
                            for jj in range(4):
                                j = 4 * jq + jj
                                mm(bank[:, jj * 128:(jj + 1) * 128], Xc[:, j, :], XTc[:, j, :], True, True, [Xcn, XTcn], [bn])
                            cp("act" if jq == 0 else "dve", XTb[nxt][:, 4 * jq:4 * jq + 4, :], v3(bank, 128), [bn], ["XT%d" % nxt])
                            pump()
                    for jq in range(2):
                        bank, bn = next_bank()
                        for jj in range(4):
                            j = 4 * jq + jj
                            mm(bank[:, jj * 128:(jj + 1) * 128], Xb[nxt][:, j, :], PTb[cur][:, j, :], True, True,
                               ["X%d" % nxt, "PT%d" % cur], [bn])
                        tt("dve", PTb[nxt][:, 4 * jq:4 * jq + 4, :], v3(bank, 128), PTb[cur][:, 4 * jq:4 * jq + 4, :], ALU.add,
                           [bn, "PT%d" % cur], ["PT%d" % nxt])
                        pump(force=(jq == 1))
                PTf, PTfn = PTb[0], "PT0"
                for par in range(2):
                    bank, bn = next_bank()
                    for jj in range(4):
                        j = 2 * jj + par
                        h = heads[j]
                        c, pb = h // 2, 64 * (h % 2)
                        o = bank[:, jj * 64:(jj + 1) * 64]
                        for (r0, nr, slot) in segs:
                            mm(bank[r0:r0 + nr, jj * 64:(jj + 1) * 64], arT[pb:pb + 64, c, 0, r0:r0 + nr], Hb[slot][pb:pb + 64, c, :],
                               True, False, ["arT", "Hb%d" % slot], [bn])
                        mm(o, M2b[:, j, 0:128], Vb[:, h * 64:(h + 1) * 64], False, True, ["M2b", "B4"], [bn])
                    cp("dve", W1b[:, par::2, :], v3(bank[:, 0:256], 64), [bn], ["W1b"])
                bank, bn = next_bank()
                for j in range(8):
                    mm(bank[:, j * 64:(j + 1) * 64], PTf[:, j, :], W1b[:, j, :], True, True, [PTfn, "W1b"], [bn])
                cp("act", Ub[:], v3(bank, 64), [bn], ["Ub"])
                for par in range(2):
                    bank, bn = next_bank()
                    for jj in range(4):
                        j = 2 * jj + par
                        h = heads[j]
                        c, pb = h // 2, 64 * (h % 2)
                        o = bank[:, jj * 64:(jj + 1) * 64]
                        for (r0, nr, slot) in segs:
                            mm(bank[r0:r0 + nr, jj * 64:(jj + 1) * 64], arT[pb:pb + 64, c, 1, r0:r0 + nr], Hb[slot][pb:pb + 64, c, :],
                               True, False, ["arT", "Hb%d" % slot], [bn])
                        mm(o, M2a[:, j, 128:256], Ub[:, j, :], False, False, ["M2a", "Ub"], [bn])
                        mm(o, M2b[:, j, 128:256], Vb[:, h * 64:(h + 1) * 64], False, True, ["M2b", "B4"], [bn])
                    cp("dve", v3(T[1][:], 64)[:, hg * 8 + par:hg * 8 + 8:2, :], v3(bank[:, 0:256], 64), [bn], ["T1"])
                hbanks = []
                for si, (r0, nr, slot) in enumerate(segs):
                    hbk, hbn = (PD[:], "PD") if si == 0 else next_bank()
                    if si > 0 and hbn == "PD":
                        hbk, hbn = next_bank()
                    hbanks.append((hbk, hbn))
                    for j in range(8):
                        h = heads[j]
                        c, pb = h // 2, 64 * (h % 2)
                        cl = c - hg * 4
                        o = hbk[pb:pb + 64, cl * 64:(cl + 1) * 64]
                        mm(o, B[2][r0:r0 + nr, h * 64:(h + 1) * 64], Ub[r0:r0 + nr, j, :], True, False, ["B2", "Ub"], [hbn])
                        mm(o, B[3][r0:r0 + nr, h * 64:(h + 1) * 64], Vb[r0:r0 + nr, h * 64:(h + 1) * 64], False, True,
                           ["B3", "B4"], [hbn])
                for si, (r0, nr, slot) in enumerate(segs):
                    hsl = Hf[slot][:, hg * 4:hg * 4 + 4, :]
                    hn = "Hf%d" % slot
                    hbk, hbn = hbanks[si]
                    tt("dve", hsl, v3(hbk[:, 0:256], 64), hsl, ALU.add, [hbn, hn], [hn])
                    tt("dve", hsl, hsl, gC[:, hg * 4:hg * 4 + 4, si:si + 1].to_broadcast([128, 4, 64]), ALU.mult,
                       [hn, "gC"], [hn])
                    cp("act", Hb[slot][:, hg * 4:hg * 4 + 4, :], hsl, [hn], ["Hb%d" % slot])

            drain()
            bank_pool[0] = 6
            if tl["last"]:
                if v == 0:
                    emit_state_out(0, wkvp)
                else:
                    for sl in range(2):
                        emit_state_out(sl, wkvs[2 * tl["samp"] + sl])

        def st_E(cx):
            tl = cx["tl"]
            ti = cx["ti"]
            v = tl["v"]
            segs = tl["segs"]
            nseg = len(segs)
            xb = xt[ti % 2]
            xbn = "xt%d" % (ti % 2)
            hTc = hT[ti % 2]
            hTn = "hT%d" % (ti % 2)
            hTprev = hT[(ti + 1) % 2]
            hTprevn = "hT%d" % ((ti + 1) % 2)
            R_, K_, V_ = Rk[0], Rk[1], Rk[2]
            oaT = v3(B[1][:], 128)
            mT = v3(B[3][:], 128)
            Y = T[1]
            for blk in range(2):
                cs = slice(blk * 512, (blk + 1) * 512)
                mm(PC[:, cs], lin[:, 1, :], lw[2][:, cs], True, True, ["lin", "lw2"], ["PC%d" % blk])
            red(sm[:, 48:64], v3(Y[:], 64), ["T1"], ["sm48"])
            act(T[7][:], Y[:], AF.Square, ["T1"], ["T7"])
            red(sm[:, 64:80], v3(T[7][:], 64), ["T7"], ["sm64"])
            ts("dve", sm[:, 48:64], sm[:, 48:64], 1.0 / 64, None, ALU.mult, None, ["sm48"], ["sm48"])
            tt("dve", sm[:, 80:96], sm[:, 48:64], sm[:, 48:64], ALU.mult, ["sm48"], ["sm80"])
            stt(sm[:, 64:80], sm[:, 64:80], 1.0 / 64, sm[:, 80:96], ALU.mult, ALU.subtract, ["sm64", "sm80"], ["sm64"])
            act(sm[:, 64:80], sm[:, 64:80], AF.Sqrt, ["sm64"], ["sm64"], bias=GN_EPS)
            recip(sm[:, 64:80], sm[:, 64:80], ["sm64"], ["sm64"])
            bch = lambda a: a.unsqueeze(2).to_broadcast([128, 16, 64])
            tt("dve", v3(Y[:], 64), v3(Y[:], 64), bch(sm[:, 48:64]), ALU.subtract, ["T1", "sm48"], ["T1"])
            tt("dve", v3(Y[:], 64), v3(Y[:], 64), bch(sm[:, 64:80]), ALU.mult, ["T1", "sm64"], ["T1"])
            tt("dve", Y[:], Y[:], bc[GNG][:], ALU.mult, ["T1", "bc3"], ["T1"])
            tt("dve", Y[:], Y[:], T[5][:], ALU.add, ["T1", "T5"], ["T1"])
            tt("dve", B[0][:], Y[:], PC[:], ALU.mult, ["T1", "PC0", "PC1"], ["B0"])
            if cx.get("split_E"):
                return
            st_E2(cx)

        def st_E2(cx):
            oaT = v3(B[1][:], 128)
            transpose_tm_to_fm(B[0], "B0", oaT, "B1", "act")

        def st_G(cx):
            tl = cx["tl"]
            ti = cx["ti"]
            v = tl["v"]
            segs = tl["segs"]
            nseg = len(segs)
            xb = xt[ti % 2]
            xbn = "xt%d" % (ti % 2)
            hTc = hT[ti % 2]
            hTn = "hT%d" % (ti % 2)
            hTprev = hT[(ti + 1) % 2]
            hTprevn = "hT%d" % ((ti + 1) % 2)
            R_, K_, V_ = Rk[0], Rk[1], Rk[2]
            oaT = v3(B[1][:], 128)
            mT = v3(B[3][:], 128)
            for b in range(2):
                rp, rpn = load_unit(U_PA + b)
                wp_ = v3(rp[:], 512)
                for cc in range(4):
                    for kc in range(8):
                        mm(PC[:, b * 512 + cc * 128:b * 512 + (cc + 1) * 128], wp_[:, kc, cc * 128:(cc + 1) * 128], oaT[:, kc, :],
                           kc == 0, kc == 7, [rpn, "B1"], ["PC%d" % b])
            tt("dve", T[5][:], PC[:], T[3][:], ALU.mult, ["PC0", "PC1", "T3"], ["T5"])
            mT = v3(B[3][:], 128)
            tt("pool", B[3][:], T[5][:], T[6][:], ALU.add, ["T5", "T6"], ["B3"])

        def st_H(cx):
            tl = cx["tl"]
            ti = cx["ti"]
            v = tl["v"]
            segs = tl["segs"]
            nseg = len(segs)
            xb = xt[ti % 2]
            xbn = "xt%d" % (ti % 2)
            hTc = hT[ti % 2]
            hTn = "hT%d" % (ti % 2)
            hTprev = hT[(ti + 1) % 2]
            hTprevn = "hT%d" % ((ti + 1) % 2)
            R_, K_, V_ = Rk[0], Rk[1], Rk[2]
            oaT = v3(B[1][:], 128)
            mT = v3(B[3][:], 128)
            for b in range(2):
                ro, ron = load_unit(U_PA + 4 + b)
                wo_ = v3(ro[:], 512)
                for kc in range(8):
                    mm(PB[:, b * 512:(b + 1) * 512], mT[:, kc, :], wo_[:, kc, :], kc == 0, kc == 7, ["B3", ron], ["PB%d" % b])
            cp("dve", T[0][:], PB[:], ["PB0", "PB1"], ["T0"])
            rstd = rms_rstd(T[0][:], ["T0"], T[2][:], "T2", RMS_EPS)
            stt(T[0][:], T[0][:], rstd, bc[GPM][:], ALU.mult, ALU.mult, ["T0", "sm1", "bc7"], ["T0"])
            tt("dve", xb[:], xb[:], T[0][:], ALU.add, [xbn, "T0"], [xbn])

        def st_I(cx):
            tl = cx["tl"]
            ti = cx["ti"]
            v = tl["v"]
            segs = tl["segs"]
            nseg = len(segs)
            xb = xt[ti % 2]
            xbn = "xt%d" % (ti % 2)
            hTc = hT[ti % 2]
            hTn = "hT%d" % (ti % 2)
            hTprev = hT[(ti + 1) % 2]
            hTprevn = "hT%d" % ((ti + 1) % 2)
            R_, K_, V_ = Rk[0], Rk[1], Rk[2]
            oaT = v3(B[1][:], 128)
            mT = v3(B[3][:], 128)
            rstd = rms_rstd(xb[:], [xbn], T[0][:], "T0", RMS_EPS)
            ts("dve", hb[:], xb[:], rstd, None, ALU.mult, None, [xbn, "sm1"], ["hb"])
            transpose_tm_to_fm(hb, "hb", fTb[:], "fTb", "act")
            pending.extend(make_ffn_chunks(xb, xbn, tl["y"]))


        cxs = [dict(tl=tl, ti=ti) for ti, tl in enumerate(tiles)]
        st_A(cxs[0])
        st_A2(cxs[0])
        st_B1(cxs[0], range(6))
        st_B2(cxs[0])
        for ti, cx in enumerate(cxs):
            nx = cxs[ti + 1] if ti + 1 < len(cxs) else None
            piped = PIPELINE and nx is not None and not nx["tl"]["last"]
            if piped:
                cx["mid_hook"] = (lambda nx=nx: st_A(nx))
            st_CD(cx)
            if piped:
                st_A2(nx)
                bank_pool[0] = 2
                cx["split_E"] = True
                st_E(cx)
                st_B1(nx, [0, 1, 2, 3])
                st_E2(cx)
                st_G(cx)
                st_B1(nx, [4], defer_evac=True)
                st_H(cx)
                st_B1ev(nx)
                st_B1(nx, [5], defer_evac=True)
                st_I(cx)
                st_B1ev(nx)
                bank_pool[0] = 6
                st_B2(nx)
            else:
                st_E(cx)
                st_G(cx)
                st_H(cx)
                st_I(cx)
                if nx is not None:
                    st_A(nx)
                    st_A2(nx)
                    st_B1(nx, range(6))
                    st_B2(nx)

        drain()
        S.finish("sp")
        with nc.Block() as block:
            S.replay(block)
    return nc


_CACHE = {}


def _consts():
    s = np.arange(128)[:, None]
    t = np.arange(128)[None, :]
    same = (s // 64) == (t // 64)
    ms_p = (s < t).astype(np.float32)
    mi_p = (s <= t).astype(np.float32)
    ms_s = ((s < t) & same).astype(np.float32)
    mi_s = ((s <= t) & same).astype(np.float32)
    msn_p = ms_p.T.copy()
    msn_s = ms_s.T.copy()
    seg_p = np.stack([np.ones(128, np.float32), np.zeros(128, np.float32)], axis=1)
    seg_s = np.stack([(np.arange(128) < 64).astype(np.float32), (np.arange(128) >= 64).astype(np.float32)], axis=1)
    cst = np.concatenate([np.eye(128, dtype=np.float32), ms_p, mi_p, ms_s, mi_s, msn_p, msn_s, seg_p, seg_s], axis=1)
    sel = np.zeros((2, 384), np.float32)
    sel[0, 0] = 1.0
    sel[1, 64] = 1.0
    sel[0, 128:256] = 1.0
    sel[1, 256:384] = 1.0
    return np.ascontiguousarray(cst), sel


def make_in_maps(inp, n_ptiles=N_PTILES, n_stiles=N_STILES, ncores=NCORE):
    f = lambda a: np.ascontiguousarray(np.asarray(a, dtype=np.float32))
    bcv = lambda a: np.broadcast_to(f(a).reshape(1, -1), (128, f(a).size))
    cst, sel = _consts()
    L = 0
    bcp = np.ascontiguousarray(np.stack([
        bcv(inp["k_k"][L]), bcv(inp["k_a"][L]), bcv(inp["r_k"][L]), bcv(inp["gn_g"][L]), bcv(inp["gn_b"][L]),
        bcv(inp["ln_v_g"][L]), bcv(inp["ln_v_b"][L]), bcv(inp["g_post_mix"][L]), bcv(inp["g_post_ffn"][L])]))
    mu_bc = np.ascontiguousarray(bcv(inp["mu_shift"][L]))
    gvec = np.ascontiguousarray(np.concatenate([f(inp["g_pre_mix"][L]).reshape(8, 128).T,
                                                f(inp["g_pre_ffn"][L]).reshape(8, 128).T], axis=1))
    lw = np.zeros((3, 128, D), np.float32)
    lw[0, 0:64] = f(inp["w_lora_w"][L])
    lw[1, 64:128] = f(inp["w_lora_a"][L])
    lw[2] = f(inp["w_lora_g"][L])
    wa0 = np.ascontiguousarray(np.stack([f(inp["w0"][L]), f(inp["a0"][L])]))
    wsp = f(inp["w_spatial"][L])
    wsT_p = np.transpose(wsp, (2, 0, 1))
    idx = np.arange(128) % 64
    wsT_s = np.transpose(wsp[:, idx][:, :, idx], (2, 0, 1))
    wsT = np.ascontiguousarray(np.stack([wsT_p, wsT_s]))
    bs = f(inp["b_spatial"][L])
    bspv = np.ascontiguousarray(np.stack([bs, bs[:, idx]]))
    shared = dict(w_in=f(inp["w_in"][L]), w_pa=f(inp["w_proj_a"][L]), w_pb=f(inp["w_proj_b"][L]), w_o=f(inp["w_o"][L]),
                  w_gate=f(inp["w_gate"][L]), w_up=f(inp["w_up"][L]), w_down=f(inp["w_down"][L]),
                  bcp=bcp, mu_bc=mu_bc, gvec=gvec, lora=lw, wa0=wa0, wsT=wsT, bspv=bspv, cst=cst, selc=sel)
    xpr = f(inp["x_prompt"])
    xsm = f(inp["x_sample"])
    ssh = f(inp["state_shift"][L])[:, 0, :]
    swk = f(inp["state_wkv"][L])
    nseq = 2 * n_stiles
    maps = []
    for c in range(ncores):
        m = dict(shared)
        m["xp"] = np.ascontiguousarray(xpr[c, :n_ptiles * 128])
        m["xs"] = np.ascontiguousarray(xsm[c * nseq:(c + 1) * nseq].reshape(nseq * 64, D))
        m["sshift"] = np.ascontiguousarray(ssh[c * nseq:(c + 1) * nseq])
        m["swkv"] = np.ascontiguousarray(swk[c * nseq:(c + 1) * nseq])
        maps.append(m)
    return maps


def assemble(results, n_ptiles=N_PTILES, n_stiles=N_STILES):
    nseq = 2 * n_stiles
    yp = np.stack([r["yp"] for r in results])
    ys = np.concatenate([r["ys"].reshape(nseq, 64, D) for r in results])
    shp = np.stack([r["shp"] for r in results])[None]
    wkvp = np.stack([r["wkvp"] for r in results])[None]
    shs = np.concatenate([r["shs"] for r in results])[None, :, None, :]
    wkvs = np.concatenate([r["wkvs"] for r in results])[None]
    vs = np.concatenate([r["vs"].reshape(nseq, 64, D) for r in results])[None]
    f = lambda a: np.ascontiguousarray(a, dtype=np.float32)
    return (f(yp), f(ys), f(shp), f(wkvp), f(shs), f(wkvs), f(vs))


def kernel(**inputs):
    if "nc" not in _CACHE:
        _CACHE["nc"] = build_program()
    nc = _CACHE["nc"]
    in_maps = make_in_maps(inputs)
    res = run_bass_kernel_spmd(nc, in_maps, core_ids=list(range(NCORE)))
    return assemble(res.results)
```
